# Optimizing a Trainium2 kernel written in Bass

```python
import math
import jax, jax.numpy as jnp
from jax import lax
import numpy as np

D_MODEL = 1024
BATCH = 32
SEQ = 256
DEPTH = 4
DEC_BATCH = 8
DEC_SEQ = 1024
PAST_LEN = 512

F32 = jnp.float32
GRID_W = 64
N_EVEN = (DEPTH + 1) // 2
N_ODD = DEPTH // 2
HEAD_DIM = 64
ROPE_BASE = 10000.0
Q_BLOCK = 128
MASK_VALUE = -1e30
F_FLOOR = 1e-30
A_HEADS = 8
A_KV = 2
WINDOW = 128
B_HEADS = 4
B_DK = 128
B_DV = 128
B_CHUNK = 16
C_HEADS = 8
C_Q_LORA = 384
C_KV_LORA = 256
C_NOPE = 64
C_ROPE = 32
C_V = 64
D_HEADS = 8
D_KV = 4
MIX = A_HEADS * HEAD_DIM + B_HEADS * B_DV
D_FF = -(-8 * D_MODEL // (3 * 256)) * 256
EVEN_SIZES = (A_HEADS * HEAD_DIM, A_KV * HEAD_DIM, A_KV * HEAD_DIM,
              B_HEADS * B_DK, B_HEADS * B_DV, B_HEADS * B_DK, B_HEADS * B_DK, B_HEADS * B_DV)
ODD_SIZES = (C_Q_LORA, C_KV_LORA, C_ROPE, D_HEADS * HEAD_DIM, D_KV * HEAD_DIM, D_KV * HEAD_DIM)
EVEN_SPLIT = tuple(int(i) for i in np.cumsum(EVEN_SIZES)[:-1])
ODD_SPLIT = tuple(int(i) for i in np.cumsum(ODD_SIZES)[:-1])
EVEN_IN = int(sum(EVEN_SIZES))
ODD_IN = int(sum(ODD_SIZES))
ALPHA = (2 * DEPTH) ** 0.25
BETA = (8 * DEPTH) ** -0.25

kernel_name = 'hybrid_flow_backbone_step'


def rms_norm(x, g, eps=1e-6):
    xf = x.astype(F32)
    y = xf * lax.rsqrt(jnp.mean(xf * xf, axis=-1, keepdims=True) + eps)
    return (y * g.astype(F32)).astype(x.dtype)


def layer_norm(x, g, b, eps=1e-5):
    xf = x.astype(F32)
    mu = jnp.mean(xf, axis=-1, keepdims=True)
    xc = xf - mu
    var = jnp.mean(xc * xc, axis=-1, keepdims=True)
    return (xc * lax.rsqrt(var + eps) * g.astype(F32) + b.astype(F32)).astype(x.dtype)


def axial_rope_tables(n_tokens, rot_dim):
    n_rows = n_tokens // GRID_W
    row = jnp.broadcast_to(jnp.arange(n_rows, dtype=F32)[:, None], (n_rows, GRID_W)).reshape(-1)
    col = jnp.broadcast_to(jnp.arange(GRID_W, dtype=F32)[None, :], (n_rows, GRID_W)).reshape(-1)
    quarter = rot_dim // 4
    inv = ROPE_BASE ** (-jnp.arange(quarter, dtype=F32) / quarter)
    ar = row[:, None] * inv
    ac = col[:, None] * inv
    return (jnp.cos(ar), jnp.sin(ar), jnp.cos(ac), jnp.sin(ac))


def apply_axial_rope(x, tables):
    cos_r, sin_r, cos_c, sin_c = tables
    half = x.shape[-1] // 2
    xf = x.astype(F32)

    def rot(t, cos, sin):
        t1, t2 = jnp.split(t, 2, axis=-1)
        cos = cos[:, None, :]
        sin = sin[:, None, :]
        return jnp.concatenate([t1 * cos - t2 * sin, t2 * cos + t1 * sin], axis=-1)

    out = jnp.concatenate([rot(xf[..., :half], cos_r, sin_r), rot(xf[..., half:], cos_c, sin_c)], axis=-1)
    return out.astype(x.dtype)


def dense_gqa(q, k, v, sink=None):
    b, lq, h, dq = q.shape
    kv = k.shape[2]
    g = h // kv
    dv = v.shape[-1]
    scale = dq ** -0.5
    qb = q.reshape(b, lq // Q_BLOCK, Q_BLOCK, kv, g, dq).transpose(1, 0, 2, 3, 4, 5)

    def one_block(qblk):
        s = jnp.einsum('bqkgd,bskd->bkgqs', qblk, k).astype(F32) * scale
        if sink is not None:
            sk = jnp.broadcast_to(sink.astype(F32).reshape(1, kv, g, 1, 1), s.shape[:-1] + (1,))
            p = jax.nn.softmax(jnp.concatenate([s, sk], axis=-1), axis=-1)[..., :-1]
        else:
            p = jax.nn.softmax(s, axis=-1)
        return jnp.einsum('bkgqs,bskd->bqkgd', p.astype(v.dtype), v)

    o = lax.map(one_block, qb)
    return o.transpose(1, 0, 2, 3, 4, 5).reshape(b, lq, h, dv)


def banded_gqa_with_context(q, k, v, k_ctx, v_ctx, sink):
    b, l, h, d = q.shape
    kv = k.shape[2]
    g = h // kv
    nb = l // Q_BLOCK
    p_len = k_ctx.shape[1]
    scale = d ** -0.5
    qb = q.reshape(b, nb, Q_BLOCK, kv, g, d)

    def windows(t):
        tp = jnp.pad(t, ((0, 0), (Q_BLOCK, Q_BLOCK), (0, 0), (0, 0))).reshape(b, nb + 2, Q_BLOCK, kv, d)
        return jnp.concatenate([tp[:, :-2], tp[:, 1:-1], tp[:, 2:]], axis=2)

    kw, vw = windows(k), windows(v)
    qpos = jnp.arange(l).reshape(nb, Q_BLOCK)
    kpos = (jnp.arange(nb)[:, None] - 1) * Q_BLOCK + jnp.arange(3 * Q_BLOCK)[None, :]
    kp = kpos[:, None, :]
    valid = (kp >= 0) & (kp < l) & (jnp.abs(kp - qpos[:, :, None]) <= WINDOW)
    s_loc = jnp.einsum('bnqkgd,bnskd->bkgnqs', qb, kw).astype(F32) * scale
    s_loc = jnp.where(valid, s_loc, MASK_VALUE)
    s_ctx = jnp.einsum('bnqkgd,bpkd->bkgnqp', qb, k_ctx).astype(F32) * scale
    sk = jnp.broadcast_to(sink.astype(F32).reshape(1, kv, g, 1, 1, 1), s_loc.shape[:-1] + (1,))
    p = jax.nn.softmax(jnp.concatenate([s_loc, s_ctx, sk], axis=-1), axis=-1)
    p_loc = p[..., :3 * Q_BLOCK].astype(v.dtype)
    p_ctx = p[..., 3 * Q_BLOCK:3 * Q_BLOCK + p_len].astype(v.dtype)
    o = jnp.einsum('bkgnqs,bnskd->bnqkgd', p_loc, vw) + jnp.einsum('bkgnqp,bpkd->bnqkgd', p_ctx, v_ctx)
    return o.reshape(b, l, h, d)


def hgrn2_gate(z, lb):
    f = lb + (1.0 - lb) * jax.nn.sigmoid(z.astype(F32))
    return jnp.log(jnp.maximum(f, F_FLOOR)), 1.0 - f


def hgrn2_scan(q, k, v, logf, s0):
    b, l, h, dk = q.shape
    dv = v.shape[-1]
    n = l // B_CHUNK

    def chunks(t):
        return t.astype(F32).reshape(b, n, B_CHUNK, h, t.shape[-1]).transpose(0, 3, 1, 2, 4)

    qc, kc, vc, gc = chunks(q) * dk ** -0.5, chunks(k), chunks(v), chunks(logf)
    cum = jnp.cumsum(gc, axis=3)
    tri = jnp.tril(jnp.ones((B_CHUNK, B_CHUNK), bool))[:, :, None]
    diff = cum[..., :, None, :] - cum[..., None, :, :]
    decay = jnp.where(tri, jnp.exp(jnp.where(tri, diff, 0.0)), 0.0)
    attn = jnp.einsum('bhntk,bhntsk,bhnsk->bhnts', qc, decay, kc)
    o_intra = jnp.einsum('bhnts,bhnsv->bhntv', attn, vc)
    last = cum[..., -1:, :]
    upd = jnp.einsum('bhnsk,bhnsv->bhnkv', kc * jnp.exp(last - cum), vc)
    dec = jnp.exp(last[..., 0, :])

    def step(s, inp):
        d_n, u_n = inp
        return d_n[..., None] * s + u_n, s

    s_fin, s_start = lax.scan(step, s0.astype(F32), (dec.transpose(2, 0, 1, 3), upd.transpose(2, 0, 1, 3, 4)))
    s_start = s_start.transpose(1, 2, 0, 3, 4)
    o_inter = jnp.einsum('bhntk,bhnkv->bhntv', qc * jnp.exp(cum), s_start)
    o = (o_intra + o_inter).transpose(0, 2, 3, 1, 4).reshape(b, l, h, dv)
    return o.astype(v.dtype), s_fin


def hgrn2_bidir(q, k_f, logf_f, k_b, logf_b, v, s0_f, s0_b):
    o_f, s_f = hgrn2_scan(q, k_f, v, logf_f, s0_f)
    flip = lambda t: jnp.flip(t, axis=1)
    o_b, s_b = hgrn2_scan(flip(q), flip(k_b), flip(v), flip(logf_b), s0_b)
    return o_f + flip(o_b), s_f, s_b


def mixer_ab(h, w_in, w_out, sink, lb, gnorm, rope=None, ctx=None):
    bsz, l, _ = h.shape
    qa, ka, va, qb, ib, ff, fb, gb = jnp.split(h @ w_in, EVEN_SPLIT, axis=-1)
    qa = qa.reshape(bsz, l, A_HEADS, HEAD_DIM)
    ka = ka.reshape(bsz, l, A_KV, HEAD_DIM)
    va = va.reshape(bsz, l, A_KV, HEAD_DIM)
    qb = jax.nn.silu(qb).reshape(bsz, l, B_HEADS, B_DK)
    ib = ib.reshape(bsz, l, B_HEADS, B_DV)
    logf_f, kf = hgrn2_gate(ff, lb[0])
    logf_b, kb = hgrn2_gate(fb, lb[1])
    hs = (bsz, l, B_HEADS, B_DK)
    logf_f, kf, logf_b, kb = logf_f.reshape(hs), kf.reshape(hs), logf_b.reshape(hs), kb.reshape(hs)
    if ctx is None:
        oa = dense_gqa(qa, ka, va, sink)
        zeros = jnp.zeros((bsz, B_HEADS, B_DK, B_DV), F32)
        ob, s_f, s_b = hgrn2_bidir(qb, kf, logf_f, kb, logf_b, ib, zeros, zeros)
        new = (ka, va, jnp.stack([s_f, s_b], axis=1))
    else:
        k_ctx, v_ctx, s_ctx = ctx
        oa = banded_gqa_with_context(apply_axial_rope(qa, rope), apply_axial_rope(ka, rope), va, k_ctx, v_ctx, sink)
        ob, _, _ = hgrn2_bidir(qb, kf, logf_f, kb, logf_b, ib, s_ctx[:, 0], s_ctx[:, 1])
        new = None
    ob = rms_norm(ob, gnorm) * jax.nn.silu(gb.reshape(bsz, l, B_HEADS, B_DV))
    y = jnp.concatenate([oa.reshape(bsz, l, -1), ob.reshape(bsz, l, -1)], axis=-1) @ w_out
    return y, new


def mixer_cd(h, w_in, w_out, g_cq, g_ckv, w_q_up, w_kv_up, g_dq, g_dk, rope_c=None, rope_d=None, ctx=None):
    bsz, l, _ = h.shape
    cq, ckv, kpe, qd, kd, vd = jnp.split(h @ w_in, ODD_SPLIT, axis=-1)
    qc = (rms_norm(cq, g_cq) @ w_q_up).reshape(bsz, l, C_HEADS, C_NOPE + C_ROPE)
    ckv = rms_norm(ckv, g_ckv)
    kpe = kpe[:, :, None, :]
    qd = rms_norm(qd.reshape(bsz, l, D_HEADS, HEAD_DIM), g_dq)
    kd = rms_norm(kd.reshape(bsz, l, D_KV, HEAD_DIM), g_dk)
    vd = vd.reshape(bsz, l, D_KV, HEAD_DIM)

    def mla_kv(lat, pe):
        n = lat.shape[1]
        kvu = (lat @ w_kv_up).reshape(bsz, n, C_HEADS, C_NOPE + C_V)
        k = jnp.concatenate([kvu[..., :C_NOPE], jnp.broadcast_to(pe, (bsz, n, C_HEADS, C_ROPE))], axis=-1)
        return k, kvu[..., C_NOPE:]

    if ctx is None:
        kc, vc = mla_kv(ckv, kpe)
        oc = dense_gqa(qc, kc, vc)
        od = dense_gqa(qd, kd, vd)
        new = (ckv, kpe[:, :, 0], kd, vd)
    else:
        c_ctx, pe_ctx, kd_ctx, vd_ctx = ctx
        qc = jnp.concatenate([qc[..., :C_NOPE], apply_axial_rope(qc[..., C_NOPE:], rope_c)], axis=-1)
        kpe = apply_axial_rope(kpe, rope_c)
        kc, vc = mla_kv(jnp.concatenate([ckv, c_ctx], axis=1), jnp.concatenate([kpe, pe_ctx[:, :, None, :]], axis=1))
        oc = dense_gqa(qc, kc, vc)
        qd, kd = apply_axial_rope(qd, rope_d), apply_axial_rope(kd, rope_d)
        od = dense_gqa(qd, jnp.concatenate([kd, kd_ctx], axis=1), jnp.concatenate([vd, vd_ctx], axis=1))
        new = None
    y = jnp.concatenate([oc.reshape(bsz, l, -1), od.reshape(bsz, l, -1)], axis=-1) @ w_out
    return y, new


def swiglu(h, w_gate, w_up, w_down):
    return (jax.nn.silu(h @ w_gate) * (h @ w_up)) @ w_down


def modulation(cond, w, b):
    return jnp.split((jax.nn.silu(cond) @ w + b)[:, None, :], 6, axis=-1)


def setup_inputs(seed: int = 0) -> dict:
    key = jax.random.key(seed)
    ks = list(jax.random.split(key, 40))

    def nrm(shape, scale=1.0):
        return jax.random.normal(ks.pop(), shape, F32) * scale

    def gain(shape):
        return 1.0 + nrm(shape, 0.02)

    d = D_MODEL
    return {
        'x_prompt': nrm((BATCH, SEQ, d)),
        'x_sample': nrm((DEC_BATCH, DEC_SEQ, d)),
        'cache_a_k': nrm((DEC_BATCH, N_EVEN, PAST_LEN, A_KV, HEAD_DIM)),
        'cache_a_v': nrm((DEC_BATCH, N_EVEN, PAST_LEN, A_KV, HEAD_DIM)),
        'state_b': nrm((DEC_BATCH, N_EVEN, 2, B_HEADS, B_DK, B_DV), 0.5),
        'cache_c_kv': nrm((DEC_BATCH, N_ODD, PAST_LEN, C_KV_LORA)),
        'cache_c_pe': nrm((DEC_BATCH, N_ODD, PAST_LEN, C_ROPE)),
        'cache_d_k': nrm((DEC_BATCH, N_ODD, PAST_LEN, D_KV, HEAD_DIM)),
        'cache_d_v': nrm((DEC_BATCH, N_ODD, PAST_LEN, D_KV, HEAD_DIM)),
        'c': nrm((DEC_BATCH, d)),
        'c_ctx': nrm((d,)),
        'w_ada': nrm((DEPTH, d, 6 * d), 0.5 * d ** -0.5),
        'b_ada': nrm((DEPTH, 6 * d), 0.02),
        'ln_g': gain((DEPTH, 2, d)),
        'ln_b': nrm((DEPTH, 2, d), 0.02),
        'w_in_ab': nrm((N_EVEN, d, EVEN_IN), d ** -0.5),
        'a_sink': nrm((N_EVEN, A_HEADS), 0.5),
        'b_lb': nrm((N_EVEN, 2, B_HEADS * B_DK), 0.5),
        'b_gnorm': gain((N_EVEN, B_DV)),
        'w_in_cd': nrm((N_ODD, d, ODD_IN), d ** -0.5),
        'c_q_norm': gain((N_ODD, C_Q_LORA)),
        'c_kv_norm': gain((N_ODD, C_KV_LORA)),
        'c_w_q_up': nrm((N_ODD, C_Q_LORA, C_HEADS * (C_NOPE + C_ROPE)), C_Q_LORA ** -0.5),
        'c_w_kv_up': nrm((N_ODD, C_KV_LORA, C_HEADS * (C_NOPE + C_V)), C_KV_LORA ** -0.5),
        'd_q_norm': gain((N_ODD, HEAD_DIM)),
        'd_k_norm': gain((N_ODD, HEAD_DIM)),
        'w_out': nrm((DEPTH, MIX, d), BETA * MIX ** -0.5),
        'w_ffn_gate': nrm((DEPTH, d, D_FF), d ** -0.5),
        'w_ffn_up': nrm((DEPTH, d, D_FF), d ** -0.5),
        'w_ffn_down': nrm((DEPTH, D_FF, d), BETA * D_FF ** -0.5),
    }


def reference(x_prompt, x_sample, cache_a_k, cache_a_v, state_b, cache_c_kv, cache_c_pe, cache_d_k, cache_d_v,
              c, c_ctx, w_ada, b_ada, ln_g, ln_b, w_in_ab, a_sink, b_lb, b_gnorm, w_in_cd, c_q_norm, c_kv_norm,
              c_w_q_up, c_w_kv_up, d_q_norm, d_k_norm, w_out, w_ffn_gate, w_ffn_up, w_ffn_down):
    n_lat = x_sample.shape[1]
    rope_hd = axial_rope_tables(n_lat, HEAD_DIM)
    rope_c = axial_rope_tables(n_lat, C_ROPE)
    lb_w = jax.nn.softmax(b_lb.astype(F32), axis=0)
    lb_all = jnp.cumsum(lb_w, axis=0) - lb_w[:1]
    xp, xs = x_prompt, x_sample
    ak, av, sb, ckv, cpe, dk, dv = [], [], [], [], [], [], []
    for l in range(DEPTH):
        mp = modulation(c_ctx[None, :], w_ada[l], b_ada[l])
        ms = modulation(c, w_ada[l], b_ada[l])
        hp = xp * (1 + mp[1]) + mp[0]
        hs = xs * (1 + ms[1]) + ms[0]
        if l % 2 == 0:
            e = l // 2
            wts = (w_in_ab[e], w_out[l], a_sink[e], lb_all[e], b_gnorm[e])
            yp, (k_new, v_new, s_new) = mixer_ab(hp, *wts)
            ys, _ = mixer_ab(hs, *wts, rope=rope_hd, ctx=(cache_a_k[:, e], cache_a_v[:, e], state_b[:, e]))
            ak.append(k_new)
            av.append(v_new)
            sb.append(s_new)
        else:
            o = l // 2
            wts = (w_in_cd[o], w_out[l], c_q_norm[o], c_kv_norm[o], c_w_q_up[o], c_w_kv_up[o], d_q_norm[o], d_k_norm[o])
            yp, (c_new, pe_new, k_new, v_new) = mixer_cd(hp, *wts)
            ys, _ = mixer_cd(hs, *wts, rope_c=rope_c, rope_d=rope_hd,
                             ctx=(cache_c_kv[:, o], cache_c_pe[:, o], cache_d_k[:, o], cache_d_v[:, o]))
            ckv.append(c_new)
            cpe.append(pe_new)
            dk.append(k_new)
            dv.append(v_new)
        xp = layer_norm(ALPHA * xp + mp[2] * yp, ln_g[l, 0], ln_b[l, 0])
        xs = layer_norm(ALPHA * xs + ms[2] * ys, ln_g[l, 0], ln_b[l, 0])
        hp = xp * (1 + mp[4]) + mp[3]
        hs = xs * (1 + ms[4]) + ms[3]
        xp = layer_norm(ALPHA * xp + mp[5] * swiglu(hp, w_ffn_gate[l], w_ffn_up[l], w_ffn_down[l]), ln_g[l, 1], ln_b[l, 1])
        xs = layer_norm(ALPHA * xs + ms[5] * swiglu(hs, w_ffn_gate[l], w_ffn_up[l], w_ffn_down[l]), ln_g[l, 1], ln_b[l, 1])
    new_cache_a_k = jnp.stack(ak, axis=1)
    new_cache_a_v = jnp.stack(av, axis=1)
    new_state_b = jnp.stack(sb, axis=1).astype(x_prompt.dtype)
    new_cache_c_kv = jnp.stack(ckv, axis=1)
    new_cache_c_pe = jnp.stack(cpe, axis=1)
    new_cache_d_k = jnp.stack(dk, axis=1)
    new_cache_d_v = jnp.stack(dv, axis=1)
    return (xp, xs, new_cache_a_k, new_cache_a_v, new_state_b, new_cache_c_kv, new_cache_c_pe, new_cache_d_k, new_cache_d_v)
```

```python
from contextlib import ExitStack
import math
import numpy as np
import ml_dtypes
import concourse.bass as bass
import concourse.mybir as mybir
from concourse.bass_utils import run_bass_kernel_spmd

F32 = mybir.dt.float32
BF16 = mybir.dt.bfloat16
AF = mybir.ActivationFunctionType
ALU = mybir.AluOpType

D = 1024
NCH = 8
T = 1024
DEPTH = 4
N_EVEN = 2
N_ODD = 2
PAST = 512
HD = 64
GRID_W = 64
D_FF = 2816
NFF = 22
EVEN_IN = 3328
ODD_IN = 1696
ALPHA = (2 * DEPTH) ** 0.25
LN_EPS = 1e-5 / (ALPHA * ALPHA)
RMS_EPS = 1e-6
CH = 64
NSLOT = 6
STRICT_SAME_ENGINE = False
WT = 256

ENGS = ("pe", "act", "dve", "pool", "sp")


class Op:
    __slots__ = ("eng", "fn", "deps", "signal", "target", "sem", "is_dma", "chan", "lane", "dbg")

    def __init__(self, eng, fn):
        self.eng = eng
        self.fn = fn
        self.deps = {}
        self.signal = False
        self.target = None
        self.sem = None
        self.is_dma = False
        self.chan = None
        self.lane = None
        self.dbg = None


class Sched:
    def __init__(self, nc):
        self.nc = nc
        self.q = {e: [] for e in ENGS}
        self.last_w = {}
        self.readers = {}
        self.chans = {}
        self.out_chans = set()
        self.arena_key = True
        self.lane = None
        self.lq = None
        self.session = 0

    def _push(self, eng, op, reads=(), writes=()):
        if self.lane is None:
            self.q[eng].append(op)
        else:
            op.lane = (self.session, self.lane)
            op.dbg = (tuple(reads), tuple(writes))
            self.lq[self.lane][eng].append(op)
            self.lseq[self.lane].append((eng, op))

    def lanes_begin(self, n):
        self.session += 1
        self.lq = [{e: [] for e in ENGS} for _ in range(n)]
        self.lseq = [[] for _ in range(n)]

    def lanes_merge(self):
        for lane in self.lq:
            for e in ENGS:
                for op in lane[e]:
                    for d in op.deps:
                        if d.lane is not None and d.lane[0] == op.lane[0] and d.lane[1] != op.lane[1]:
                            raise RuntimeError("cross-lane dependency: %r -> %r" % (op.dbg, d.dbg))
        seqs = self.lseq
        n = [len(q) for q in seqs]
        idx = [0] * len(seqs)
        while any(idx[i] < n[i] for i in range(len(seqs))):
            best = None
            for i in range(len(seqs)):
                if idx[i] < n[i]:
                    frac = idx[i] / float(n[i])
                    if best is None or frac < best[0]:
                        best = (frac, i)
            i = best[1]
            e, op = seqs[i][idx[i]]
            self.q[e].append(op)
            idx[i] += 1
        self.lseq = None
        self.lq = None
        self.lane = None

    def _track(self, op, reads, writes):
        for r in reads:
            w = self.last_w.get(r)
            if w is not None and w is not op:
                op.deps[w] = True
            if isinstance(r, tuple) and r[0] == "ps":
                for rd in self.readers.get(r, ()):
                    if rd is not op and rd.eng != op.eng:
                        op.deps.setdefault(rd, False)
            self.readers.setdefault(r, []).append(op)
        for r in writes:
            w = self.last_w.get(r)
            if w is not None and w is not op:
                op.deps.setdefault(w, "W")
            for rd in self.readers.get(r, ()):
                if rd is not op:
                    op.deps.setdefault(rd, False)
            self.last_w[r] = op
            self.readers[r] = []

    def add(self, eng, fn, reads=(), writes=(), arena=True):
        op = Op(eng, fn)
        reads = list(reads)
        if arena:
            reads.append("arena")
        self._track(op, reads, writes)
        self._push(eng, op, reads, writes)
        return op

    def mark(self, label):
        op = Op("pe", None)
        op.chan = ("mark", label)
        self.q["pe"].append(op)

    def fence(self, eng, fn):
        op = Op(eng, fn)
        self._track(op, [], ["arena"])
        self.q[eng].append(op)
        return op

    def dma(self, eng, fn, reads=(), writes=(), chan=None, is_out=False):
        op = Op(eng, fn)
        op.is_dma = True
        self._track(op, reads, writes)
        if chan is None:
            chan = ("chan",) + tuple(writes if writes else reads)
        op.chan = chan
        st = self.chans.setdefault(chan, [None, 0])
        st[1] += 16
        op.target = st[1]
        if is_out:
            self.out_chans.add(chan)
        self._push(eng, op, reads, writes)
        return op

    def emit(self, stack):
        nc = self.nc
        esem = {e: stack.enter_context(nc.semaphore("s_" + e)) for e in ENGS if e != "sp"}
        for i, (k, st) in enumerate(self.chans.items()):
            st[0] = stack.enter_context(nc.semaphore("c%d" % i))
        for e in ENGS:
            for op in self.q[e]:
                keep = {}
                for d, raw in op.deps.items():
                    if d.is_dma:
                        keep[d] = raw
                    elif d.eng == op.eng and not op.is_dma:
                        if op.eng == "pe":
                            continue
                        if raw or STRICT_SAME_ENGINE:
                            keep[d] = raw
                    else:
                        keep[d] = raw
                op.deps = keep
                for d in keep:
                    if not d.is_dma:
                        d.signal = True
        for e in ENGS:
            cnt = 0
            for op in self.q[e]:
                if op.is_dma:
                    op.sem = self.chans[op.chan][0]
                elif op.signal:
                    cnt += 1
                    op.target = cnt
                    op.sem = esem[e]
        block = stack.enter_context(nc.Block())
        finals = [(self.chans[c][0], self.chans[c][1]) for c in self.out_chans]

        self.marks = []

        class _Cnt:
            def __init__(self, eh):
                self.eh = eh
                self.n = 0

            def matmul(self, *a, **k):
                self.n += 1
                return self.eh.matmul(*a, **k)

            def transpose(self, *a, **k):
                self.n += 1
                return self.eh.transpose(*a, **k)

            def wait_ge(self, *a, **k):
                return self.eh.wait_ge(*a, **k)

        def run(e, eh):
            waited = {}
            if e == "pe":
                eh = _Cnt(eh)
            for op in self.q[e]:
                if op.fn is None:
                    self.marks.append((op.chan[1], eh.n))
                    continue
                need = {}
                for d in op.deps:
                    key = id(d.sem)
                    if need.get(key, (None, 0))[1] < d.target:
                        need[key] = (d.sem, d.target)
                for key, (sem, tgt) in need.items():
                    if waited.get(key, 0) >= tgt:
                        continue
                    eh.wait_ge(sem, tgt)
                    waited[key] = tgt
                ins = op.fn(eh)
                if op.is_dma:
                    ins.then_inc(op.sem, 16)
                elif op.signal:
                    ins.then_inc(op.sem, 1)
            if e == "sp":
                for sem, tgt in finals:
                    eh.wait_ge(sem, tgt)

        @block.tensor
        def _(eh):
            run("pe", eh)

        @block.scalar
        def _(eh):
            run("act", eh)

        @block.vector
        def _(eh):
            run("dve", eh)

        @block.gpsimd
        def _(eh):
            run("pool", eh)

        @block.sync
        def _(eh):
            run("sp", eh)


def _rope_tables(rot_dim):
    n_rows = T // GRID_W
    t = np.arange(T)
    row = (t // GRID_W).astype(np.float32)
    col = (t % GRID_W).astype(np.float32)
    quarter = rot_dim // 4
    inv = (10000.0 ** (-np.arange(quarter, dtype=np.float32) / quarter)).astype(np.float32)
    ar = row[None, :] * inv[:, None]
    ac = col[None, :] * inv[:, None]
    cos = np.zeros((rot_dim, T), np.float32)
    sin = np.zeros((rot_dim, T), np.float32)
    partner = np.zeros(rot_dim, np.int64)
    half = rot_dim // 2
    for base, ang in ((0, ar), (half, ac)):
        for i in range(quarter):
            d1 = base + i
            d2 = base + quarter + i
            cos[d1] = np.cos(ang[i])
            cos[d2] = np.cos(ang[i])
            sin[d1] = -np.sin(ang[i])
            sin[d2] = np.sin(ang[i])
            partner[d1] = d2
            partner[d2] = d1
    return cos, sin, partner


def _host_consts():
    c = {}
    c["ident"] = np.eye(128, dtype=np.float32)
    cos, sin, partner = _rope_tables(64)
    c["cos_hd"] = np.concatenate([cos, cos], 0)
    c["sin_hd"] = np.concatenate([sin, sin], 0)
    pm = np.zeros((128, 128), np.float32)
    for hh in range(2):
        for m in range(64):
            pm[hh * 64 + partner[m], hh * 64 + m] = 1.0
    c["perm_hd"] = pm
    cos, sin, partner = _rope_tables(32)
    cc = np.zeros((128, T), np.float32)
    sc = np.zeros((128, T), np.float32)
    pc = np.zeros((128, 128), np.float32)
    for b in (0, 64):
        cc[b:b + 32] = cos
        sc[b:b + 32] = sin
        for m in range(32):
            pc[b + partner[m], b + m] = 1.0
    c["cos_c"] = cc
    c["sin_c"] = sc
    c["perm_c"] = pc
    k = np.arange(128)[:, None]
    q = np.arange(128)[None, :]
    c["m_prev"] = (k >= q).astype(np.float32)
    c["m_next"] = (k <= q).astype(np.float32)
    s = np.arange(CH)[:, None]
    tt = np.arange(CH)[None, :]
    hm = np.zeros((128, 2 * 128), np.float32)
    for b in (0, 64):
        hm[b:b + CH, b:b + CH] = (s <= tt)
        hm[b:b + CH, 128 + b:128 + b + CH] = (s >= tt)
    c["hmask"] = hm.astype(np.uint8)
    bd = np.zeros((128, 128), np.float32)
    bd[0:64, 0:64] = 1.0
    bd[64:128, 64:128] = 1.0
    c["blockdiag"] = bd
    return c


def _fm(v):
    v = np.asarray(v, np.float32)
    return np.ascontiguousarray(v.reshape(-1, 128).T)


VC = {}
_off = 0
for _name, _n in (("ln_g", 64), ("ln_b", 64), ("b_lb", 16), ("gnorm", 2), ("cqn", 6), ("dqn", 2), ("dkn", 2),
                  ("sink", 16), ("b_ada", 192), ("cond", 16)):
    VC[_name] = _off
    _off += _n
NVEC = _off


def _pack_vecs(inp, b):
    v = np.zeros((128, NVEC), np.float32)
    for l in range(DEPTH):
        for j in range(2):
            o = (l * 2 + j) * 8
            v[:, VC["ln_g"] + o:VC["ln_g"] + o + 8] = _fm(inp["ln_g"][l, j])
            v[:, VC["ln_b"] + o:VC["ln_b"] + o + 8] = _fm(inp["ln_b"][l, j])
        v[:, VC["b_ada"] + l * 48:VC["b_ada"] + (l + 1) * 48] = _fm(inp["b_ada"][l])
    for e in range(N_EVEN):
        for d in range(2):
            o = (e * 2 + d) * 4
            v[:, VC["b_lb"] + o:VC["b_lb"] + o + 4] = _fm(inp["b_lb"][e, d])
        v[:, VC["gnorm"] + e] = inp["b_gnorm"][e]
        v[:, VC["sink"] + e * 8:VC["sink"] + e * 8 + 8] = np.broadcast_to(inp["a_sink"][e][None, :], (128, 8))
    for o in range(N_ODD):
        v[:, VC["cqn"] + o * 3:VC["cqn"] + o * 3 + 3] = _fm(inp["c_q_norm"][o])
        v[:, VC["dqn"] + o] = np.tile(inp["d_q_norm"][o], 2)
        v[:, VC["dkn"] + o] = np.tile(inp["d_k_norm"][o], 2)
    cond2 = np.stack([inp["c_ctx"], inp["c"][b]], 0)
    for c2 in range(2):
        v[:, VC["cond"] + c2 * 8:VC["cond"] + c2 * 8 + 8] = _fm(cond2[c2])
    return v


class StopBuild(Exception):
    pass


class KB:
    def stage(self, n):
        if self.cfg.get("stage", 99) <= n:
            raise StopBuild()

    def dbg(self, name, ap, keys):
        if name in self.cfg.get("taps", {}):
            self.dma("sp", self.dram[name], ap, list(keys) + ["arena"], [], is_out=True, chan=("och", name))

    def __init__(self, nc, st, cfg):
        self.nc = nc
        self.st = st
        self.cfg = cfg
        self.S = Sched(nc)
        self.rr = {}
        self.wcount = 0
        self.dram = {}
        self._decl_dram()
        self._alloc()

    def din(self, name, shape, dt=F32):
        self.dram[name] = self.nc.dram_tensor(name, list(shape), dt, kind="ExternalInput").ap()
        return self.dram[name]

    def dout(self, name, shape):
        self.dram[name] = self.nc.dram_tensor(name, list(shape), F32, kind="ExternalOutput").ap()
        return self.dram[name]

    def _decl_dram(self):
        d = self.din
        d("xp", [T, D]); d("xs", [T, D])
        d("cakd", [N_EVEN, PAST, 256]); d("cav", [N_EVEN, PAST, 128])
        d("stb", [N_EVEN, 2, 4, 128, 128])
        d("cckv", [N_ODD, PAST, 256]); d("ccpe", [N_ODD, PAST, 32])
        d("cdkd", [N_ODD, PAST, 512]); d("cdv", [N_ODD, PAST, 256])
        d("w_ada", [DEPTH, D, 6 * D])
        d("w_in_ab", [N_EVEN, D, EVEN_IN + 256]); d("w_in_cd", [N_ODD, D, ODD_IN + 512])
        d("c_w_q_up", [N_ODD, 384, 768]); d("c_w_kv_up", [N_ODD, 256, 1024])
        d("w_out", [DEPTH, D, D])
        d("w_ffn_gate", [DEPTH, D, D_FF]); d("w_ffn_up", [DEPTH, D, D_FF]); d("w_ffn_down", [DEPTH, D_FF, D])
        d("vecs", [128, NVEC]); d("ckvn_b", [128, N_ODD * 256])
        for k, v in _host_consts().items():
            d(k, v.shape, mybir.dt.uint8 if v.dtype == np.uint8 else F32)
        o = self.dout
        o("yp", [T, D]); o("ys", [T, D])
        o("nak", [4, N_EVEN, 256, 128]); o("nav", [4, N_EVEN, 256, 128])
        o("nsb", [4, N_EVEN, 2, 4, 128, 128])
        o("nckv", [4, N_ODD, 256, 256]); o("ncpe", [4, N_ODD, 256, 32])
        o("ndk", [4, N_ODD, 256, 256]); o("ndv", [4, N_ODD, 256, 256])
        for name, shape in self.cfg.get("taps", {}).items():
            if name.endswith("_bf"):
                self.dram[name] = self.nc.dram_tensor(name, list(shape), BF16, kind="ExternalOutput").ap()
            else:
                o(name, shape)

    def sb(self, name, shape, dt):
        return self.st.enter_context(self.nc.sbuf_tensor("s_" + name, list(shape), dt))

    def _alloc(self):
        sb = self.sb
        self.x = sb("xres", [128, NCH, T], F32)
        self.hm = sb("hm", [128, NCH, T], BF16)
        self.wsl = [sb("wsl%d" % i, [128, 8, WT], BF16) for i in range(NSLOT)]
        self.ident = sb("ident", [128, 128], F32)
        self.identb = sb("identb", [128, 128], BF16)
        self.ones = sb("ones", [128, 128], BF16)
        self.bd = sb("bd", [128, 128], BF16)
        self.perm_hd = sb("perm_hd", [128, 128], BF16)
        self.perm_c = sb("perm_c", [128, 128], BF16)
        self.m_prev = sb("m_prev", [128, 128], BF16)
        self.m_next = sb("m_next", [128, 128], BF16)
        self.hmask = sb("hmask", [128, 2, 128], mybir.dt.uint8)
        self.epsc = sb("epsc", [128, 2], F32)
        self.cos_hd = sb("cos_hd", [128, T], BF16)
        self.sin_hd = sb("sin_hd", [128, T], BF16)
        self.cos_c = sb("cos_c", [128, T], BF16)
        self.sin_c = sb("sin_c", [128, T], BF16)
        self.scanmask = sb("scanmask", [128, 512], F32)
        self.vecs = sb("vecs", [128, NVEC], F32)
        self.ckvn_b = sb("ckvn_b", [128, N_ODD * 256], F32)
        self.modv = sb("modv", [128, DEPTH, 48, 2], F32)
        self.lnf = sb("lnf", [128, DEPTH, 2, 2, 8, 2], F32)
        self.lb = sb("lbv", [128, N_EVEN, 2, 2, 4], F32)
        self.esink = sb("esink", [128, 16], F32)
        self.esinkU = sb("esinkU", [128, 2, 2, 4], F32)
        self.scond = sb("scond", [128, 8, 2], BF16)
        self.zb = sb("zb", [128, 8, 512], BF16)
        self.zsq = sb("zsq", [128, 8, 512], BF16)
        self.stg = [self.zb[:, 0:4, :].bitcast(F32).rearrange("p a b -> p (a b)"),
                    self.zb[:, 4:8, :].bitcast(F32).rearrange("p a b -> p (a b)")]
        self.mean = sb("mean", [128, 512], F32)
        self.rstd = sb("rstd", [128, 512], F32)
        self.tA = sb("tA", [128, 512], F32)
        self.xn = [sb("xn%d" % i, [128, 512], F32) for i in range(2)]
        self.ptb = sb("ptb", [128, 4, 512], BF16)
        self.rd = sb("rd", [128, 2, 512], F32)
        self.cst = sb("cst", [128, 2, 288], F32)
        self.dummy = sb("dummy", [128, 8], F32)
        self.ARENA = 20224
        self.arena = sb("arena", [128, self.ARENA], F32)
        self.pb = [self.st.enter_context(self.nc.psum_tensor("pb%d" % i, [128, 512], F32)) for i in range(8)]

    def carve_reset(self):
        self.aoff = 0

    def carve(self, shape, dt):
        n = int(np.prod(shape))
        words = n if dt == F32 else (n + 1) // 2
        a = self.arena[:, self.aoff:self.aoff + words]
        self.aoff += words
        assert self.aoff <= self.ARENA, ("arena overflow", self.aoff)
        if dt != F32:
            a = a.bitcast(dt)
        if len(shape) == 2:
            a = a.rearrange("p (a b) -> p a b", b=shape[1])
        elif len(shape) == 3:
            a = a.rearrange("p (a b c) -> p a b c", b=shape[1], c=shape[2])
        return a

    def fence(self):
        dm = self.dummy
        self.S.fence("dve", lambda e: e.memset(dm[:, 0:1], 0.0))

    def rot(self, pool, n):
        i = self.rr.get(pool, 0)
        self.rr[pool] = i + 1
        return i % n

    def psb(self, pool):
        lb = getattr(self, "lanebanks", None)
        ln = self.S.lane
        if lb is not None and ln is not None:
            banks = lb[ln][pool]
            return banks[self.rot("%s_l%d" % (pool, ln), len(banks))]
        if pool == "mm":
            return self.rot("mm", 4)
        if pool == "st":
            return 4 + self.rot("st", 2)
        return 6 + self.rot("aux", 2)

    def ptslot(self):
        ln = self.S.lane
        if ln is None:
            return self.rot("ptb", 4)
        return 2 * ln + self.rot("ptb_l%d" % ln, 2)

    def rdslot(self):
        ln = self.S.lane
        if ln is None:
            return self.rot("rdb", 2)
        return ln

    def run_lanes(self, gens, banks, slots=None):
        self.lanebanks = banks
        self.laneslots = slots
        self.S.lanes_begin(len(gens))
        for i, gen in enumerate(gens):
            self.S.lane = i
            for _ in gen:
                pass
        self.S.lane = None
        self.S.lanes_merge()
        self.lanebanks = None
        self.laneslots = None

    def mm(self, out, ops, R, W):
        n = len(ops)

        def fn(e):
            ins = None
            for i, (l, r) in enumerate(ops):
                ins = e.matmul(out, l, r, start=(i == 0), stop=(i == n - 1))
            return ins
        return self.S.add("pe", fn, R, W)

    def tr(self, out, in_, ident, R, W):
        return self.S.add("pe", lambda e: e.transpose(out, in_, ident), R, W)

    def act(self, out, in_, func, R, W, bias=0.0, scale=1.0, eng="act"):
        return self.S.add(eng, lambda e: e.activation(out=out, in_=in_, func=func, bias=bias, scale=scale), R, W)

    def tt(self, eng, out, in0, in1, op, R, W):
        return self.S.add(eng, lambda e: e.tensor_tensor(out=out, in0=in0, in1=in1, op=op), R, W)

    def ts(self, eng, out, in0, s1, s2, op0, op1, R, W):
        if s2 is None:
            return self.S.add(eng, lambda e: e.tensor_scalar(out=out, in0=in0, scalar1=s1, scalar2=None, op0=op0), R, W)
        return self.S.add(eng, lambda e: e.tensor_scalar(out=out, in0=in0, scalar1=s1, scalar2=s2, op0=op0, op1=op1), R, W)

    def stt(self, eng, out, in0, scalar, in1, op0, op1, R, W):
        return self.S.add(eng, lambda e: e.scalar_tensor_tensor(out=out, in0=in0, scalar=scalar, in1=in1, op0=op0, op1=op1), R, W)

    def cp(self, eng, out, in_, R, W):
        if eng == "act":
            return self.S.add(eng, lambda e: e.copy(out=out, in_=in_), R, W)
        return self.S.add(eng, lambda e: e.tensor_copy(out=out, in_=in_), R, W)

    def dma(self, eng, out, in_, R, W, is_out=False, chan=None, slow=False):
        if slow:
            return self.S.dma(eng, lambda e: e.dma_start(out=out, in_=in_, allow_slow_non_contiguous=True), R, W, chan=chan, is_out=is_out)
        return self.S.dma(eng, lambda e: e.dma_start(out=out, in_=in_), R, W, chan=chan, is_out=is_out)

    def wtile(self, src2d, r0, nk, c0, ncols):
        ln = self.S.lane
        ls = getattr(self, "laneslots", None)
        if ln is None or ls is None:
            i = self.wcount % NSLOT
            self.wcount += 1
        else:
            i = ls[ln][self.rot("w_l%d" % ln, len(ls[ln]))]
        slot = self.wsl[i]
        key = ("w", i)
        src = src2d[r0:r0 + nk * 128, c0:c0 + ncols].rearrange("(kc p) n -> p kc n", p=128)
        self.dma("pool", slot[:, 0:nk, 0:ncols], src, [], [key], chan=("wch", i))
        return slot, key

    def proj_fm(self, w2d, c0, ncols, src, srckey, nk, handler, halves=(0, 1), nmm=512, srcf=None):
        col = c0
        oc_i = 0
        while col < c0 + ncols:
            w = min(WT, c0 + ncols - col)
            slot, wkey = self.wtile(w2d, 0, nk, col, w)
            for j in range(w // 128):
                for th in halves:
                    b = self.psb("mm")
                    out = self.pb[b][:, 0:nmm]
                    if srcf is None:
                        ops = [(slot[:, kc, j * 128:(j + 1) * 128], src[:, kc, th * nmm:(th + 1) * nmm]) for kc in range(nk)]
                        sk_ = [(srckey, kc) for kc in range(nk)]
                    else:
                        ops = [(slot[:, kc, j * 128:(j + 1) * 128], srcf(kc)[0][:, th * nmm:(th + 1) * nmm]) for kc in range(nk)]
                        sk_ = [k_ for kc in range(nk) for k_ in srcf(kc)[1]]
                    self.mm(out, ops, [wkey] + sk_, [("ps", b)])
                    handler(oc_i, th, out, ("ps", b))
                oc_i += 1
            col += w

    def setup(self):
        dr = self.dram
        self.dma("sp", self.ident[:], dr["ident"], [], ["ident"])
        self.dma("sp", self.vecs[:], dr["vecs"], [], ["vecs"])
        self.dma("act", self.ckvn_b[:], dr["ckvn_b"], [], ["ckvn_b"])
        for name, t in (("ident", self.identb), ("blockdiag", self.bd), ("perm_hd", self.perm_hd), ("perm_c", self.perm_c),
                        ("m_prev", self.m_prev), ("m_next", self.m_next), ("cos_hd", self.cos_hd),
                        ("sin_hd", self.sin_hd), ("cos_c", self.cos_c), ("sin_c", self.sin_c)):
            self.dma("pool", t[:], dr[name], [], ["c_" + name], chan=("cch", name))
        self.dma("act", self.hmask[:].rearrange("p a b -> p (a b)"), dr["hmask"], [], ["c_hmask"])
        ones, sm = self.ones, self.scanmask
        self.S.add("dve", lambda e: e.memset(ones[:], 1.0), [], ["ones"], arena=False)
        epsc = self.epsc
        self.S.add("dve", lambda e: e.memset(epsc[:, 0:1], LN_EPS), [], ["epsc0"], arena=False)
        self.S.add("dve", lambda e: e.memset(epsc[:, 1:2], RMS_EPS), [], ["epsc1"], arena=False)
        self.S.add("dve", lambda e: e.memset(sm[:], 1.0), [], ["scanmask"], arena=False)
        self.S.add("dve", lambda e: e.memset(sm[:].rearrange("p (n c) -> p n c", c=CH)[:, :, 0:1], 0.0), [], ["scanmask", "scanmask2"], arena=False)
        v = self.vecs
        c0 = VC["cond"]
        self.act(self.scond[:], v[:, c0:c0 + 16].rearrange("p (c k) -> p k c", c=2), AF.Silu, ["vecs"], ["scond"])
        self.act(self.esink[:], v[:, VC["sink"]:VC["sink"] + 16], AF.Exp, ["vecs"], ["esink"])
        es4 = self.esink[:].rearrange("p (e k i) -> p e k i", e=2, k=2)
        for pos, idx in enumerate((0, 2, 1, 3)):
            self.cp("dve", self.esinkU[:, :, :, pos], es4[:, :, :, idx], ["esink"], ["esinkU"])
        lb = self.lb
        self.S.add("dve", lambda e: e.memset(lb[:, 0, :, 0, :], 0.0), [], ["lb0a"], arena=False)
        self.S.add("dve", lambda e: e.memset(lb[:, 0, :, 1, :], 1.0), [], ["lb0b"], arena=False)
        b0 = VC["b_lb"]
        dtmp = self.dummy
        self.tt("dve", dtmp[:, 0:8], v[:, b0 + 8:b0 + 16], v[:, b0:b0 + 8], ALU.subtract, ["vecs"], ["dummy"])
        self.act(lb[:, 1, :, 0, :], dtmp[:, 0:8].rearrange("p (d h) -> p d h", d=2), AF.Sigmoid, ["dummy"], ["lb1a"])
        self.ts("dve", lb[:, 1, :, 1, :], lb[:, 1, :, 0, :], -1.0, 1.0, ALU.mult, ALU.add, ["lb1a"], ["lb1b"])
        self.lbkeys = ["lb0a", "lb0b", "lb1a", "lb1b"]

    def mods(self, l):
        for _ in self.mods_gen(l):
            pass

    def mods_gen(self, l):
        w2d = self.dram["w_ada"][l]
        b = self.psb("aux")
        acc = self.pb[b]
        for t in range(24):
            slot, wkey = self.wtile(w2d, 0, 8, t * WT, WT)
            for j in range(2):
                oc = t * 2 + j
                ops = [(slot[:, kc, j * 128:(j + 1) * 128], self.scond[:, kc, :]) for kc in range(8)]
                self.mm(acc[:, oc * 2:oc * 2 + 2], ops, [wkey, "scond"], [("ps", b)])
            yield
        v = self.vecs
        bcol = VC["b_ada"] + l * 48
        for c in range(2):
            src = acc[:, 0:96].rearrange("p (o c) -> p o c", c=2)[:, :, c]
            self.tt("dve", self.modv[:, l, :, c], src, v[:, bcol:bcol + 48], ALU.add, [("ps", b), "vecs"], [("modv", l, c)])
        mk = [("modv", l, 0), ("modv", l, 1)]
        for which in (1, 4):
            m = self.modv[:, l, which * 8:(which + 1) * 8, :]
            self.ts("dve", m, m, 1.0, None, ALU.add, None, mk, mk)
        for which in (2, 5):
            m = self.modv[:, l, which * 8:(which + 1) * 8, :]
            self.ts("dve", m, m, 1.0 / ALPHA, None, ALU.mult, None, mk, mk)

    def lnfold(self, l, whichs=(0, 1)):
        v = self.vecs
        for which in whichs:
            if which == 0:
                ls, sc_i, sh_i = l, 4, 3
            else:
                if l == DEPTH - 1:
                    continue
                ls, sc_i, sh_i = l + 1, 1, 0
            gcol = VC["ln_g"] + (l * 2 + which) * 8
            bcol = VC["ln_b"] + (l * 2 + which) * 8
            for c in range(2):
                sc = self.modv[:, ls, sc_i * 8:(sc_i + 1) * 8, c]
                sh = self.modv[:, ls, sh_i * 8:(sh_i + 1) * 8, c]
                G = self.lnf[:, l, which, 0, :, c]
                B = self.lnf[:, l, which, 1, :, c]
                R = [("modv", ls, c), "vecs"]
                self.tt("dve", G, sc, v[:, gcol:gcol + 8], ALU.mult, R, [("lnf", l, which, c, 0)])
                self.tt("dve", B, sc, v[:, bcol:bcol + 8], ALU.mult, R, [("lnf", l, which, c, 1)])
                self.tt("dve", B, B, sh, ALU.add, R + [("lnf", l, which, c, 1)], [("lnf", l, which, c, 1)])

    def stgkeys(self, i):
        return [("zb", k) for k in range(i * 4, i * 4 + 4)]

    def load_x(self, g):
        xd = self.dram["xp" if g == 0 else "xs"]
        for tt in range(8):
            stg = self.stg[tt % 2]
            sk = self.stgkeys(tt % 2)
            self.dma("sp" if tt % 2 == 0 else "act", stg, xd[tt * 128:(tt + 1) * 128, :], [], sk)
            for half in range(2):
                b = self.psb("mm")
                bank = self.pb[b]
                ident = self.ident

                def fn(e, bank=bank, stg=stg, half=half, ident=ident):
                    ins = None
                    for c4 in range(4):
                        c = half * 4 + c4
                        ins = e.transpose(bank[:, c4 * 128:(c4 + 1) * 128], stg[:, c * 128:(c + 1) * 128], ident[:])
                    return ins
                self.S.add("pe", fn, sk + ["ident"], [("ps", b)])
                dst = self.x[:, half * 4:(half + 1) * 4, tt * 128:(tt + 1) * 128]
                self.cp("act" if half == 0 else "dve", dst, bank[:].rearrange("p (a b) -> p a b", b=128),
                        [("ps", b)], [("x", c, tt // 4) for c in range(half * 4, half * 4 + 4)])

    def modulate0(self, g, l):
        for c in range(8):
            sc = self.modv[:, l, 8 + c, g:g + 1]
            sh = self.modv[:, l, c, g:g + 1]
            R = [("x", c, 0), ("x", c, 1), ("modv", l, g)]
            if (c % 2 == 0 or self.cfg.get("mod_act", False)) and not self.cfg.get("mod_dve", False):
                self.act(self.hm[:, c, :], self.x[:, c, :], AF.Identity, R, [("hm", c)], bias=sh, scale=sc)
            else:
                self.ts("dve", self.hm[:, c, :], self.x[:, c, :], sc, sh, ALU.mult, ALU.add, R, [("hm", c)])

    def store_x(self, g):
        yd = self.dram["yp" if g == 0 else "ys"]
        for tt in range(8):
            stg = self.stg[tt % 2]
            sk = self.stgkeys(tt % 2)
            for half in range(2):
                b = self.psb("mm")
                bank = self.pb[b]
                x, ident = self.x, self.ident

                def fn(e, bank=bank, half=half, tt=tt, x=x, ident=ident):
                    ins = None
                    for c4 in range(4):
                        c = half * 4 + c4
                        ins = e.transpose(bank[:, c4 * 128:(c4 + 1) * 128], x[:, c, tt * 128:(tt + 1) * 128], ident[:])
                    return ins
                self.S.add("pe", fn, [("x", c, tt // 4) for c in range(half * 4, half * 4 + 4)] + ["ident"], [("ps", b)])
                self.cp("act" if half == 0 else "dve", stg[:, half * 512:(half + 1) * 512], bank[:], [("ps", b)], sk)
            self.dma("sp", yd[tt * 128:(tt + 1) * 128, :], stg, sk, [], is_out=True, chan=("och", "y", tt % 2))

    def tap(self, name):
        if name in self.cfg.get("taps", {}):
            self.dma("sp", self.dram[name], self.x[:], [("x", c, h_) for c in range(8) for h_ in range(2)], [], is_out=True, chan=("och", name))

    def ln_z(self, l, gate_i, g):
        def h(oc, th, ps, pskey):
            xs = self.x[:, oc, th * 512:(th + 1) * 512]
            gc = self.modv[:, l, gate_i * 8 + oc, g:g + 1]
            self.stt("dve", xs, ps, gc, xs, ALU.mult, ALU.add, [pskey, ("x", oc, th), ("modv", l, g)], [("x", oc, th)])
        return h

    def ln_finish(self, g, l, which, last):
        for th in range(2):
            self.ln_a(th)
            self.ln_b(g, l, which, last, th)

    def ln_a(self, th):
        sl = slice(th * 512, (th + 1) * 512)
        for oc in range(8):
            xs = self.x[:, oc, sl]
            self.cp("act" if oc % 2 == 0 else "dve", self.zb[:, oc, :], xs, [("x", oc, th)], [("zb", oc)])
            if oc % 2 == 1:
                self.act(self.zsq[:, oc, :], xs, AF.Square, [("x", oc, th)], [("zsq", oc)])
            else:
                self.tt("dve", self.zsq[:, oc, :], xs, xs, ALU.mult, [("x", oc, th)], [("zsq", oc)])

    def ln_b(self, g, l, which, last, th):
        v = self.vecs
        sl = slice(th * 512, (th + 1) * 512)
        b1 = self.psb("st")
        b2 = self.psb("st")
        self.mm(self.pb[b1][:], [(self.ones[:], self.zb[:, oc, :]) for oc in range(8)], ["ones"] + [("zb", oc) for oc in range(8)], [("ps", b1)])
        self.mm(self.pb[b2][:], [(self.ones[:], self.zsq[:, oc, :]) for oc in range(8)], ["ones"] + [("zsq", oc) for oc in range(8)], [("ps", b2)])
        mean, rstd, tA, tB = self.mean, self.rstd, self.xn[0], self.xn[1]
        self.ts("dve", mean[:], self.pb[b1][:], 1.0 / D, None, ALU.mult, None, [("ps", b1)], ["mean"])
        self.tt("dve", tA[:], mean[:], mean[:], ALU.mult, ["mean"], ["xn0"])
        self.stt("dve", tB[:], self.pb[b2][:], 1.0 / D, tA[:], ALU.mult, ALU.subtract, [("ps", b2), "xn0"], ["xn1"])
        self.rsqrt(rstd[:], tB[:], 1.0, 0, ["xn1"], ["rstd"])
        gcol = VC["ln_g"] + (l * 2 + which) * 8
        bcol = VC["ln_b"] + (l * 2 + which) * 8
        for oc in range(8):
            xs = self.x[:, oc, sl]
            xn = self.xn[oc % 2]
            xk = "xn%d" % (oc % 2)
            self.tt("dve", xn[:], xs, mean[:], ALU.subtract, [("x", oc, th), "mean"], [xk])
            self.tt("dve", xn[:], xn[:], rstd[:], ALU.mult, [xk, "rstd"], [xk])
            self.act(xs, xn[:], AF.Identity, [xk, "vecs"], [("x", oc, th)], bias=v[:, bcol + oc:bcol + oc + 1], scale=v[:, gcol + oc:gcol + oc + 1])
            if not last:
                G = self.lnf[:, l, which, 0, oc, g:g + 1]
                B = self.lnf[:, l, which, 1, oc, g:g + 1]
                self.act(self.hm[:, oc, sl], xn[:], AF.Identity, [xk, ("lnf", l, which, g, 0), ("lnf", l, which, g, 1)], [("hm", oc)], bias=B, scale=G)

    def rsqrt(self, out, in_, scale, eps_i, R, W):
        self.act(out, in_, AF.Ln, list(R) + ["epsc0", "epsc1"], W, bias=self.epsc[0:out.shape[0], eps_i:eps_i + 1], scale=scale)
        self.act(out, out, AF.Exp, W, W, scale=-0.5)

    def out_proj(self, g, l):
        self.proj_fm(self.dram["w_out"][l], 0, D, self.hm, "hm", 8, self.ln_z(l, 2, g))
        self.ln_finish(g, l, 0, False)

    def ffn(self, g, l, side=None):
        self.fence()
        self.carve_reset()
        hid = self.carve([NFF, T], BF16)
        wg, wu, wd = self.dram["w_ffn_gate"][l], self.dram["w_ffn_up"][l], self.dram["w_ffn_down"][l]
        hk = [("hm", kc) for kc in range(8)]
        for t in range(D_FF // WT):
            sg_, kg = self.wtile(wg, 0, 8, t * WT, WT)
            su_, ku = self.wtile(wu, 0, 8, t * WT, WT)
            for j in range(2):
                fc = t * 2 + j
                for th in range(2):
                    sl = slice(th * 512, (th + 1) * 512)
                    bg = self.psb("mm")
                    bu = self.psb("mm")
                    self.mm(self.pb[bg][:], [(sg_[:, kc, j * 128:(j + 1) * 128], self.hm[:, kc, sl]) for kc in range(8)], [kg] + hk, [("ps", bg)])
                    self.mm(self.pb[bu][:], [(su_[:, kc, j * 128:(j + 1) * 128], self.hm[:, kc, sl]) for kc in range(8)], [ku] + hk, [("ps", bu)])
                    i = self.rot("ffnt", 2)
                    tmp = self.rd[:, i, :]
                    self.act(tmp, self.pb[bg][:], AF.Silu, [("ps", bg)], [("rd", i)])
                    self.tt("dve", hid[:, fc, sl], tmp, self.pb[bu][:], ALU.mult, [("rd", i), ("ps", bu)], [("hid", fc)])
            if side is not None:
                for _ in range(3):
                    next(side, None)
        if side is not None:
            for _ in side:
                pass
            self.lnfold(l, (1,))
        self.S.mark("g%d l%d down" % (g, l))
        hz = self.ln_z(l, 5, g)
        pieces = ((0, 8), (8, 8), (16, 6))
        last = (l == DEPTH - 1)

        def down_quarter(q, th):
            sl = slice(th * 512, (th + 1) * 512)
            slots = [self.wtile(wd, k0 * 128, nk, q * WT, WT) for (k0, nk) in pieces]
            for j in range(2):
                oc = q * 2 + j
                b = self.psb("mm")
                ops = []
                for (slot, _), (k0, nk) in zip(slots, pieces):
                    for kc in range(nk):
                        ops.append((slot[:, kc, j * 128:(j + 1) * 128], hid[:, k0 + kc, sl]))
                self.mm(self.pb[b][:], ops, [k for _, k in slots] + [("hid", fc) for fc in range(NFF)], [("ps", b)])
                hz(oc, th, self.pb[b][:], ("ps", b))
        for q in range(4):
            down_quarter(q, 0)
        self.ln_a(0)
        down_quarter(0, 1)
        self.S.mark("g%d l%d ln2" % (g, l))
        self.ln_b(g, l, 1, last, 0)
        for q in range(1, 4):
            down_quarter(q, 1)
        self.ln_a(1)
        self.ln_b(g, l, 1, last, 1)
        self.fence()

    def attn_unit(self, qaps, qkeys, kblocks, scale, nq, fin, ng=1):
        hp = len(qaps)
        gs = hp // ng
        ob = self.psb("st")
        O = self.pb[ob]
        nkb = len(kblocks)

        def stage_a(bi):
            kts, vap, mask, kkeys = kblocks[bi]
            pi = self.ptslot()
            pt = self.ptb[:, pi, 0:hp * nq]
            for gi in range(ng):
                b = self.psb("mm")
                Sb = self.pb[b]

                def fn(e, Sb=Sb, kts=kts, gi=gi):
                    ins = None
                    for j in range(gs):
                        i = gi * gs + j
                        ins = e.matmul(Sb[:, j * nq:(j + 1) * nq], kts[i], qaps[i], start=True, stop=True)
                    return ins
                self.S.add("pe", fn, list(qkeys) + list(kkeys), [("ps", b)])
                self.act(pt[:, gi * gs * nq:(gi + 1) * gs * nq], Sb[:, 0:gs * nq], AF.Exp, [("ps", b)], [("ptb", pi, gi)], scale=scale)
            pkeys = [("ptb", pi, gi) for gi in range(ng)]
            if mask is not None:
                p3 = pt.rearrange("p (h q) -> p h q", q=nq)
                m3 = mask.unsqueeze(1).to_broadcast([128, hp, nq])
                self.tt("dve", p3, p3, m3, ALU.mult, pkeys + ["c_m_prev", "c_m_next"], pkeys)
            return pt, pkeys

        def stage_b(bi, pt, pkeys):
            kts, vap, mask, kkeys = kblocks[bi]

            def fn2(e, pt=pt, vap=vap, first=(bi == 0), lastb=(bi == nkb - 1)):
                ins = None
                for i in range(hp):
                    ins = e.matmul(O[:, i * nq:(i + 1) * nq], vap, pt[:, i * nq:(i + 1) * nq], start=(first and i == 0), stop=lastb,
                                   skip_group_check=True)
                return ins
            self.S.add("pe", fn2, pkeys + list(kkeys), [("ps", ob)])

        cur = stage_a(0)
        for bi in range(nkb):
            nxt = stage_a(bi + 1) if bi + 1 < nkb else None
            stage_b(bi, *cur)
            cur = nxt
        if getattr(fin, "batched", False):
            fin(O[:, 0:hp * nq], ("ps", ob))
        else:
            for i in range(hp):
                fin(i, O[:, i * nq:(i + 1) * nq], ("ps", ob))

    def attn_fin_batched(self, parts, dkeys, nq, hp, sink_b=None):
        def fin(O, okey):
            ri = self.rdslot()
            rd = self.rd[64:128, ri, 0:hp * nq]
            if sink_b is not None:
                self.tt("dve", rd.rearrange("p (h q) -> p h q", q=nq), O[64:128, :].rearrange("p (h q) -> p h q", q=nq), sink_b, ALU.add,
                        [okey, "esinkU"], [("rd", ri)])
                self.act(rd, rd, AF.Ln, [("rd", ri)], [("rd", ri)])
            else:
                self.act(rd, O[64:128, :], AF.Ln, [okey], [("rd", ri)])
            self.act(rd, rd, AF.Exp, [("rd", ri)], [("rd", ri)], scale=-1.0)
            for (c0, ncols, out_ap, shp) in parts:
                i0 = O[0:64, c0:c0 + ncols]
                i1 = self.rd[64:128, ri, c0:c0 + ncols]
                if shp is not None:
                    i0 = i0.rearrange("p (a q) -> p a q", q=shp)
                    i1 = i1.rearrange("p (a q) -> p a q", q=shp)
                self.tt("dve", out_ap, i0, i1, ALU.mult, [okey, ("rd", ri)], dkeys)
        fin.batched = True
        return fin

    def attn_fin(self, dst, dkey, sink_ap=None):
        def fin(i, O, okey):
            nq = O.shape[-1]
            ri = self.rdslot()
            rd = self.rd[64:128, ri, 0:nq]
            if sink_ap is not None:
                self.ts("dve", rd, O[64:128, :], sink_ap(i), None, ALU.add, None, [okey, "esink"], [("rd", ri)])
                self.act(rd, rd, AF.Ln, [("rd", ri)], [("rd", ri)])
            else:
                self.act(rd, O[64:128, :], AF.Ln, [okey], [("rd", ri)])
            self.act(rd, rd, AF.Exp, [("rd", ri)], [("rd", ri)], scale=-1.0)
            self.tt("dve", dst(i), O[0:64, :], rd, ALU.mult, [okey, ("rd", ri)], dkey(i))
        return fin

    def rope(self, xap, xkey, perm, cos, sin, n, rows=128):
        for c0 in range(0, n, 512):
            w = min(512, n - c0)
            b = self.psb("aux")
            ps = self.pb[b][0:rows, 0:w]
            xs = xap[:, c0:c0 + w]
            self.mm(ps, [(perm, xs)], list(xkey) + ["c_perm_hd", "c_perm_c"], [("ps", b)])
            t1 = self.xn[0][0:rows, 0:w]
            t2 = self.xn[1][0:rows, 0:w]
            self.tt("dve", t1, xs, cos[:, c0:c0 + w], ALU.mult, list(xkey) + ["c_cos_hd", "c_cos_c"], ["xn0"])
            self.tt("dve", t2, ps, sin[:, c0:c0 + w], ALU.mult, [("ps", b), "c_sin_hd", "c_sin_c"], ["xn1"])
            self.tt("dve", xs, t1, t2, ALU.add, ["xn0", "xn1"], list(xkey))

    def even_mixer(self, g, l):
        e = l // 2
        dr = self.dram
        w2d = dr["w_in_ab"][e]
        hk = [("hm", kc) for kc in range(8)]
        self.carve_reset()
        A = {}
        A["qa"] = qa = self.carve([4, T], BF16)
        A["kaT"] = kaT = self.carve([2, T + PAST], BF16)
        A["vaug"] = vaug = self.carve([12, 2, 128], BF16)
        A["vtok"] = vtok = self.carve([8, 512], BF16)
        A["mixb"] = mixb = self.carve([4, T], BF16)
        A["qs"] = self.carve([T], BF16)
        A["gsil"] = self.carve([T], BF16)
        A["tset"] = [(self.carve([512], F32), self.carve([512], F32), self.carve([512], F32)) for _ in range(2)]
        A["qhat"] = self.carve([2, T], BF16)
        A["khat"] = self.carve([2, T], BF16)
        A["ktok"] = self.carve([8, 2, 128], BF16)
        A["AT"] = self.carve([2, 8, 128], BF16)
        A["obuf"] = self.carve([T], F32)
        A["Ebuf2"] = [self.carve([8], F32), self.carve([8], F32)]
        A["Lb"] = self.carve([2, 16], F32)
        A["Eh"] = self.carve([2, 16], F32)
        A["Fb"] = self.carve([2, 16], F32)
        A["Sst"] = self.carve([8, 128], F32)
        A["dbf"] = self.carve([8, 128], BF16)
        self.mixb = mixb

        self.S.mark("g%d l%d hgrn+attnA" % (g, l))

        def pre_hgrn():
            AT_ = A["AT"]
            self.S.add("dve", lambda e_: e_.memset(AT_[:], 0.0), [], ["AT0", ("AT", 0), ("AT", 1)])
            for ti in range(2):
                slot, wkey = self.wtile(w2d, 0, 8, 1280 + ti * 256, 256)
                for tt in range(8):
                    b = self.psb("mm")
                    self.mm(self.pb[b][:, 0:256], [(self.hm[:, kc, tt * 128:(tt + 1) * 128], slot[:, kc, :]) for kc in range(8)], [wkey] + hk, [("ps", b)])
                    self.cp("act" if tt % 2 == 0 else "dve", vtok[:, tt, ti * 256:(ti + 1) * 256], self.pb[b][:, 0:256], [("ps", b)], [("vtok", tt)])

            yield

        def pre_attn():
            self.S.add("dve", lambda e_: e_.memset(vaug[:, :, :, 64:128], 1.0), [], ["vaug_ones"])
            if g == 1:
                for j in range(4):
                    self.dma("pool", vaug[:, 8 + j, :, 0:64], dr["cav"][e, j * 128:(j + 1) * 128, :].rearrange("p (k d) -> p k d", d=64),
                             ["arena"], [("vaug", 8 + j)], chan=("cch", "cav", j))
                for j in range(4):
                    buf = self.rot("cst", 2)
                    self.dma("sp", self.cst[:, buf, 0:256], dr["cakd"][e, j * 128:(j + 1) * 128, :], [], [("cst", buf)], chan=("lch", "cst", buf))
                    b = self.psb("aux")
                    bank, cst, ident = self.pb[b], self.cst, self.ident

                    def fn(e_, bank=bank, cst=cst, buf=buf, ident=ident):
                        e_.transpose(bank[:, 0:128], cst[:, buf, 0:128], ident[:])
                        return e_.transpose(bank[:, 128:256], cst[:, buf, 128:256], ident[:])
                    self.S.add("pe", fn, [("cst", buf), "ident"], [("ps", b)])
                    self.cp("act", kaT[:, :, T + j * 128:T + (j + 1) * 128], bank[:, 0:256].rearrange("p (k t) -> p k t", k=2), [("ps", b)], [("kaT", 0), ("kaT", 1)])

            def h_qa(oc, th, ps, pk):
                self.cp("act" if th == 0 else "dve", qa[:, oc, th * 512:(th + 1) * 512], ps, [pk], [("qa", oc)])
            self.proj_fm(w2d, 0, 512, self.hm, "hm", 8, h_qa)

            def h_ka(oc, th, ps, pk):
                self.cp("act" if th == 0 else "dve", kaT[:, oc, th * 512:(th + 1) * 512], ps, [pk], [("kaT", oc)])
            self.proj_fm(w2d, EVEN_IN, 256, self.hm, "hm", 8, h_ka)
            if g == 1:
                for c in range(4):
                    self.rope(qa[:, c, :], [("qa", c)], self.perm_hd[:], self.cos_hd, self.sin_hd, T)
                for c in range(2):
                    self.rope(kaT[:, c, 0:T], [("kaT", c)], self.perm_hd[:], self.cos_hd, self.sin_hd, T)
            slot, wkey = self.wtile(w2d, 0, 8, 512, 256)
            for tt in range(8):
                b = self.psb("mm")
                self.mm(self.pb[b][:, 0:256], [(self.hm[:, kc, tt * 128:(tt + 1) * 128], slot[:, kc, :]) for kc in range(8)], [wkey] + hk, [("ps", b)])
                sk_ = self.cfg.get("skip", "")
                if "v" not in sk_:
                    self.cp("act", vaug[:, tt, :, 0:64], self.pb[b][:, 128:256].rearrange("p (k d) -> p k d", d=64), [("ps", b)], [("vaug", tt)])
                if g == 0 and "c" not in sk_:
                    buf = self.rot("cst", 2)
                    self.cp("dve", self.cst[:, buf, 0:256], self.pb[b][:, 0:256], [("ps", b)], [("cst", buf)])
                    seq, p0 = tt // 2, (tt % 2) * 128
                    if "d" not in sk_:
                        self.dma("sp", dr["nak"][seq, e, p0:p0 + 128, :], self.cst[:, buf, 0:128], [("cst", buf)], [], is_out=True, chan=("och", "csta", buf))
                        self.dma("sp", dr["nav"][seq, e, p0:p0 + 128, :], self.cst[:, buf, 128:256], [("cst", buf)], [], is_out=True, chan=("och", "cstb", buf))

            yield

        self.mixa = mixa = self.zb[:].rearrange("p (c a) b -> p c (a b)", a=2)

        def hgrn_all():
            yield from pre_hgrn()
            for hd in range(4):
                yield from self.hgrn_head(g, e, hd, w2d, A)

        def attn_all():
            yield from pre_attn()
            scale = HD ** -0.5
            if g == 0:
                units = [(seq * 256 + qb * 128, [(seq * 2 + kb, None) for kb in range(2)]) for seq in range(4) for qb in range(2)]
            else:
                units = []
                for qt in range(8):
                    kbs = []
                    for j in (qt - 1, qt, qt + 1):
                        if 0 <= j < 8:
                            kbs.append((j, None if j == qt else (self.m_prev[:] if j == qt - 1 else self.m_next[:])))
                    kbs += [(8 + j, None) for j in range(4)]
                    units.append((qt * 128, kbs))
            for q0, kbs in units:
                for kvh in range(2):
                    heads = [kvh * 4 + i for i in (0, 2, 1, 3)]
                    qaps = [qa[(h % 2) * 64:(h % 2) * 64 + 64, h // 2, q0:q0 + 128] for h in heads]
                    qkeys = [("qa", c) for c in (kvh * 2, kvh * 2 + 1)]
                    kblocks = []
                    for (kt, mask) in kbs:
                        kts = [kaT[(h % 2) * 64:(h % 2) * 64 + 64, kvh, kt * 128:(kt + 1) * 128] for h in heads]
                        kblocks.append((kts, vaug[:, kt, kvh, :], mask, [("kaT", kvh), ("vaug", kt), "vaug_ones"]))

                    parts = [(0, 256, mixa[0:64, kvh * 2:kvh * 2 + 2, q0:q0 + 128], 128),
                             (256, 256, mixa[64:128, kvh * 2:kvh * 2 + 2, q0:q0 + 128], 128)]
                    dk = [("zb", 2 * c + q0 // 512) for c in (kvh * 2, kvh * 2 + 1)]
                    sb_ = self.esinkU[64:128, e, kvh, :].unsqueeze(2).to_broadcast([64, 4, 128])
                    self.attn_unit(qaps, qkeys, kblocks, scale, 128, self.attn_fin_batched(parts, dk, 128, 4, sb_), ng=2)
                    yield
        self.run_lanes([hgrn_all(), attn_all()], [{"mm": [0, 1], "st": [2, 3], "aux": [4]}, {"mm": [5, 6], "st": [7], "aux": [7]}],
                       slots=[[0, 1, 2, 3], [4, 5]])

    def interleave(self, gens, counts):
        done = [False] * len(gens)
        acc = [0.0] * len(gens)
        while not all(done):
            for gi, gen in enumerate(gens):
                if done[gi]:
                    continue
                acc[gi] += counts[gi] / float(counts[0]) if not done[0] else 1.0
                while acc[gi] >= 1.0 and not done[gi]:
                    acc[gi] -= 1.0
                    self.stream = gi
                    try:
                        next(gen)
                    except StopIteration:
                        done[gi] = True
                    self.stream = None

    def hgrn_head(self, g, e, hd, w2d, A):
        dr = self.dram
        qs, gsil = A["qs"], A["gsil"]
        qhat, khat, ktok, AT, obuf = A["qhat"], A["khat"], A["ktok"], A["AT"], A["obuf"]
        Lb, Eh, Fb, Sst, dbf, vtok, mixb = A["Lb"], A["Eh"], A["Fb"], A["Sst"], A["dbf"], A["vtok"], A["mixb"]
        DKS = 128 ** -0.5
        NCK = T // CH
        for (col, dst, key) in ((768 + hd * 128, qs, "qs"), (2816 + hd * 128, gsil, "gsil")):
            def hh(oc, th, ps, pk, dst=dst, key=key):
                self.act(dst[:, th * 512:(th + 1) * 512], ps, AF.Silu, [pk], [key])
            self.proj_fm(w2d, col, 128, self.hm, "hm", 8, hh)
        for d in range(2):
            lbv = self.lb[:, e, d, 0, hd:hd + 1]
            omv = self.lb[:, e, d, 1, hd:hd + 1]

            def hg(oc, th, ps, pk, d=d, lbv=lbv, omv=omv):
                sl = slice(th * 512, (th + 1) * 512)
                nc_ = 512 // CH
                ti_ = self.rot("tset", 2)
                t1, t2, t3 = A["tset"][ti_]
                k1, k2, k3, ke = ("t1", ti_), ("t2", ti_), ("t3", ti_), ("Ebuf", ti_)
                Ebuf = A["Ebuf2"][ti_]
                self.act(t1[:], ps, AF.Sigmoid, [pk], [k1])
                self.ts("dve", t1[:], t1[:], omv, lbv, ALU.mult, ALU.add, [k1] + self.lbkeys, [k1])
                self.ts("dve", t2[:], t1[:], -1.0, 1.0, ALU.mult, ALU.add, [k1], [k2])
                self.act(t1[:], t1[:], AF.Ln, [k1], [k1])
                sm = self.scanmask
                self.S.add("dve", lambda e_: e_.tensor_tensor_scan(out=t3[:], data0=sm[:], data1=t1[:], initial=0.0, op0=ALU.mult, op1=ALU.add),
                           [k1, "scanmask", "scanmask2"], [k3])
                t33 = t3[:].rearrange("p (n c) -> p n c", c=CH)
                self.cp("dve", Lb[:, d, th * nc_:(th + 1) * nc_], t33[:, :, CH - 1], [k3], [("Lb", d)])
                self.ts("dve", Ebuf[:], t33[:, :, CH - 1], 0.5, None, ALU.mult, None, [k3], [ke])
                eb = Ebuf[:].unsqueeze(2).to_broadcast([128, nc_, CH])
                if d == 0:
                    self.tt("dve", t33, t33, eb, ALU.subtract, [k3, ke], [k3])
                else:
                    self.tt("dve", t3[:], t1[:], t3[:], ALU.subtract, [k1, k3], [k3])
                    self.tt("dve", t33, t33, eb, ALU.add, [k3, ke], [k3])
                self.act(t1[:], t3[:], AF.Exp, [k3], [k1])
                self.stt("dve", qhat[:, d, sl], qs[:, sl], DKS, t1[:], ALU.mult, ALU.mult, ["qs", k1], [("qhat", d)])
                self.act(t1[:], t3[:], AF.Exp, [k3], [k1], scale=-1.0)
                self.tt("dve", khat[:, d, sl], t2[:], t1[:], ALU.mult, [k1, k2], [("khat", d)])
            self.proj_fm(w2d, 1792 + d * 512 + hd * 128, 128, self.hm, "hm", 8, hg)
            yield
        yield
        nseq = 4 if g == 0 else 1
        nst = NCK // nseq
        self.act(Eh[:], Lb[:], AF.Exp, [("Lb", 0), ("Lb", 1)], ["Eh"], scale=0.5)
        self.cp("dve", Fb[:], Eh[:], ["Eh"], ["Fb"])
        if nst > 1:
            f0 = Fb[:, 0, :].rearrange("p (s n) -> p s n", s=nseq)
            e0 = Eh[:, 0, :].rearrange("p (s n) -> p s n", s=nseq)
            self.tt("dve", f0[:, :, 0:nst - 1], f0[:, :, 0:nst - 1], e0[:, :, 1:nst], ALU.mult, ["Eh", "Fb"], ["Fb"])
            f1 = Fb[:, 1, :].rearrange("p (s n) -> p s n", s=nseq)
            e1 = Eh[:, 1, :].rearrange("p (s n) -> p s n", s=nseq)
            self.tt("dve", f1[:, :, 1:nst], f1[:, :, 1:nst], e1[:, :, 0:nst - 1], ALU.mult, ["Eh", "Fb"], ["Fb"])
        for d in range(2):
            for th in range(2):
                sl = slice(th * 512, (th + 1) * 512)
                kf = self.zsq[:, 1, :]
                fb_ = Fb[:, d, th * 8:(th + 1) * 8].unsqueeze(2).to_broadcast([128, 8, CH])
                self.tt("dve", kf.rearrange("p (n c) -> p n c", c=CH), khat[:, d, sl].rearrange("p (n c) -> p n c", c=CH), fb_, ALU.mult,
                        [("khat", d), "Fb"], [("zsq", 1)])
                b = self.psb("aux")
                bankb = self.pb[b][:].bitcast(BF16)
                identb = self.identb

                def fn(e_, bankb=bankb, kf=kf, identb=identb):
                    ins = None
                    for i in range(4):
                        ins = e_.transpose(bankb[:, i * 128:(i + 1) * 128], kf[:, i * 128:(i + 1) * 128], identb[:])
                    return ins
                self.S.add("pe", fn, [("zsq", 1), "c_ident"], [("ps", b)])
                self.cp("act", ktok[:, th * 4:(th + 1) * 4, d, :], bankb[:, 0:512].rearrange("p (i k) -> p i k", k=128), [("ps", b)], [("ktok", d)])
                b = self.psb("aux")
                bank = self.pb[b]

                def fn2(e_, bank=bank, d=d, th=th):
                    ins = None
                    for i in range(4):
                        c0 = (th * 4 + i) * 128
                        ins = e_.matmul(bank[:, i * 128:(i + 1) * 128], khat[:, d, c0:c0 + 128], qhat[:, d, c0:c0 + 128], start=True, stop=True)
                    return ins
                self.S.add("pe", fn2, [("khat", d), ("qhat", d)], [("ps", b)])
                m = self.hmask[:, d, :].unsqueeze(1).to_broadcast([128, 4, 128])
                atv = AT[:, d, th * 4:(th + 1) * 4, :]
                bv = bank[:].rearrange("p (i t) -> p i t", t=128)
                self.S.add("dve", lambda e_, atv=atv, m=m, bv=bv: e_.copy_predicated(out=atv, mask=m, data=bv), [("ps", b), "c_hmask", ("AT", d), "AT0"], [("AT", d)])
                yield
        yield
        if g == 0:
            self.S.add("dve", lambda e_: e_.memset(Sst[:], 0.0), [], [("Sst", i) for i in range(8)])
        else:
            for d in range(2):
                self.dma("sp", Sst[:, d, :], dr["stb"][e, d, hd], ["arena"], [("Sst", d)], chan=("lch", "Sst", d))
                c_first = 0 if d == 0 else nst - 1
                self.ts("dve", Sst[:, d, :], Sst[:, d, :], Eh[:, d, c_first:c_first + 1], None, ALU.mult, None, [("Sst", d), "Eh"], [("Sst", d)])
        nchain = 2 * nseq
        self.cp("act", dbf[:, 0:nchain, :], Sst[:, 0:nchain, :], [("Sst", i) for i in range(nchain)], [("dbf", i) for i in range(nchain)])
        written = set()
        for step in range(nst):
            for d in range(2):
                ob_ = self.psb("st")
                obank = self.pb[ob_]
                cn0 = None
                for sq in range(nseq):
                    ci = sq * 2 + d if g == 0 else d
                    cn = sq * nst + (step if d == 0 else nst - 1 - step)
                    if cn0 is None:
                        cn0 = cn
                    tt_, hb = cn // 2, (cn % 2) * 64
                    Vn = vtok[hb:hb + 64, tt_, hd * 128:(hd + 1) * 128]
                    ops = [(Vn, AT[hb:hb + 64, d, tt_, hb:hb + 64]), (dbf[:, ci, :], qhat[:, d, cn * CH:(cn + 1) * CH])]
                    self.mm(obank[:, sq * CH:(sq + 1) * CH], ops, [("vtok", tt_), ("AT", d), ("dbf", ci), ("qhat", d)], [("ps", ob_)])
                    ub = self.psb("mm")
                    self.mm(self.pb[ub][:, 0:128], [(ktok[hb:hb + 64, tt_, d, :], Vn)], [("ktok", d), ("vtok", tt_)], [("ps", ub)])
                    self.stt("dve", Sst[:, ci, :], Sst[:, ci, :], Fb[:, d, cn:cn + 1], self.pb[ub][:, 0:128], ALU.mult, ALU.add,
                             [("Sst", ci), "Fb", ("ps", ub)], [("Sst", ci)])
                    if step < nst - 1:
                        self.cp("act", dbf[:, ci, :], Sst[:, ci, :], [("Sst", ci)], [("dbf", ci)])
                off = (cn0 % nst) * CH
                dstv = obuf[:].rearrange("p (s t) -> p s t", s=nseq)[:, :, off:off + CH]
                srcv = obank[:, 0:nseq * CH].rearrange("p (s t) -> p s t", t=CH)
                if cn0 in written:
                    self.tt("dve", dstv, dstv, srcv, ALU.add, [("ps", ob_), "obuf"], ["obuf"])
                else:
                    self.cp("act", dstv, srcv, [("ps", ob_)], ["obuf"])
                    written.add(cn0)
            yield
        if g == 0:
            for sq in range(4):
                for d in range(2):
                    ci = sq * 2 + d
                    self.dma("sp", dr["nsb"][sq, e, d, hd], Sst[:, ci, :], [("Sst", ci), "arena"], [], is_out=True, chan=("och", "Sst", ci))
        yield
        gcol = self.vecs[:, VC["gnorm"] + e:VC["gnorm"] + e + 1]
        for th in range(2):
            sl = slice(th * 512, (th + 1) * 512)
            self.act(self.zsq[:, 0, :], obuf[:, sl], AF.Square, ["obuf"], [("zsq", 0)])
            b = self.psb("aux")
            self.mm(self.pb[b][:], [(self.ones[:], self.zsq[:, 0, :])], ["ones", ("zsq", 0)], [("ps", b)])
            self.rsqrt(self.rstd[:], self.pb[b][:], 1.0 / 128, 1, [("ps", b)], ["rstd"])
            self.stt("dve", self.mean[:], obuf[:, sl], gcol, self.rstd[:], ALU.mult, ALU.mult, ["obuf", "rstd", "vecs"], ["mean"])
            self.tt("dve", mixb[:, hd, sl], self.mean[:], gsil[:, sl], ALU.mult, ["mean", "gsil"], [("mixb", hd)])
        yield

    def odd_mixer(self, g, l):
        o = l // 2
        dr = self.dram
        w2d = dr["w_in_cd"][o]
        hk = [("hm", kc) for kc in range(8)]
        NK = T if g == 0 else T + PAST
        nkt = NK // 128
        self.carve_reset()
        cqn = self.carve([3, T], BF16)
        latT = self.carve([2, T + PAST], BF16)
        kpeT = self.carve([T + PAST], BF16)
        Qc = self.carve([4, T], BF16)
        Kc = self.carve([4, T + PAST], BF16)
        vaugC = self.carve([12, 4, 128], BF16)
        qd = self.carve([4, T], BF16)
        kdT = self.carve([4, T + PAST], BF16)
        vaugD = self.carve([12, 4, 128], BF16)
        v = self.vecs
        self.S.add("dve", lambda e_: e_.memset(vaugC[:, :, :, 64:128], 1.0), [], ["vaugC_ones"])
        self.S.add("dve", lambda e_: e_.memset(Qc[:], 0.0), [], [("Qc", h_) for h_ in range(4)])
        self.S.add("dve", lambda e_: e_.memset(Kc[:], 0.0), [], [("Kc", h_) for h_ in range(4)])
        self.S.add("dve", lambda e_: e_.memset(vaugD[:, :, :, 64:128], 1.0), [], ["vaugD_ones"])

        for th in range(2):
            sl = slice(th * 512, (th + 1) * 512)

            def h_cq(oc, th_, ps, pk, sl=sl):
                self.cp("dve", cqn[:, oc, sl], ps, [pk], [("cqn", oc)])
                self.act(self.zsq[:, oc, :], ps, AF.Square, [pk], [("zsq", oc)])
            self.proj_fm(w2d, 0, 384, self.hm, "hm", 8, h_cq, halves=(th,))
            b = self.psb("st")
            self.mm(self.pb[b][:], [(self.ones[:], self.zsq[:, oc, :]) for oc in range(3)], ["ones"] + [("zsq", oc) for oc in range(3)], [("ps", b)])
            self.rsqrt(self.rstd[:], self.pb[b][:], 1.0 / 384, 1, [("ps", b)], ["rstd"])
            for oc in range(3):
                gc = v[:, VC["cqn"] + o * 3 + oc:VC["cqn"] + o * 3 + oc + 1]
                self.stt("dve", cqn[:, oc, sl], cqn[:, oc, sl], gc, self.rstd[:], ALU.mult, ALU.mult, [("cqn", oc), "rstd", "vecs"], [("cqn", oc)])

        def normed(dstbuf, dkeyname, gcol):
            pend = []

            def post(oc, sl, zi):
                b = self.psb("aux")
                self.mm(self.pb[b][:], [(self.bd[:], self.zsq[:, zi, :])], ["c_blockdiag", ("zsq", zi)], [("ps", b)])
                ti = self.rot("rdn", 2)
                tmp = self.rd[:, ti, :]
                self.rsqrt(tmp, self.pb[b][:], 1.0 / 64, 1, [("ps", b)], [("rd", ti)])
                self.stt("dve", dstbuf[:, oc, sl], dstbuf[:, oc, sl], gcol, tmp, ALU.mult, ALU.mult, [(dkeyname, oc), ("rd", ti), "vecs"], [(dkeyname, oc)])

            def h(oc, th, ps, pk):
                sl = slice(th * 512, (th + 1) * 512)
                zi = 4 + self.rot("zq", 4)
                self.cp("dve", dstbuf[:, oc, sl], ps, [pk], [(dkeyname, oc)])
                self.act(self.zsq[:, zi, :], ps, AF.Square, [pk], [("zsq", zi)])
                while len(pend) > 1:
                    post(*pend.pop(0))
                pend.append((oc, sl, zi))

            def flush():
                while pend:
                    post(*pend.pop(0))
            h.flush = flush
            return h
        hq = normed(qd, "qd", v[:, VC["dqn"] + o:VC["dqn"] + o + 1])
        hkd = normed(kdT, "kdT", v[:, VC["dkn"] + o:VC["dkn"] + o + 1])
        self.proj_fm(w2d, 672, 512, self.hm, "hm", 8, hq)
        hq.flush()
        self.proj_fm(w2d, ODD_IN, 512, self.hm, "hm", 8, hkd)
        hkd.flush()
        if g == 1:
            for c in range(4):
                self.rope(qd[:, c, :], [("qd", c)], self.perm_hd[:], self.cos_hd, self.sin_hd, T)
                self.rope(kdT[:, c, 0:T], [("kdT", c)], self.perm_hd[:], self.cos_hd, self.sin_hd, T)

        def lat_tile(buf, col0, R):
            b2 = self.psb("aux")
            bank, cst, ident = self.pb[b2], self.cst, self.ident

            def fn(e_, bank=bank, cst=cst, buf=buf, ident=ident):
                e_.transpose(bank[:, 0:128], cst[:, buf, 0:128], ident[:])
                e_.transpose(bank[:, 128:256], cst[:, buf, 128:256], ident[:])
                return e_.transpose(bank[0:32, 256:384], cst[:, buf, 256:288], ident[:])
            self.S.add("pe", fn, R + ["ident"], [("ps", b2)])
            self.cp("act", latT[:, :, col0:col0 + 128], bank[:, 0:256].rearrange("p (k t) -> p k t", k=2), [("ps", b2)], [("latT", 0), ("latT", 1)])
            self.cp("dve", kpeT[0:32, col0:col0 + 128], bank[0:32, 256:384], [("ps", b2)], ["kpeT"])

        slotA, kA = self.wtile(w2d, 0, 8, 384, 256)
        slotB, kB = self.wtile(w2d, 0, 8, 640, 32)
        ss1 = self.dummy[:, 2:3]
        for tt in range(8):
            b = self.psb("mm")
            ps = self.pb[b]
            self.mm(ps[:, 0:256], [(self.hm[:, kc, tt * 128:(tt + 1) * 128], slotA[:, kc, :]) for kc in range(8)], [kA] + hk, [("ps", b)])
            self.mm(ps[:, 256:288], [(self.hm[:, kc, tt * 128:(tt + 1) * 128], slotB[:, kc, 0:32]) for kc in range(8)], [kB] + hk, [("ps", b)])
            self.act(self.tA[:, 0:256], ps[:, 0:256], AF.Square, [("ps", b)], ["tA"])
            self.S.add("dve", lambda e_: e_.reduce_sum(out=ss1, in_=self.tA[:, 0:256], axis=mybir.AxisListType.X), ["tA"], ["ss1"])
            self.rsqrt(ss1, ss1, 1.0 / 256, 1, ["ss1"], ["ss1"])
            buf = self.rot("cst", 2)
            self.stt("dve", self.cst[:, buf, 0:256], ps[:, 0:256], ss1, self.ckvn_b[:, o * 256:(o + 1) * 256], ALU.mult, ALU.mult,
                     [("ps", b), "ss1", "ckvn_b"], [("cst", buf)])
            self.cp("act", self.cst[:, buf, 256:288], ps[:, 256:288], [("ps", b)], [("cstpe", buf)])
            if g == 0:
                seq, p0 = tt // 2, (tt % 2) * 128
                self.dma("sp", dr["nckv"][seq, o, p0:p0 + 128, :], self.cst[:, buf, 0:256], [("cst", buf)], [], is_out=True, chan=("och", "csta", buf))
                self.dma("sp", dr["ncpe"][seq, o, p0:p0 + 128, :], self.cst[:, buf, 256:288], [("cstpe", buf)], [], is_out=True, chan=("och", "cstb", buf))
            lat_tile(buf, tt * 128, [("cst", buf), ("cstpe", buf)])
        if g == 1:
            for j in range(4):
                buf = self.rot("cst", 2)
                self.dma("sp", self.cst[:, buf, 0:256], dr["cckv"][o, j * 128:(j + 1) * 128, :], [], [("cst", buf)], chan=("lch", "cst", buf))
                self.dma("act", self.cst[:, buf, 256:288], dr["ccpe"][o, j * 128:(j + 1) * 128, :], [], [("cstpe", buf)], chan=("lch", "cstpe", buf))
                lat_tile(buf, T + j * 128, [("cst", buf), ("cstpe", buf)])
            self.rope(kpeT[0:32, 0:T], ["kpeT"], self.perm_c[0:32, 0:32], self.cos_c[0:32, :], self.sin_c[0:32, :], T, rows=32)

        slot, wkey = self.wtile(w2d, 0, 8, 1440, 256)
        for tt in range(8):
            b = self.psb("mm")
            self.mm(self.pb[b][:, 0:256], [(self.hm[:, kc, tt * 128:(tt + 1) * 128], slot[:, kc, :]) for kc in range(8)], [wkey] + hk, [("ps", b)])
            self.cp("act", vaugD[:, tt, :, 0:64], self.pb[b][:, 0:256].rearrange("p (k d) -> p k d", d=64), [("ps", b)], [("vaugD", tt)])
            if g == 0:
                buf = self.rot("cst", 2)
                self.cp("dve", self.cst[:, buf, 0:256], self.pb[b][:, 0:256], [("ps", b)], [("cst", buf)])
                seq, p0 = tt // 2, (tt % 2) * 128
                self.dma("sp", dr["ndv"][seq, o, p0:p0 + 128, :], self.cst[:, buf, 0:256], [("cst", buf)], [], is_out=True, chan=("och", "csta", buf))
        if g == 1:
            for j in range(4):
                self.dma("pool", vaugD[:, 8 + j, :, 0:64], dr["cdv"][o, j * 128:(j + 1) * 128, :].rearrange("p (k d) -> p k d", d=64),
                         ["arena"], [("vaugD", 8 + j)], chan=("cch", "cdv", j))
                stg = self.stg[j % 2]
                sk = self.stgkeys(j % 2)
                self.dma("sp", stg[:, 0:512], dr["cdkd"][o, j * 128:(j + 1) * 128, :], [], sk, chan=("lch", "stg", j % 2))
                b = self.psb("mm")
                bank, ident = self.pb[b], self.ident

                def fn(e_, bank=bank, stg=stg, ident=ident):
                    ins = None
                    for c in range(4):
                        ins = e_.transpose(bank[:, c * 128:(c + 1) * 128], stg[:, c * 128:(c + 1) * 128], ident[:])
                    return ins
                self.S.add("pe", fn, sk + ["ident"], [("ps", b)])
                self.cp("act", kdT[:, :, T + j * 128:T + (j + 1) * 128], bank[:].rearrange("p (c t) -> p c t", c=4), [("ps", b)], [("kdT", c) for c in range(4)])
        else:
            for tt in range(8):
                b = self.psb("aux")
                bankb = self.pb[b][:].bitcast(BF16)
                identb = self.identb

                def fn(e_, bankb=bankb, tt=tt, identb=identb):
                    ins = None
                    for c in range(4):
                        ins = e_.transpose(bankb[:, c * 128:(c + 1) * 128], kdT[:, c, tt * 128:(tt + 1) * 128], identb[:])
                    return ins
                self.S.add("pe", fn, [("kdT", c) for c in range(4)] + ["c_ident"], [("ps", b)])
                buf = self.rot("cst", 2)
                self.cp("dve", self.cst[:, buf, 0:256].rearrange("p (c d) -> p c d", d=64), bankb[:, 0:512].rearrange("p (c x) -> p c x", x=128)[:, :, 0:64],
                        [("ps", b)], [("cst", buf)])
                seq, p0 = tt // 2, (tt % 2) * 128
                self.dma("sp", dr["ndk"][seq, o, p0:p0 + 128, :], self.cst[:, buf, 0:256], [("cst", buf)], [], is_out=True, chan=("och", "csta", buf))

        self.S.mark("g%d l%d mixC+D" % (g, l))

        def c_stream():
            wq, wkv = dr["c_w_q_up"][o], dr["c_w_kv_up"][o]
            scale_c = 96 ** -0.5
            for bi in range(2):
                for pair in range(2):
                    slot, wkey = self.wtile(wq, 0, 3, (bi * 2 + pair) * 192, 192)
                    for hh in range(2):
                        hl = pair * 2 + hh
                        for th in range(2):
                            sl = slice(th * 512, (th + 1) * 512)
                            b = self.psb("mm")
                            self.mm(self.pb[b][0:96, :], [(slot[:, kc, hh * 96:(hh + 1) * 96], cqn[:, kc, sl]) for kc in range(3)],
                                    [wkey] + [("cqn", kc) for kc in range(3)], [("ps", b)])
                            self.cp("act" if th == 0 else "dve", Qc[0:96, hl, sl], self.pb[b][0:96, :], [("ps", b)], [("Qc", hl)])
                        if g == 1:
                            for th in range(2):
                                sl = slice(th * 512, (th + 1) * 512)
                                b = self.psb("aux")
                                ps = self.pb[b][0:96, :]
                                self.mm(ps, [(self.perm_c[0:96, 0:96], Qc[0:96, hl, sl])], [("Qc", hl), "c_perm_c"], [("ps", b)])
                                t1 = self.xn[0][0:32, :]
                                t2 = self.xn[1][0:32, :]
                                self.tt("dve", t1, Qc[64:96, hl, sl], self.cos_c[64:96, sl], ALU.mult, [("Qc", hl), "c_cos_c"], ["xn0"])
                                self.tt("dve", t2, self.pb[b][64:96, :], self.sin_c[64:96, sl], ALU.mult, [("ps", b), "c_sin_c"], ["xn1"])
                                self.tt("dve", Qc[64:96, hl, sl], t1, t2, ALU.add, ["xn0", "xn1"], [("Qc", hl)])
                yield
                for pair in range(2):
                    slot, wkey = self.wtile(wkv, 0, 2, (bi * 2 + pair) * 256, 256)
                    for hh in range(2):
                        hl = pair * 2 + hh
                        for c0 in range(0, NK, 512):
                            b = self.psb("mm")
                            self.mm(self.pb[b][0:64, :], [(slot[:, kc, hh * 128:hh * 128 + 64], latT[:, kc, c0:c0 + 512]) for kc in range(2)],
                                    [wkey, ("latT", 0), ("latT", 1)], [("ps", b)])
                            self.cp("act" if (c0 // 512) % 2 == 0 else "dve", Kc[0:64, hl, c0:c0 + 512], self.pb[b][0:64, :], [("ps", b)], [("Kc", hl)])
                    for kt in range(nkt):
                        b = self.psb("mm")
                        self.mm(self.pb[b][:, 0:128], [(latT[:, kc, kt * 128:(kt + 1) * 128], slot[:, kc, :].rearrange("p (h x) -> p h x", x=128)[:, :, 64:128]) for kc in range(2)],
                                [wkey, ("latT", 0), ("latT", 1)], [("ps", b)])
                        self.cp("act" if kt % 2 == 0 else "dve", vaugC[:, kt, pair * 2:pair * 2 + 2, 0:64], self.pb[b][:, 0:128].rearrange("p (h d) -> p h d", d=64),
                                [("ps", b)], [("vaugC", kt)])
                yield
                for hl in range(4):
                    self.cp("act" if hl % 2 == 0 else "dve", Kc[64:96, hl, 0:NK], kpeT[0:32, 0:NK], ["kpeT"], [("Kc", hl)])
                if g == 0:
                    units = [(seq * 256, 256, [seq * 2, seq * 2 + 1]) for seq in range(4)]
                else:
                    units = [(qt * 512, 512, list(range(12))) for qt in range(2)]
                for hl in range(4):
                    h = bi * 4 + hl
                    for (q0, nq, kts_) in units:
                        kblocks = [([Kc[:, hl, kt * 128:(kt + 1) * 128]], vaugC[:, kt, hl, :], None, [("Kc", hl), ("vaugC", kt), "vaugC_ones"]) for kt in kts_]

                        def dst(i, h=h, q0=q0, nq=nq):
                            return self.hm[(h % 2) * 64:(h % 2) * 64 + 64, h // 2, q0:q0 + nq]

                        def dkey(i, h=h):
                            return [("hm", h // 2)]
                        self.attn_unit([Qc[:, hl, q0:q0 + nq]], [("Qc", hl)], kblocks, scale_c, nq, self.attn_fin(dst, dkey))
                        yield


        def d_stream():
            scale_d = HD ** -0.5
            if g == 0:
                units = [(seq * 256, [seq * 2, seq * 2 + 1]) for seq in range(4)]
            else:
                units = [(qt * 256, list(range(12))) for qt in range(4)]
            for c in range(4):
                for (q0, kts_) in units:
                    qaps = [qd[i * 64:(i + 1) * 64, c, q0:q0 + 256] for i in range(2)]
                    kblocks = [([kdT[i * 64:(i + 1) * 64, c, kt * 128:(kt + 1) * 128] for i in range(2)], vaugD[:, kt, c, :], None,
                                [("kdT", c), ("vaugD", kt), "vaugD_ones"]) for kt in kts_]

                    parts = [(0, 256, self.hm[0:64, 4 + c, q0:q0 + 256], None), (256, 256, self.hm[64:128, 4 + c, q0:q0 + 256], None)]
                    self.attn_unit(qaps, [("qd", c)], kblocks, scale_d, 256, self.attn_fin_batched(parts, [("hm", 4 + c)], 256, 2), ng=2)
                    yield

        self.run_lanes([c_stream(), d_stream()], [{"mm": [0, 1, 7], "st": [2], "aux": [3]}, {"mm": [4, 5], "st": [6], "aux": [6]}])

    def build(self):
        try:
            self.build_()
        except StopBuild:
            pass
        self.S.emit(self.st)

    def build_(self):
        cfg = self.cfg
        groups = cfg.get("groups", (0, 1))
        NL = cfg.get("nl", DEPTH)
        self.setup()
        self.stage(0.25)
        self.mods(0)
        self.dbg("dbg_modv", self.modv[:], [("modv", 0, 0), ("modv", 0, 1)])
        self.stage(0.5)
        first = True
        for g in groups:
            self.load_x(g)
            self.dbg("dbg_x0", self.x[:], [("x", c, h_) for c in range(8) for h_ in range(2)])
            self.stage(0.75)
            self.modulate0(g, 0)
            self.dbg("dbg_h0", self.hm[:], [("hm", c) for c in range(8)])
            self.stage(1)
            for l in range(NL):
                side = None
                if first:
                    self.lnfold(l, (0,))
                    if l + 1 < DEPTH:
                        side = self.mods_gen(l + 1)
                self.stage(1.2)
                self.S.mark("g%d l%d mixer" % (g, l))
                if l % 2 == 0:
                    self.even_mixer(g, l)
                    mixb = self.mixb
                    mixa = self.mixa
                    srcf = lambda kc, mixb=mixb, mixa=mixa: ((mixa[:, kc, :], [("zb", 2 * kc), ("zb", 2 * kc + 1)]) if kc < 4 else (mixb[:, kc - 4, :], [("mixb", kc - 4)]))
                    self.proj_fm(self.dram["w_out"][l], 0, D, None, None, 8, self.ln_z(l, 2, g), srcf=srcf)
                else:
                    self.odd_mixer(g, l)
                    self.proj_fm(self.dram["w_out"][l], 0, D, self.hm, "hm", 8, self.ln_z(l, 2, g))
                self.S.mark("g%d l%d ln1" % (g, l))
                if l % 2 == 0:
                    self.dbg("dbg_mix_bf", self.mixa, [("zb", c) for c in range(8)])
                else:
                    self.dbg("dbg_mix_bf", self.hm[:], [("hm", c) for c in range(8)])
                if l % 2 == 0:
                    self.dbg("dbg_mixb_bf", self.mixb[:], [("mixb", c) for c in range(4)])
                self.dbg("dbg_z", self.x[:], [("x", c, h_) for c in range(8) for h_ in range(2)])
                self.stage(4)
                self.ln_finish(g, l, 0, False)
                self.dbg("dbg_xmid", self.x[:], [("x", c, h_) for c in range(8) for h_ in range(2)])
                self.dbg("dbg_hmid_bf", self.hm[:], [("hm", c) for c in range(8)])
                self.stage(5)
                self.S.mark("g%d l%d ffn" % (g, l))
                self.ffn(g, l, side)
                self.tap("tap_x_%d_%d" % (g, l))
            self.S.mark("g%d store" % g)
            self.store_x(g)
            first = False


def _build_program(cfg):
    nc = bass.Bass("TRN2", target_bir_lowering=False)
    st = ExitStack()
    kb = KB(nc, st, cfg)
    kb.build()
    st.close()
    return nc, kb


def _core_inputs(inp, b, shared):
    m = dict(shared)
    f = lambda a: np.ascontiguousarray(np.asarray(a, np.float32))
    m["xp"] = f(inp["x_prompt"][4 * b:4 * b + 4].reshape(T, D))
    m["xs"] = f(inp["x_sample"][b])
    cak = np.asarray(inp["cache_a_k"][b]).reshape(N_EVEN, PAST, 2, 64)
    m["cakd"] = f(np.concatenate([cak[:, :, 0:1], cak[:, :, 0:1], cak[:, :, 1:2], cak[:, :, 1:2]], 2).reshape(N_EVEN, PAST, 256))
    m["cav"] = f(np.asarray(inp["cache_a_v"][b]).reshape(N_EVEN, PAST, 128))
    m["stb"] = f(inp["state_b"][b])
    m["cckv"] = f(inp["cache_c_kv"][b])
    m["ccpe"] = f(inp["cache_c_pe"][b])
    cdk = np.asarray(inp["cache_d_k"][b]).reshape(N_ODD, PAST, 4, 64)
    m["cdkd"] = f(np.repeat(cdk, 2, axis=2).reshape(N_ODD, PAST, 512))
    m["cdv"] = f(np.asarray(inp["cache_d_v"][b]).reshape(N_ODD, PAST, 256))
    m["vecs"] = _pack_vecs(inp, b)
    return m


def _shared_inputs(inp):
    f = lambda a: np.ascontiguousarray(np.asarray(a, np.float32))
    s = {}
    wab = np.asarray(inp["w_in_ab"], np.float32)
    s["w_in_ab"] = f(np.concatenate([wab, wab[:, :, 512:576], wab[:, :, 512:576], wab[:, :, 576:640], wab[:, :, 576:640]], 2))
    wcd = np.asarray(inp["w_in_cd"], np.float32)
    kd = wcd[:, :, 1184:1440].reshape(N_ODD, D, 4, 64)
    s["w_in_cd"] = f(np.concatenate([wcd, np.repeat(kd, 2, axis=2).reshape(N_ODD, D, 512)], 2))
    for k in ("w_ada", "c_w_q_up", "c_w_kv_up", "w_out", "w_ffn_gate", "w_ffn_up", "w_ffn_down"):
        s[k] = f(inp[k])
    s["ckvn_b"] = f(np.broadcast_to(np.asarray(inp["c_kv_norm"], np.float32).reshape(1, N_ODD * 256), (128, N_ODD * 256)))
    for k, v in _host_consts().items():
        s[k] = np.ascontiguousarray(v) if v.dtype == np.uint8 else f(v)
    return s


def _run(inp, cfg, cores):
    nc, kb = _build_program(cfg)
    shared = _shared_inputs(inp)
    in_maps = [_core_inputs(inp, b, shared) for b in cores]
    res = run_bass_kernel_spmd(nc, in_maps, core_ids=list(range(len(cores))))
    return res.results


def kernel(**inputs):
    inp = {k: np.asarray(v) for k, v in inputs.items()}
    results = _run(inp, {}, list(range(8)))
    B, SEQ = 32, 256
    yp = np.zeros((B, SEQ, D), np.float32)
    ys = np.zeros((8, T, D), np.float32)
    nak = np.zeros((B, N_EVEN, SEQ, 2, 64), np.float32)
    nav = np.zeros((B, N_EVEN, SEQ, 2, 64), np.float32)
    nsb = np.zeros((B, N_EVEN, 2, 4, 128, 128), np.float32)
    nckv = np.zeros((B, N_ODD, SEQ, 256), np.float32)
    ncpe = np.zeros((B, N_ODD, SEQ, 32), np.float32)
    ndk = np.zeros((B, N_ODD, SEQ, 4, 64), np.float32)
    ndv = np.zeros((B, N_ODD, SEQ, 4, 64), np.float32)
    for i, r in enumerate(results):
        sl = slice(4 * i, 4 * i + 4)
        yp[sl] = r["yp"].reshape(4, SEQ, D)
        ys[i] = r["ys"]
        nak[sl] = r["nak"].reshape(4, N_EVEN, SEQ, 2, 64)
        nav[sl] = r["nav"].reshape(4, N_EVEN, SEQ, 2, 64)
        nsb[sl] = r["nsb"]
        nckv[sl] = r["nckv"]
        ncpe[sl] = r["ncpe"]
        ndk[sl] = r["ndk"].reshape(4, N_ODD, SEQ, 4, 64)
        ndv[sl] = r["ndv"].reshape(4, N_ODD, SEQ, 4, 64)
    return (yp, ys, nak, nav, nsb, nckv, ncpe, ndk, ndv)
```

```python
from contextlib import ExitStack
import math
import numpy as np
import ml_dtypes
import concourse.bass as bass
import concourse.mybir as mybir
from concourse.bass_utils import run_bass_kernel_spmd

F32 = mybir.dt.float32
BF16 = mybir.dt.bfloat16
AF = mybir.ActivationFunctionType
ALU = mybir.AluOpType

D = 1024
NCH = 8
T = 1024
DEPTH = 4
N_EVEN = 2
N_ODD = 2
PAST = 512
HD = 64
GRID_W = 64
D_FF = 2816
NFF = 22
EVEN_IN = 3328
ODD_IN = 1696
ALPHA = (2 * DEPTH) ** 0.25
LN_EPS = 1e-5 / (ALPHA * ALPHA)
RMS_EPS = 1e-6
CH = 64
NSLOT = 6
STRICT_SAME_ENGINE = False
WT = 256

ENGS = ("pe", "act", "dve", "pool", "sp")


class Op:
    __slots__ = ("eng", "fn", "deps", "signal", "target", "sem", "is_dma", "chan", "lane", "dbg")

    def __init__(self, eng, fn):
        self.eng = eng
        self.fn = fn
        self.deps = {}
        self.signal = False
        self.target = None
        self.sem = None
        self.is_dma = False
        self.chan = None
        self.lane = None
        self.dbg = None


class Sched:
    def __init__(self, nc):
        self.nc = nc
        self.q = {e: [] for e in ENGS}
        self.last_w = {}
        self.readers = {}
        self.chans = {}
        self.out_chans = set()
        self.arena_key = True
        self.lane = None
        self.lq = None
        self.session = 0

    def _push(self, eng, op, reads=(), writes=()):
        if self.lane is None:
            self.q[eng].append(op)
        else:
            op.lane = (self.session, self.lane)
            op.dbg = (tuple(reads), tuple(writes))
            self.lq[self.lane][eng].append(op)
            self.lseq[self.lane].append((eng, op))

    def lanes_begin(self, n):
        self.session += 1
        self.lq = [{e: [] for e in ENGS} for _ in range(n)]
        self.lseq = [[] for _ in range(n)]

    def lanes_merge(self):
        for lane in self.lq:
            for e in ENGS:
                for op in lane[e]:
                    for d in op.deps:
                        if d.lane is not None and d.lane[0] == op.lane[0] and d.lane[1] != op.lane[1]:
                            raise RuntimeError("cross-lane dependency: %r -> %r" % (op.dbg, d.dbg))
        seqs = self.lseq
        n = [len(q) for q in seqs]
        idx = [0] * len(seqs)
        while any(idx[i] < n[i] for i in range(len(seqs))):
            best = None
            for i in range(len(seqs)):
                if idx[i] < n[i]:
                    frac = idx[i] / float(n[i])
                    if best is None or frac < best[0]:
                        best = (frac, i)
            i = best[1]
            e, op = seqs[i][idx[i]]
            self.q[e].append(op)
            idx[i] += 1
        self.lseq = None
        self.lq = None
        self.lane = None

    def _track(self, op, reads, writes):
        for r in reads:
            w = self.last_w.get(r)
            if w is not None and w is not op:
                op.deps[w] = True
            if isinstance(r, tuple) and r[0] == "ps":
                for rd in self.readers.get(r, ()):
                    if rd is not op and rd.eng != op.eng:
                        op.deps.setdefault(rd, False)
            self.readers.setdefault(r, []).append(op)
        for r in writes:
            w = self.last_w.get(r)
            if w is not None and w is not op:
                op.deps.setdefault(w, "W")
            for rd in self.readers.get(r, ()):
                if rd is not op:
                    op.deps.setdefault(rd, False)
            self.last_w[r] = op
            self.readers[r] = []

    def add(self, eng, fn, reads=(), writes=(), arena=True):
        op = Op(eng, fn)
        reads = list(reads)
        if arena:
            reads.append("arena")
        self._track(op, reads, writes)
        self._push(eng, op, reads, writes)
        return op

    def mark(self, label):
        op = Op("pe", None)
        op.chan = ("mark", label)
        self.q["pe"].append(op)

    def fence(self, eng, fn):
        op = Op(eng, fn)
        self._track(op, [], ["arena"])
        self.q[eng].append(op)
        return op

    def dma(self, eng, fn, reads=(), writes=(), chan=None, is_out=False):
        op = Op(eng, fn)
        op.is_dma = True
        self._track(op, reads, writes)
        if chan is None:
            chan = ("chan",) + tuple(writes if writes else reads)
        op.chan = chan
        st = self.chans.setdefault(chan, [None, 0])
        st[1] += 16
        op.target = st[1]
        if is_out:
            self.out_chans.add(chan)
        self._push(eng, op, reads, writes)
        return op

    def emit(self, stack):
        nc = self.nc
        esem = {e: stack.enter_context(nc.semaphore("s_" + e)) for e in ENGS if e != "sp"}
        for i, (k, st) in enumerate(self.chans.items()):
            st[0] = stack.enter_context(nc.semaphore("c%d" % i))
        for e in ENGS:
            for op in self.q[e]:
                keep = {}
                for d, raw in op.deps.items():
                    if d.is_dma:
                        keep[d] = raw
                    elif d.eng == op.eng and not op.is_dma:
                        if op.eng == "pe":
                            continue
                        if raw or STRICT_SAME_ENGINE:
                            keep[d] = raw
                    else:
                        keep[d] = raw
                op.deps = keep
                for d in keep:
                    if not d.is_dma:
                        d.signal = True
        for e in ENGS:
            cnt = 0
            for op in self.q[e]:
                if op.is_dma:
                    op.sem = self.chans[op.chan][0]
                elif op.signal:
                    cnt += 1
                    op.target = cnt
                    op.sem = esem[e]
        block = stack.enter_context(nc.Block())
        finals = [(self.chans[c][0], self.chans[c][1]) for c in self.out_chans]

        self.marks = []

        class _Cnt:
            def __init__(self, eh):
                self.eh = eh
                self.n = 0

            def matmul(self, *a, **k):
                self.n += 1
                return self.eh.matmul(*a, **k)

            def transpose(self, *a, **k):
                self.n += 1
                return self.eh.transpose(*a, **k)

            def wait_ge(self, *a, **k):
                return self.eh.wait_ge(*a, **k)

        def run(e, eh):
            waited = {}
            if e == "pe":
                eh = _Cnt(eh)
            for op in self.q[e]:
                if op.fn is None:
                    self.marks.append((op.chan[1], eh.n))
                    continue
                need = {}
                for d in op.deps:
                    key = id(d.sem)
                    if need.get(key, (None, 0))[1] < d.target:
                        need[key] = (d.sem, d.target)
                for key, (sem, tgt) in need.items():
                    if waited.get(key, 0) >= tgt:
                        continue
                    eh.wait_ge(sem, tgt)
                    waited[key] = tgt
                ins = op.fn(eh)
                if op.is_dma:
                    ins.then_inc(op.sem, 16)
                elif op.signal:
                    ins.then_inc(op.sem, 1)
            if e == "sp":
                for sem, tgt in finals:
                    eh.wait_ge(sem, tgt)

        @block.tensor
        def _(eh):
            run("pe", eh)

        @block.scalar
        def _(eh):
            run("act", eh)

        @block.vector
        def _(eh):
            run("dve", eh)

        @block.gpsimd
        def _(eh):
            run("pool", eh)

        @block.sync
        def _(eh):
            run("sp", eh)


def _rope_tables(rot_dim):
    n_rows = T // GRID_W
    t = np.arange(T)
    row = (t // GRID_W).astype(np.float32)
    col = (t % GRID_W).astype(np.float32)
    quarter = rot_dim // 4
    inv = (10000.0 ** (-np.arange(quarter, dtype=np.float32) / quarter)).astype(np.float32)
    ar = row[None, :] * inv[:, None]
    ac = col[None, :] * inv[:, None]
    cos = np.zeros((rot_dim, T), np.float32)
    sin = np.zeros((rot_dim, T), np.float32)
    partner = np.zeros(rot_dim, np.int64)
    half = rot_dim // 2
    for base, ang in ((0, ar), (half, ac)):
        for i in range(quarter):
            d1 = base + i
            d2 = base + quarter + i
            cos[d1] = np.cos(ang[i])
            cos[d2] = np.cos(ang[i])
            sin[d1] = -np.sin(ang[i])
            sin[d2] = np.sin(ang[i])
            partner[d1] = d2
            partner[d2] = d1
    return cos, sin, partner


def _host_consts():
    c = {}
    c["ident"] = np.eye(128, dtype=np.float32)
    cos, sin, partner = _rope_tables(64)
    c["cos_hd"] = np.concatenate([cos, cos], 0)
    c["sin_hd"] = np.concatenate([sin, sin], 0)
    pm = np.zeros((128, 128), np.float32)
    for hh in range(2):
        for m in range(64):
            pm[hh * 64 + partner[m], hh * 64 + m] = 1.0
    c["perm_hd"] = pm
    cos, sin, partner = _rope_tables(32)
    cc = np.zeros((128, T), np.float32)
    sc = np.zeros((128, T), np.float32)
    pc = np.zeros((128, 128), np.float32)
    for b in (0, 64):
        cc[b:b + 32] = cos
        sc[b:b + 32] = sin
        for m in range(32):
            pc[b + partner[m], b + m] = 1.0
    c["cos_c"] = cc
    c["sin_c"] = sc
    c["perm_c"] = pc
    k = np.arange(128)[:, None]
    q = np.arange(128)[None, :]
    c["m_prev"] = (k >= q).astype(np.float32)
    c["m_next"] = (k <= q).astype(np.float32)
    s = np.arange(CH)[:, None]
    tt = np.arange(CH)[None, :]
    hm = np.zeros((128, 2 * 128), np.float32)
    for b in (0, 64):
        hm[b:b + CH, b:b + CH] = (s <= tt)
        hm[b:b + CH, 128 + b:128 + b + CH] = (s >= tt)
    c["hmask"] = hm.astype(np.uint8)
    bd = np.zeros((128, 128), np.float32)
    bd[0:64, 0:64] = 1.0
    bd[64:128, 64:128] = 1.0
    c["blockdiag"] = bd
    return c


def _fm(v):
    v = np.asarray(v, np.float32)
    return np.ascontiguousarray(v.reshape(-1, 128).T)


VC = {}
_off = 0
for _name, _n in (("ln_g", 64), ("ln_b", 64), ("b_lb", 16), ("gnorm", 2), ("cqn", 6), ("dqn", 2), ("dkn", 2),
                  ("sink", 16), ("b_ada", 192), ("cond", 16)):
    VC[_name] = _off
    _off += _n
NVEC = _off


def _pack_vecs(inp, b):
    v = np.zeros((128, NVEC), np.float32)
    for l in range(DEPTH):
        for j in range(2):
            o = (l * 2 + j) * 8
            v[:, VC["ln_g"] + o:VC["ln_g"] + o + 8] = _fm(inp["ln_g"][l, j])
            v[:, VC["ln_b"] + o:VC["ln_b"] + o + 8] = _fm(inp["ln_b"][l, j])
        v[:, VC["b_ada"] + l * 48:VC["b_ada"] + (l + 1) * 48] = _fm(inp["b_ada"][l])
    for e in range(N_EVEN):
        for d in range(2):
            o = (e * 2 + d) * 4
            v[:, VC["b_lb"] + o:VC["b_lb"] + o + 4] = _fm(inp["b_lb"][e, d])
        v[:, VC["gnorm"] + e] = inp["b_gnorm"][e]
        v[:, VC["sink"] + e * 8:VC["sink"] + e * 8 + 8] = np.broadcast_to(inp["a_sink"][e][None, :], (128, 8))
    for o in range(N_ODD):
        v[:, VC["cqn"] + o * 3:VC["cqn"] + o * 3 + 3] = _fm(inp["c_q_norm"][o])
        v[:, VC["dqn"] + o] = np.tile(inp["d_q_norm"][o], 2)
        v[:, VC["dkn"] + o] = np.tile(inp["d_k_norm"][o], 2)
    cond2 = np.stack([inp["c_ctx"], inp["c"][b]], 0)
    for c2 in range(2):
        v[:, VC["cond"] + c2 * 8:VC["cond"] + c2 * 8 + 8] = _fm(cond2[c2])
    return v


class StopBuild(Exception):
    pass


class KB:
    def stage(self, n):
        if self.cfg.get("stage", 99) <= n:
            raise StopBuild()

    def dbg(self, name, ap, keys):
        if name in self.cfg.get("taps", {}):
            self.dma("sp", self.dram[name], ap, list(keys) + ["arena"], [], is_out=True, chan=("och", name))

    def __init__(self, nc, st, cfg):
        self.nc = nc
        self.st = st
        self.cfg = cfg
        self.S = Sched(nc)
        self.rr = {}
        self.wcount = 0
        self.dram = {}
        self._decl_dram()
        self._alloc()

    def din(self, name, shape, dt=F32):
        self.dram[name] = self.nc.dram_tensor(name, list(shape), dt, kind="ExternalInput").ap()
        return self.dram[name]

    def dout(self, name, shape):
        self.dram[name] = self.nc.dram_tensor(name, list(shape), F32, kind="ExternalOutput").ap()
        return self.dram[name]

    def _decl_dram(self):
        d = self.din
        d("xp", [T, D]); d("xs", [T, D])
        d("cakd", [N_EVEN, PAST, 256]); d("cav", [N_EVEN, PAST, 128])
        d("stb", [N_EVEN, 2, 4, 128, 128])
        d("cckv", [N_ODD, PAST, 256]); d("ccpe", [N_ODD, PAST, 32])
        d("cdkd", [N_ODD, PAST, 512]); d("cdv", [N_ODD, PAST, 256])
        d("w_ada", [DEPTH, D, 6 * D])
        d("w_in_ab", [N_EVEN, D, EVEN_IN + 256]); d("w_in_cd", [N_ODD, D, ODD_IN + 512])
        d("c_w_q_up", [N_ODD, 384, 768]); d("c_w_kv_up", [N_ODD, 256, 1024])
        d("w_out", [DEPTH, D, D])
        d("w_ffn_gate", [DEPTH, D, D_FF]); d("w_ffn_up", [DEPTH, D, D_FF]); d("w_ffn_down", [DEPTH, D_FF, D])
        d("vecs", [128, NVEC]); d("ckvn_b", [128, N_ODD * 256])
        for k, v in _host_consts().items():
            d(k, v.shape, mybir.dt.uint8 if v.dtype == np.uint8 else F32)
        o = self.dout
        o("yp", [T, D]); o("ys", [T, D])
        o("nak", [4, N_EVEN, 256, 128]); o("nav", [4, N_EVEN, 256, 128])
        o("nsb", [4, N_EVEN, 2, 4, 128, 128])
        o("nckv", [4, N_ODD, 256, 256]); o("ncpe", [4, N_ODD, 256, 32])
        o("ndk", [4, N_ODD, 256, 256]); o("ndv", [4, N_ODD, 256, 256])
        for name, shape in self.cfg.get("taps", {}).items():
            if name.endswith("_bf"):
                self.dram[name] = self.nc.dram_tensor(name, list(shape), BF16, kind="ExternalOutput").ap()
            else:
                o(name, shape)

    def sb(self, name, shape, dt):
        return self.st.enter_context(self.nc.sbuf_tensor("s_" + name, list(shape), dt))

    def _alloc(self):
        sb = self.sb
        self.x = sb("xres", [128, NCH, T], F32)
        self.hm = sb("hm", [128, NCH, T], BF16)
        self.wsl = [sb("wsl%d" % i, [128, 8, WT], BF16) for i in range(NSLOT)]
        self.ident = sb("ident", [128, 128], F32)
        self.identb = sb("identb", [128, 128], BF16)
        self.ones = sb("ones", [128, 128], BF16)
        self.bd = sb("bd", [128, 128], BF16)
        self.perm_hd = sb("perm_hd", [128, 128], BF16)
        self.perm_c = sb("perm_c", [128, 128], BF16)
        self.m_prev = sb("m_prev", [128, 128], BF16)
        self.m_next = sb("m_next", [128, 128], BF16)
        self.hmask = sb("hmask", [128, 2, 128], mybir.dt.uint8)
        self.epsc = sb("epsc", [128, 3], F32)
        self.cos_hd = sb("cos_hd", [128, T], BF16)
        self.sin_hd = sb("sin_hd", [128, T], BF16)
        self.cos_c = sb("cos_c", [128, T], BF16)
        self.sin_c = sb("sin_c", [128, T], BF16)
        self.scanmask = sb("scanmask", [128, 512], F32)
        self.vecs = sb("vecs", [128, NVEC], F32)
        self.ckvn_b = sb("ckvn_b", [128, N_ODD * 256], F32)
        self.modv = sb("modv", [128, DEPTH, 48, 2], F32)
        self.lnf = sb("lnf", [128, DEPTH, 2, 2, 8, 2], F32)
        self.lb = sb("lbv", [128, N_EVEN, 2, 2, 4], F32)
        self.esink = sb("esink", [128, 16], F32)
        self.esinkU = sb("esinkU", [128, 2, 2, 4], F32)
        self.scond = sb("scond", [128, 8, 2], BF16)
        self.zb = sb("zb", [128, 8, 512], BF16)
        self.zsq = sb("zsq", [128, 8, 512], BF16)
        self.stg = [self.zb[:, 0:4, :].bitcast(F32).rearrange("p a b -> p (a b)"),
                    self.zb[:, 4:8, :].bitcast(F32).rearrange("p a b -> p (a b)")]
        self.mean = sb("mean", [128, 512], F32)
        self.rstd = sb("rstd", [128, 512], F32)
        self.tA = sb("tA", [128, 512], F32)
        self.xn = [sb("xn%d" % i, [128, 512], F32) for i in range(2)]
        self.ptb = sb("ptb", [128, 4, 512], BF16)
        self.rd = sb("rd", [128, 2, 512], F32)
        self.cst = sb("cst", [128, 2, 288], F32)
        self.dummy = sb("dummy", [128, 8], F32)
        self.ARENA = 20224
        self.arena = sb("arena", [128, self.ARENA], F32)
        self.pb = [self.st.enter_context(self.nc.psum_tensor("pb%d" % i, [128, 512], F32)) for i in range(8)]

    def carve_reset(self):
        self.aoff = 0

    def carve(self, shape, dt):
        n = int(np.prod(shape))
        words = n if dt == F32 else (n + 1) // 2
        a = self.arena[:, self.aoff:self.aoff + words]
        self.aoff += words
        assert self.aoff <= self.ARENA, ("arena overflow", self.aoff)
        if dt != F32:
            a = a.bitcast(dt)
        if len(shape) == 2:
            a = a.rearrange("p (a b) -> p a b", b=shape[1])
        elif len(shape) == 3:
            a = a.rearrange("p (a b c) -> p a b c", b=shape[1], c=shape[2])
        return a

    def fence(self):
        dm = self.dummy
        self.S.fence("dve", lambda e: e.memset(dm[:, 0:1], 0.0))

    def rot(self, pool, n):
        i = self.rr.get(pool, 0)
        self.rr[pool] = i + 1
        return i % n

    def psb(self, pool):
        lb = getattr(self, "lanebanks", None)
        ln = self.S.lane
        if lb is not None and ln is not None:
            banks = lb[ln][pool]
            return banks[self.rot("%s_l%d" % (pool, ln), len(banks))]
        if pool == "mm":
            return self.rot("mm", 4)
        if pool == "st":
            return 4 + self.rot("st", 2)
        return 6 + self.rot("aux", 2)

    def ptslot(self):
        ln = self.S.lane
        if ln is None:
            return self.rot("ptb", 4)
        return 2 * ln + self.rot("ptb_l%d" % ln, 2)

    def rdslot(self):
        ln = self.S.lane
        if ln is None:
            return self.rot("rdb", 2)
        return ln

    def run_lanes(self, gens, banks, slots=None):
        self.lanebanks = banks
        self.laneslots = slots
        self.S.lanes_begin(len(gens))
        for i, gen in enumerate(gens):
            self.S.lane = i
            for _ in gen:
                pass
        self.S.lane = None
        self.S.lanes_merge()
        self.lanebanks = None
        self.laneslots = None

    def mm(self, out, ops, R, W):
        n = len(ops)

        def fn(e):
            ins = None
            for i, (l, r) in enumerate(ops):
                ins = e.matmul(out, l, r, start=(i == 0), stop=(i == n - 1))
            return ins
        return self.S.add("pe", fn, R, W)

    def tr(self, out, in_, ident, R, W):
        return self.S.add("pe", lambda e: e.transpose(out, in_, ident), R, W)

    def act(self, out, in_, func, R, W, bias=0.0, scale=1.0, eng="act"):
        return self.S.add(eng, lambda e: e.activation(out=out, in_=in_, func=func, bias=bias, scale=scale), R, W)

    def tt(self, eng, out, in0, in1, op, R, W):
        return self.S.add(eng, lambda e: e.tensor_tensor(out=out, in0=in0, in1=in1, op=op), R, W)

    def ts(self, eng, out, in0, s1, s2, op0, op1, R, W):
        if s2 is None:
            return self.S.add(eng, lambda e: e.tensor_scalar(out=out, in0=in0, scalar1=s1, scalar2=None, op0=op0), R, W)
        return self.S.add(eng, lambda e: e.tensor_scalar(out=out, in0=in0, scalar1=s1, scalar2=s2, op0=op0, op1=op1), R, W)

    def stt(self, eng, out, in0, scalar, in1, op0, op1, R, W):
        return self.S.add(eng, lambda e: e.scalar_tensor_tensor(out=out, in0=in0, scalar=scalar, in1=in1, op0=op0, op1=op1), R, W)

    def cp(self, eng, out, in_, R, W):
        if eng == "act":
            return self.S.add(eng, lambda e: e.copy(out=out, in_=in_), R, W)
        return self.S.add(eng, lambda e: e.tensor_copy(out=out, in_=in_), R, W)

    def dma(self, eng, out, in_, R, W, is_out=False, chan=None, slow=False):
        if slow:
            return self.S.dma(eng, lambda e: e.dma_start(out=out, in_=in_, allow_slow_non_contiguous=True), R, W, chan=chan, is_out=is_out)
        return self.S.dma(eng, lambda e: e.dma_start(out=out, in_=in_), R, W, chan=chan, is_out=is_out)

    def wtile(self, src2d, r0, nk, c0, ncols):
        ln = self.S.lane
        ls = getattr(self, "laneslots", None)
        if ln is None or ls is None:
            i = self.wcount % NSLOT
            self.wcount += 1
        else:
            i = ls[ln][self.rot("w_l%d" % ln, len(ls[ln]))]
        slot = self.wsl[i]
        key = ("w", i)
        src = src2d[r0:r0 + nk * 128, c0:c0 + ncols].rearrange("(kc p) n -> p kc n", p=128)
        self.dma("pool", slot[:, 0:nk, 0:ncols], src, [], [key], chan=("wch", i))
        return slot, key

    def proj_fm(self, w2d, c0, ncols, src, srckey, nk, handler, halves=(0, 1), nmm=512, srcf=None):
        col = c0
        oc_i = 0
        while col < c0 + ncols:
            w = min(WT, c0 + ncols - col)
            slot, wkey = self.wtile(w2d, 0, nk, col, w)
            for j in range(w // 128):
                for th in halves:
                    b = self.psb("mm")
                    out = self.pb[b][:, 0:nmm]
                    if srcf is None:
                        ops = [(slot[:, kc, j * 128:(j + 1) * 128], src[:, kc, th * nmm:(th + 1) * nmm]) for kc in range(nk)]
                        sk_ = [(srckey, kc) for kc in range(nk)]
                    else:
                        ops = [(slot[:, kc, j * 128:(j + 1) * 128], srcf(kc)[0][:, th * nmm:(th + 1) * nmm]) for kc in range(nk)]
                        sk_ = [k_ for kc in range(nk) for k_ in srcf(kc)[1]]
                    self.mm(out, ops, [wkey] + sk_, [("ps", b)])
                    handler(oc_i, th, out, ("ps", b))
                oc_i += 1
            col += w

    def setup(self):
        dr = self.dram
        self.dma("sp", self.ident[:], dr["ident"], [], ["ident"])
        self.dma("sp", self.vecs[:], dr["vecs"], [], ["vecs"])
        self.dma("act", self.ckvn_b[:], dr["ckvn_b"], [], ["ckvn_b"])
        for name, t in (("ident", self.identb), ("blockdiag", self.bd), ("perm_hd", self.perm_hd), ("perm_c", self.perm_c),
                        ("m_prev", self.m_prev), ("m_next", self.m_next), ("cos_hd", self.cos_hd),
                        ("sin_hd", self.sin_hd), ("cos_c", self.cos_c), ("sin_c", self.sin_c)):
            self.dma("pool", t[:], dr[name], [], ["c_" + name], chan=("cch", name))
        self.dma("act", self.hmask[:].rearrange("p a b -> p (a b)"), dr["hmask"], [], ["c_hmask"])
        ones, sm = self.ones, self.scanmask
        self.S.add("dve", lambda e: e.memset(ones[:], 1.0), [], ["ones"], arena=False)
        epsc = self.epsc
        self.S.add("dve", lambda e: e.memset(epsc[:, 0:1], LN_EPS), [], ["epsc0"], arena=False)
        self.S.add("dve", lambda e: e.memset(epsc[:, 1:2], RMS_EPS), [], ["epsc1"], arena=False)
        self.S.add("dve", lambda e: e.memset(epsc[:, 2:3], 1.0), [], ["epsc2"], arena=False)
        self.S.add("dve", lambda e: e.memset(sm[:], 1.0), [], ["scanmask"], arena=False)
        self.S.add("dve", lambda e: e.memset(sm[:].rearrange("p (n c) -> p n c", c=CH)[:, :, 0:1], 0.0), [], ["scanmask", "scanmask2"], arena=False)
        v = self.vecs
        c0 = VC["cond"]
        self.act(self.scond[:], v[:, c0:c0 + 16].rearrange("p (c k) -> p k c", c=2), AF.Silu, ["vecs"], ["scond"])
        self.act(self.esink[:], v[:, VC["sink"]:VC["sink"] + 16], AF.Exp, ["vecs"], ["esink"])
        es4 = self.esink[:].rearrange("p (e k i) -> p e k i", e=2, k=2)
        for pos, idx in enumerate((0, 2, 1, 3)):
            self.cp("dve", self.esinkU[:, :, :, pos], es4[:, :, :, idx], ["esink"], ["esinkU"])
        lb = self.lb
        self.S.add("dve", lambda e: e.memset(lb[:, 0, :, 0, :], 0.0), [], ["lb0a"], arena=False)
        self.S.add("dve", lambda e: e.memset(lb[:, 0, :, 1, :], 1.0), [], ["lb0b"], arena=False)
        b0 = VC["b_lb"]
        dtmp = self.dummy
        self.tt("dve", dtmp[:, 0:8], v[:, b0 + 8:b0 + 16], v[:, b0:b0 + 8], ALU.subtract, ["vecs"], ["dummy"])
        self.act(lb[:, 1, :, 0, :], dtmp[:, 0:8].rearrange("p (d h) -> p d h", d=2), AF.Sigmoid, ["dummy"], ["lb1a"])
        self.ts("dve", lb[:, 1, :, 1, :], lb[:, 1, :, 0, :], -1.0, 1.0, ALU.mult, ALU.add, ["lb1a"], ["lb1b"])
        self.lbkeys = ["lb0a", "lb0b", "lb1a", "lb1b"]

    def mods(self, l):
        for _ in self.mods_gen(l):
            pass

    def mods_gen(self, l):
        w2d = self.dram["w_ada"][l]
        b = self.psb("aux")
        acc = self.pb[b]
        for t in range(24):
            slot, wkey = self.wtile(w2d, 0, 8, t * WT, WT)
            for j in range(2):
                oc = t * 2 + j
                ops = [(slot[:, kc, j * 128:(j + 1) * 128], self.scond[:, kc, :]) for kc in range(8)]
                self.mm(acc[:, oc * 2:oc * 2 + 2], ops, [wkey, "scond"], [("ps", b)])
            yield
        v = self.vecs
        bcol = VC["b_ada"] + l * 48
        for c in range(2):
            src = acc[:, 0:96].rearrange("p (o c) -> p o c", c=2)[:, :, c]
            self.tt("dve", self.modv[:, l, :, c], src, v[:, bcol:bcol + 48], ALU.add, [("ps", b), "vecs"], [("modv", l, c)])
        mk = [("modv", l, 0), ("modv", l, 1)]
        for which in (1, 4):
            m = self.modv[:, l, which * 8:(which + 1) * 8, :]
            self.ts("dve", m, m, 1.0, None, ALU.add, None, mk, mk)
        for which in (2, 5):
            m = self.modv[:, l, which * 8:(which + 1) * 8, :]
            self.ts("dve", m, m, 1.0 / ALPHA, None, ALU.mult, None, mk, mk)

    def lnfold(self, l, whichs=(0, 1)):
        v = self.vecs
        for which in whichs:
            if which == 0:
                ls, sc_i, sh_i = l, 4, 3
            else:
                if l == DEPTH - 1:
                    continue
                ls, sc_i, sh_i = l + 1, 1, 0
            gcol = VC["ln_g"] + (l * 2 + which) * 8
            bcol = VC["ln_b"] + (l * 2 + which) * 8
            for c in range(2):
                sc = self.modv[:, ls, sc_i * 8:(sc_i + 1) * 8, c]
                sh = self.modv[:, ls, sh_i * 8:(sh_i + 1) * 8, c]
                G = self.lnf[:, l, which, 0, :, c]
                B = self.lnf[:, l, which, 1, :, c]
                R = [("modv", ls, c), "vecs"]
                self.tt("dve", G, sc, v[:, gcol:gcol + 8], ALU.mult, R, [("lnf", l, which, c, 0)])
                self.tt("dve", B, sc, v[:, bcol:bcol + 8], ALU.mult, R, [("lnf", l, which, c, 1)])
                self.tt("dve", B, B, sh, ALU.add, R + [("lnf", l, which, c, 1)], [("lnf", l, which, c, 1)])

    def stgkeys(self, i):
        return [("zb", k) for k in range(i * 4, i * 4 + 4)]

    def load_x(self, g):
        xd = self.dram["xp" if g == 0 else "xs"]
        for tt in range(8):
            stg = self.stg[tt % 2]
            sk = self.stgkeys(tt % 2)
            self.dma("sp" if tt % 2 == 0 else "act", stg, xd[tt * 128:(tt + 1) * 128, :], [], sk)
            for half in range(2):
                b = self.psb("mm")
                bank = self.pb[b]
                ident = self.ident

                def fn(e, bank=bank, stg=stg, half=half, ident=ident):
                    ins = None
                    for c4 in range(4):
                        c = half * 4 + c4
                        ins = e.transpose(bank[:, c4 * 128:(c4 + 1) * 128], stg[:, c * 128:(c + 1) * 128], ident[:])
                    return ins
                self.S.add("pe", fn, sk + ["ident"], [("ps", b)])
                dst = self.x[:, half * 4:(half + 1) * 4, tt * 128:(tt + 1) * 128]
                self.cp("act" if half == 0 else "dve", dst, bank[:].rearrange("p (a b) -> p a b", b=128),
                        [("ps", b)], [("x", c, tt // 4) for c in range(half * 4, half * 4 + 4)])

    def modulate0(self, g, l):
        for c in range(8):
            sc = self.modv[:, l, 8 + c, g:g + 1]
            sh = self.modv[:, l, c, g:g + 1]
            R = [("x", c, 0), ("x", c, 1), ("modv", l, g)]
            if (c % 2 == 0 or self.cfg.get("mod_act", False)) and not self.cfg.get("mod_dve", False):
                self.act(self.hm[:, c, :], self.x[:, c, :], AF.Identity, R, [("hm", c)], bias=sh, scale=sc)
            else:
                self.ts("dve", self.hm[:, c, :], self.x[:, c, :], sc, sh, ALU.mult, ALU.add, R, [("hm", c)])

    def store_x(self, g):
        yd = self.dram["yp" if g == 0 else "ys"]
        for tt in range(8):
            stg = self.stg[tt % 2]
            sk = self.stgkeys(tt % 2)
            for half in range(2):
                b = self.psb("mm")
                bank = self.pb[b]
                x, ident = self.x, self.ident

                def fn(e, bank=bank, half=half, tt=tt, x=x, ident=ident):
                    ins = None
                    for c4 in range(4):
                        c = half * 4 + c4
                        ins = e.transpose(bank[:, c4 * 128:(c4 + 1) * 128], x[:, c, tt * 128:(tt + 1) * 128], ident[:])
                    return ins
                self.S.add("pe", fn, [("x", c, tt // 4) for c in range(half * 4, half * 4 + 4)] + ["ident"], [("ps", b)])
                self.cp("act" if half == 0 else "dve", stg[:, half * 512:(half + 1) * 512], bank[:], [("ps", b)], sk)
            self.dma("sp", yd[tt * 128:(tt + 1) * 128, :], stg, sk, [], is_out=True, chan=("och", "y", tt % 2))

    def tap(self, name):
        if name in self.cfg.get("taps", {}):
            self.dma("sp", self.dram[name], self.x[:], [("x", c, h_) for c in range(8) for h_ in range(2)], [], is_out=True, chan=("och", name))

    def ln_z(self, l, gate_i, g):
        def h(oc, th, ps, pskey):
            xs = self.x[:, oc, th * 512:(th + 1) * 512]
            gc = self.modv[:, l, gate_i * 8 + oc, g:g + 1]
            self.stt("dve", xs, ps, gc, xs, ALU.mult, ALU.add, [pskey, ("x", oc, th), ("modv", l, g)], [("x", oc, th)])
        return h

    def ln_finish(self, g, l, which, last):
        for th in range(2):
            self.ln_a(th)
            self.ln_b(g, l, which, last, th)

    def ln_a(self, th):
        sl = slice(th * 512, (th + 1) * 512)
        for oc in range(8):
            xs = self.x[:, oc, sl]
            self.cp("act" if oc % 2 == 0 else "dve", self.zb[:, oc, :], xs, [("x", oc, th)], [("zb", oc)])
            if oc % 2 == 1:
                self.act(self.zsq[:, oc, :], xs, AF.Square, [("x", oc, th)], [("zsq", oc)])
            else:
                self.tt("dve", self.zsq[:, oc, :], xs, xs, ALU.mult, [("x", oc, th)], [("zsq", oc)])

    def ln_b(self, g, l, which, last, th):
        v = self.vecs
        sl = slice(th * 512, (th + 1) * 512)
        b1 = self.psb("st")
        b2 = self.psb("st")
        self.mm(self.pb[b1][:], [(self.ones[:], self.zb[:, oc, :]) for oc in range(8)], ["ones"] + [("zb", oc) for oc in range(8)], [("ps", b1)])
        self.mm(self.pb[b2][:], [(self.ones[:], self.zsq[:, oc, :]) for oc in range(8)], ["ones"] + [("zsq", oc) for oc in range(8)], [("ps", b2)])
        mean, rstd, tA, tB = self.mean, self.rstd, self.xn[0], self.xn[1]
        self.ts("dve", mean[:], self.pb[b1][:], 1.0 / D, None, ALU.mult, None, [("ps", b1)], ["mean"])
        self.tt("dve", tA[:], mean[:], mean[:], ALU.mult, ["mean"], ["xn0"])
        self.stt("dve", tB[:], self.pb[b2][:], 1.0 / D, tA[:], ALU.mult, ALU.subtract, [("ps", b2), "xn0"], ["xn1"])
        self.rsqrt(rstd[:], tB[:], 1.0, 0, ["xn1"], ["rstd"])
        gcol = VC["ln_g"] + (l * 2 + which) * 8
        bcol = VC["ln_b"] + (l * 2 + which) * 8
        for oc in range(8):
            xs = self.x[:, oc, sl]
            xn = self.xn[oc % 2]
            xk = "xn%d" % (oc % 2)
            self.tt("dve", xn[:], xs, mean[:], ALU.subtract, [("x", oc, th), "mean"], [xk])
            self.tt("dve", xn[:], xn[:], rstd[:], ALU.mult, [xk, "rstd"], [xk])
            self.act(xs, xn[:], AF.Identity, [xk, "vecs"], [("x", oc, th)], bias=v[:, bcol + oc:bcol + oc + 1], scale=v[:, gcol + oc:gcol + oc + 1])
            if not last:
                G = self.lnf[:, l, which, 0, oc, g:g + 1]
                B = self.lnf[:, l, which, 1, oc, g:g + 1]
                self.act(self.hm[:, oc, sl], xn[:], AF.Identity, [xk, ("lnf", l, which, g, 0), ("lnf", l, which, g, 1)], [("hm", oc)], bias=B, scale=G)

    def rsqrt(self, out, in_, scale, eps_i, R, W):
        self.act(out, in_, AF.Ln, list(R) + ["epsc0", "epsc1"], W, bias=self.epsc[0:out.shape[0], eps_i:eps_i + 1], scale=scale)
        self.act(out, out, AF.Exp, W, W, scale=-0.5)

    def out_proj(self, g, l):
        self.proj_fm(self.dram["w_out"][l], 0, D, self.hm, "hm", 8, self.ln_z(l, 2, g))
        self.ln_finish(g, l, 0, False)

    def ffn(self, g, l, side=None):
        self.fence()
        self.carve_reset()
        hid = self.carve([NFF, T], BF16)
        wg, wu, wd = self.dram["w_ffn_gate"][l], self.dram["w_ffn_up"][l], self.dram["w_ffn_down"][l]
        hk = [("hm", kc) for kc in range(8)]
        for t in range(D_FF // WT):
            sg_, kg = self.wtile(wg, 0, 8, t * WT, WT)
            su_, ku = self.wtile(wu, 0, 8, t * WT, WT)
            for j in range(2):
                fc = t * 2 + j
                for th in range(2):
                    sl = slice(th * 512, (th + 1) * 512)
                    bg = self.psb("mm")
                    bu = self.psb("mm")
                    self.mm(self.pb[bg][:], [(sg_[:, kc, j * 128:(j + 1) * 128], self.hm[:, kc, sl]) for kc in range(8)], [kg] + hk, [("ps", bg)])
                    self.mm(self.pb[bu][:], [(su_[:, kc, j * 128:(j + 1) * 128], self.hm[:, kc, sl]) for kc in range(8)], [ku] + hk, [("ps", bu)])
                    i = self.rot("ffnt", 2)
                    tmp = self.rd[:, i, :]
                    self.act(tmp, self.pb[bg][:], AF.Silu, [("ps", bg)], [("rd", i)])
                    self.tt("dve", hid[:, fc, sl], tmp, self.pb[bu][:], ALU.mult, [("rd", i), ("ps", bu)], [("hid", fc)])
            if side is not None:
                for _ in range(3):
                    next(side, None)
        if side is not None:
            for _ in side:
                pass
            self.lnfold(l, (1,))
        self.S.mark("g%d l%d down" % (g, l))
        hz = self.ln_z(l, 5, g)
        pieces = ((0, 8), (8, 8), (16, 6))
        last = (l == DEPTH - 1)

        def down_quarter(q, th):
            sl = slice(th * 512, (th + 1) * 512)
            slots = [self.wtile(wd, k0 * 128, nk, q * WT, WT) for (k0, nk) in pieces]
            for j in range(2):
                oc = q * 2 + j
                b = self.psb("mm")
                ops = []
                for (slot, _), (k0, nk) in zip(slots, pieces):
                    for kc in range(nk):
                        ops.append((slot[:, kc, j * 128:(j + 1) * 128], hid[:, k0 + kc, sl]))
                self.mm(self.pb[b][:], ops, [k for _, k in slots] + [("hid", fc) for fc in range(NFF)], [("ps", b)])
                hz(oc, th, self.pb[b][:], ("ps", b))
        for q in range(4):
            down_quarter(q, 0)
        self.ln_a(0)
        down_quarter(0, 1)
        self.S.mark("g%d l%d ln2" % (g, l))
        self.ln_b(g, l, 1, last, 0)
        for q in range(1, 4):
            down_quarter(q, 1)
        self.ln_a(1)
        self.ln_b(g, l, 1, last, 1)
        self.fence()

    def attn_unit(self, qaps, qkeys, kblocks, scale, nq, fin, ng=1):
        hp = len(qaps)
        gs = hp // ng
        ob = self.psb("st")
        O = self.pb[ob]
        nkb = len(kblocks)

        def stage_a(bi):
            kts, vap, mask, kkeys = kblocks[bi]
            pi = self.ptslot()
            pt = self.ptb[:, pi, 0:hp * nq]
            for gi in range(ng):
                b = self.psb("mm")
                Sb = self.pb[b]

                def fn(e, Sb=Sb, kts=kts, gi=gi):
                    ins = None
                    for j in range(gs):
                        i = gi * gs + j
                        ins = e.matmul(Sb[:, j * nq:(j + 1) * nq], kts[i], qaps[i], start=True, stop=True)
                    return ins
                self.S.add("pe", fn, list(qkeys) + list(kkeys), [("ps", b)])
                self.act(pt[:, gi * gs * nq:(gi + 1) * gs * nq], Sb[:, 0:gs * nq], AF.Exp, [("ps", b)], [("ptb", pi, gi)], scale=scale)
            pkeys = [("ptb", pi, gi) for gi in range(ng)]
            if mask is not None:
                p3 = pt.rearrange("p (h q) -> p h q", q=nq)
                m3 = mask.unsqueeze(1).to_broadcast([128, hp, nq])
                self.tt("dve", p3, p3, m3, ALU.mult, pkeys + ["c_m_prev", "c_m_next"], pkeys)
            return pt, pkeys

        def stage_b(bi, pt, pkeys):
            kts, vap, mask, kkeys = kblocks[bi]

            def fn2(e, pt=pt, vap=vap, first=(bi == 0), lastb=(bi == nkb - 1)):
                ins = None
                for i in range(hp):
                    ins = e.matmul(O[:, i * nq:(i + 1) * nq], vap, pt[:, i * nq:(i + 1) * nq], start=(first and i == 0), stop=lastb,
                                   skip_group_check=True)
                return ins
            self.S.add("pe", fn2, pkeys + list(kkeys), [("ps", ob)])

        cur = stage_a(0)
        for bi in range(nkb):
            nxt = stage_a(bi + 1) if bi + 1 < nkb else None
            stage_b(bi, *cur)
            cur = nxt
        if getattr(fin, "batched", False):
            fin(O[:, 0:hp * nq], ("ps", ob))
        else:
            for i in range(hp):
                fin(i, O[:, i * nq:(i + 1) * nq], ("ps", ob))

    def attn_fin_batched(self, parts, dkeys, nq, hp, sink_b=None):
        def fin(O, okey):
            ri = self.rdslot()
            rd = self.rd[64:128, ri, 0:hp * nq]
            if sink_b is not None:
                self.tt("dve", rd.rearrange("p (h q) -> p h q", q=nq), O[64:128, :].rearrange("p (h q) -> p h q", q=nq), sink_b, ALU.add,
                        [okey, "esinkU"], [("rd", ri)])
                self.act(rd, rd, AF.Ln, [("rd", ri)], [("rd", ri)])
            else:
                self.act(rd, O[64:128, :], AF.Ln, [okey], [("rd", ri)])
            self.act(rd, rd, AF.Exp, [("rd", ri)], [("rd", ri)], scale=-1.0)
            for (c0, ncols, out_ap, shp) in parts:
                i0 = O[0:64, c0:c0 + ncols]
                i1 = self.rd[64:128, ri, c0:c0 + ncols]
                if shp is not None:
                    i0 = i0.rearrange("p (a q) -> p a q", q=shp)
                    i1 = i1.rearrange("p (a q) -> p a q", q=shp)
                self.tt("dve", out_ap, i0, i1, ALU.mult, [okey, ("rd", ri)], dkeys)
        fin.batched = True
        return fin

    def attn_fin(self, dst, dkey, sink_ap=None):
        def fin(i, O, okey):
            nq = O.shape[-1]
            ri = self.rdslot()
            rd = self.rd[64:128, ri, 0:nq]
            if sink_ap is not None:
                self.ts("dve", rd, O[64:128, :], sink_ap(i), None, ALU.add, None, [okey, "esink"], [("rd", ri)])
                self.act(rd, rd, AF.Ln, [("rd", ri)], [("rd", ri)])
            else:
                self.act(rd, O[64:128, :], AF.Ln, [okey], [("rd", ri)])
            self.act(rd, rd, AF.Exp, [("rd", ri)], [("rd", ri)], scale=-1.0)
            self.tt("dve", dst(i), O[0:64, :], rd, ALU.mult, [okey, ("rd", ri)], dkey(i))
        return fin

    def rope(self, xap, xkey, perm, cos, sin, n, rows=128):
        for c0 in range(0, n, 512):
            w = min(512, n - c0)
            b = self.psb("aux")
            ps = self.pb[b][0:rows, 0:w]
            xs = xap[:, c0:c0 + w]
            self.mm(ps, [(perm, xs)], list(xkey) + ["c_perm_hd", "c_perm_c"], [("ps", b)])
            t1 = self.xn[0][0:rows, 0:w]
            t2 = self.xn[1][0:rows, 0:w]
            self.tt("dve", t1, xs, cos[:, c0:c0 + w], ALU.mult, list(xkey) + ["c_cos_hd", "c_cos_c"], ["xn0"])
            self.tt("dve", t2, ps, sin[:, c0:c0 + w], ALU.mult, [("ps", b), "c_sin_hd", "c_sin_c"], ["xn1"])
            self.tt("dve", xs, t1, t2, ALU.add, ["xn0", "xn1"], list(xkey))

    def even_mixer(self, g, l):
        e = l // 2
        dr = self.dram
        w2d = dr["w_in_ab"][e]
        hk = [("hm", kc) for kc in range(8)]
        self.carve_reset()
        A = {}
        A["qa"] = qa = self.carve([4, T], BF16)
        A["kaT"] = kaT = self.carve([2, T + PAST], BF16)
        A["vaug"] = vaug = self.carve([12, 2, 128], BF16)
        A["vtok"] = vtok = self.carve([8, 512], BF16)
        A["mixb"] = mixb = self.carve([4, T], BF16)
        A["qs"] = self.carve([T], BF16)
        A["gsil"] = self.carve([T], BF16)
        A["tset"] = [(self.carve([512], F32), self.carve([512], F32), self.carve([512], F32)) for _ in range(2)]
        A["qhat"] = self.carve([2, T], BF16)
        A["khat"] = self.carve([2, T], BF16)
        A["ktok"] = self.carve([8, 2, 128], BF16)
        A["AT"] = self.carve([2, 8, 128], BF16)
        A["obuf"] = self.carve([T], F32)
        A["Ebuf2"] = [self.carve([8], F32), self.carve([8], F32)]
        A["Lb"] = self.carve([2, 16], F32)
        A["Eh"] = self.carve([2, 16], F32)
        A["Fb"] = self.carve([2, 16], F32)
        A["Sst"] = self.carve([8, 128], F32)
        A["dbf"] = self.carve([8, 128], BF16)
        self.mixb = mixb

        self.S.mark("g%d l%d hgrn+attnA" % (g, l))

        def pre_hgrn():
            AT_ = A["AT"]
            self.S.add("dve", lambda e_: e_.memset(AT_[:], 0.0), [], ["AT0", ("AT", 0), ("AT", 1)])
            for ti in range(2):
                slot, wkey = self.wtile(w2d, 0, 8, 1280 + ti * 256, 256)
                for tt in range(8):
                    b = self.psb("mm")
                    self.mm(self.pb[b][:, 0:256], [(self.hm[:, kc, tt * 128:(tt + 1) * 128], slot[:, kc, :]) for kc in range(8)], [wkey] + hk, [("ps", b)])
                    self.cp("act" if tt % 2 == 0 else "dve", vtok[:, tt, ti * 256:(ti + 1) * 256], self.pb[b][:, 0:256], [("ps", b)], [("vtok", tt)])

            yield

        def pre_attn():
            self.S.add("dve", lambda e_: e_.memset(vaug[:, :, :, 64:128], 1.0), [], ["vaug_ones"])
            if g == 1:
                for j in range(4):
                    self.dma("pool", vaug[:, 8 + j, :, 0:64], dr["cav"][e, j * 128:(j + 1) * 128, :].rearrange("p (k d) -> p k d", d=64),
                             ["arena"], [("vaug", 8 + j)], chan=("cch", "cav", j))
                for j in range(4):
                    buf = self.rot("cst", 2)
                    self.dma("sp", self.cst[:, buf, 0:256], dr["cakd"][e, j * 128:(j + 1) * 128, :], [], [("cst", buf)], chan=("lch", "cst", buf))
                    b = self.psb("aux")
                    bank, cst, ident = self.pb[b], self.cst, self.ident

                    def fn(e_, bank=bank, cst=cst, buf=buf, ident=ident):
                        e_.transpose(bank[:, 0:128], cst[:, buf, 0:128], ident[:])
                        return e_.transpose(bank[:, 128:256], cst[:, buf, 128:256], ident[:])
                    self.S.add("pe", fn, [("cst", buf), "ident"], [("ps", b)])
                    self.cp("act", kaT[:, :, T + j * 128:T + (j + 1) * 128], bank[:, 0:256].rearrange("p (k t) -> p k t", k=2), [("ps", b)], [("kaT", 0), ("kaT", 1)])

            def h_qa(oc, th, ps, pk):
                self.cp("act" if th == 0 else "dve", qa[:, oc, th * 512:(th + 1) * 512], ps, [pk], [("qa", oc)])
            self.proj_fm(w2d, 0, 512, self.hm, "hm", 8, h_qa)

            def h_ka(oc, th, ps, pk):
                self.cp("act" if th == 0 else "dve", kaT[:, oc, th * 512:(th + 1) * 512], ps, [pk], [("kaT", oc)])
            self.proj_fm(w2d, EVEN_IN, 256, self.hm, "hm", 8, h_ka)
            if g == 1:
                for c in range(4):
                    self.rope(qa[:, c, :], [("qa", c)], self.perm_hd[:], self.cos_hd, self.sin_hd, T)
                for c in range(2):
                    self.rope(kaT[:, c, 0:T], [("kaT", c)], self.perm_hd[:], self.cos_hd, self.sin_hd, T)
            slot, wkey = self.wtile(w2d, 0, 8, 512, 256)
            for tt in range(8):
                b = self.psb("mm")
                self.mm(self.pb[b][:, 0:256], [(self.hm[:, kc, tt * 128:(tt + 1) * 128], slot[:, kc, :]) for kc in range(8)], [wkey] + hk, [("ps", b)])
                sk_ = self.cfg.get("skip", "")
                if "v" not in sk_:
                    self.cp("act", vaug[:, tt, :, 0:64], self.pb[b][:, 128:256].rearrange("p (k d) -> p k d", d=64), [("ps", b)], [("vaug", tt)])
                if g == 0 and "c" not in sk_:
                    buf = self.rot("cst", 2)
                    self.cp("dve", self.cst[:, buf, 0:256], self.pb[b][:, 0:256], [("ps", b)], [("cst", buf)])
                    seq, p0 = tt // 2, (tt % 2) * 128
                    if "d" not in sk_:
                        self.dma("sp", dr["nak"][seq, e, p0:p0 + 128, :], self.cst[:, buf, 0:128], [("cst", buf)], [], is_out=True, chan=("och", "csta", buf))
                        self.dma("sp", dr["nav"][seq, e, p0:p0 + 128, :], self.cst[:, buf, 128:256], [("cst", buf)], [], is_out=True, chan=("och", "cstb", buf))

            yield

        self.mixa = mixa = self.zb[:].rearrange("p (c a) b -> p c (a b)", a=2)

        def hgrn_all():
            yield from pre_hgrn()
            for hd in range(4):
                yield from self.hgrn_head(g, e, hd, w2d, A)

        def attn_all():
            yield from pre_attn()
            scale = HD ** -0.5
            if g == 0:
                units = [(seq * 256 + qb * 128, [(seq * 2 + kb, None) for kb in range(2)]) for seq in range(4) for qb in range(2)]
            else:
                units = []
                for qt in range(8):
                    kbs = []
                    for j in (qt - 1, qt, qt + 1):
                        if 0 <= j < 8:
                            kbs.append((j, None if j == qt else (self.m_prev[:] if j == qt - 1 else self.m_next[:])))
                    kbs += [(8 + j, None) for j in range(4)]
                    units.append((qt * 128, kbs))
            for q0, kbs in units:
                for kvh in range(2):
                    heads = [kvh * 4 + i for i in (0, 2, 1, 3)]
                    qaps = [qa[(h % 2) * 64:(h % 2) * 64 + 64, h // 2, q0:q0 + 128] for h in heads]
                    qkeys = [("qa", c) for c in (kvh * 2, kvh * 2 + 1)]
                    kblocks = []
                    for (kt, mask) in kbs:
                        kts = [kaT[(h % 2) * 64:(h % 2) * 64 + 64, kvh, kt * 128:(kt + 1) * 128] for h in heads]
                        kblocks.append((kts, vaug[:, kt, kvh, :], mask, [("kaT", kvh), ("vaug", kt), "vaug_ones"]))

                    parts = [(0, 256, mixa[0:64, kvh * 2:kvh * 2 + 2, q0:q0 + 128], 128),
                             (256, 256, mixa[64:128, kvh * 2:kvh * 2 + 2, q0:q0 + 128], 128)]
                    dk = [("zb", 2 * c + q0 // 512) for c in (kvh * 2, kvh * 2 + 1)]
                    sb_ = self.esinkU[64:128, e, kvh, :].unsqueeze(2).to_broadcast([64, 4, 128])
                    self.attn_unit(qaps, qkeys, kblocks, scale, 128, self.attn_fin_batched(parts, dk, 128, 4, sb_), ng=2)
                    yield
        self.run_lanes([hgrn_all(), attn_all()], [{"mm": [0, 1], "st": [2, 3], "aux": [4]}, {"mm": [5, 6], "st": [7], "aux": [7]}],
                       slots=[[0, 1, 2, 3], [4, 5]])

    def interleave(self, gens, counts):
        done = [False] * len(gens)
        acc = [0.0] * len(gens)
        while not all(done):
            for gi, gen in enumerate(gens):
                if done[gi]:
                    continue
                acc[gi] += counts[gi] / float(counts[0]) if not done[0] else 1.0
                while acc[gi] >= 1.0 and not done[gi]:
                    acc[gi] -= 1.0
                    self.stream = gi
                    try:
                        next(gen)
                    except StopIteration:
                        done[gi] = True
                    self.stream = None

    def hgrn_head(self, g, e, hd, w2d, A):
        dr = self.dram
        qs, gsil = A["qs"], A["gsil"]
        qhat, khat, ktok, AT, obuf = A["qhat"], A["khat"], A["ktok"], A["AT"], A["obuf"]
        Lb, Eh, Fb, Sst, dbf, vtok, mixb = A["Lb"], A["Eh"], A["Fb"], A["Sst"], A["dbf"], A["vtok"], A["mixb"]
        DKS = 128 ** -0.5
        NCK = T // CH
        for (col, dst, key) in ((768 + hd * 128, qs, "qs"), (2816 + hd * 128, gsil, "gsil")):
            def hh(oc, th, ps, pk, dst=dst, key=key):
                self.act(dst[:, th * 512:(th + 1) * 512], ps, AF.Silu, [pk], [key])
            self.proj_fm(w2d, col, 128, self.hm, "hm", 8, hh)
        for d in range(2):
            lbv = self.lb[:, e, d, 0, hd:hd + 1]
            omv = self.lb[:, e, d, 1, hd:hd + 1]

            def hg(oc, th, ps, pk, d=d, lbv=lbv, omv=omv):
                sl = slice(th * 512, (th + 1) * 512)
                nc_ = 512 // CH
                ti_ = self.rot("tset", 2)
                t1, t2, t3 = A["tset"][ti_]
                k1, k2, k3, ke = ("t1", ti_), ("t2", ti_), ("t3", ti_), ("Ebuf", ti_)
                Ebuf = A["Ebuf2"][ti_]
                self.act(t2[:], ps, AF.Sigmoid, [pk], [k2], scale=-1.0)
                self.act(t2[:], t2[:], AF.Identity, [k2] + self.lbkeys, [k2], scale=omv)
                self.act(t1[:], t2[:], AF.Ln, [k2, "epsc2"], [k1], bias=self.epsc[:, 2:3], scale=-1.0)
                sm = self.scanmask
                self.S.add("dve", lambda e_: e_.tensor_tensor_scan(out=t3[:], data0=sm[:], data1=t1[:], initial=0.0, op0=ALU.mult, op1=ALU.add),
                           [k1, "scanmask", "scanmask2"], [k3])
                t33 = t3[:].rearrange("p (n c) -> p n c", c=CH)
                self.cp("dve", Lb[:, d, th * nc_:(th + 1) * nc_], t33[:, :, CH - 1], [k3], [("Lb", d)])
                self.ts("dve", Ebuf[:], t33[:, :, CH - 1], 0.5, None, ALU.mult, None, [k3], [ke])
                eb = Ebuf[:].unsqueeze(2).to_broadcast([128, nc_, CH])
                if d == 0:
                    self.tt("dve", t33, t33, eb, ALU.subtract, [k3, ke], [k3])
                else:
                    self.tt("dve", t3[:], t1[:], t3[:], ALU.subtract, [k1, k3], [k3])
                    self.tt("dve", t33, t33, eb, ALU.add, [k3, ke], [k3])
                self.act(t1[:], t3[:], AF.Exp, [k3], [k1])
                self.act(t3[:], t3[:], AF.Exp, [k3], [k3], scale=-1.0)
                self.stt("dve", qhat[:, d, sl], qs[:, sl], DKS, t1[:], ALU.mult, ALU.mult, ["qs", k1], [("qhat", d)])
                self.tt("dve", khat[:, d, sl], t2[:], t3[:], ALU.mult, [k3, k2], [("khat", d)])
            self.proj_fm(w2d, 1792 + d * 512 + hd * 128, 128, self.hm, "hm", 8, hg)
            yield
        yield
        nseq = 4 if g == 0 else 1
        nst = NCK // nseq
        self.act(Eh[:], Lb[:], AF.Exp, [("Lb", 0), ("Lb", 1)], ["Eh"], scale=0.5)
        self.cp("dve", Fb[:], Eh[:], ["Eh"], ["Fb"])
        if nst > 1:
            f0 = Fb[:, 0, :].rearrange("p (s n) -> p s n", s=nseq)
            e0 = Eh[:, 0, :].rearrange("p (s n) -> p s n", s=nseq)
            self.tt("dve", f0[:, :, 0:nst - 1], f0[:, :, 0:nst - 1], e0[:, :, 1:nst], ALU.mult, ["Eh", "Fb"], ["Fb"])
            f1 = Fb[:, 1, :].rearrange("p (s n) -> p s n", s=nseq)
            e1 = Eh[:, 1, :].rearrange("p (s n) -> p s n", s=nseq)
            self.tt("dve", f1[:, :, 1:nst], f1[:, :, 1:nst], e1[:, :, 0:nst - 1], ALU.mult, ["Eh", "Fb"], ["Fb"])
        for d in range(2):
            for th in range(2):
                sl = slice(th * 512, (th + 1) * 512)
                kf = self.zsq[:, 1, :]
                fb_ = Fb[:, d, th * 8:(th + 1) * 8].unsqueeze(2).to_broadcast([128, 8, CH])
                self.tt("dve", kf.rearrange("p (n c) -> p n c", c=CH), khat[:, d, sl].rearrange("p (n c) -> p n c", c=CH), fb_, ALU.mult,
                        [("khat", d), "Fb"], [("zsq", 1)])
                b = self.psb("aux")
                bankb = self.pb[b][:].bitcast(BF16)
                identb = self.identb

                def fn(e_, bankb=bankb, kf=kf, identb=identb):
                    ins = None
                    for i in range(4):
                        ins = e_.transpose(bankb[:, i * 128:(i + 1) * 128], kf[:, i * 128:(i + 1) * 128], identb[:])
                    return ins
                self.S.add("pe", fn, [("zsq", 1), "c_ident"], [("ps", b)])
                self.cp("act", ktok[:, th * 4:(th + 1) * 4, d, :], bankb[:, 0:512].rearrange("p (i k) -> p i k", k=128), [("ps", b)], [("ktok", d)])
                b = self.psb("aux")
                bank = self.pb[b]

                def fn2(e_, bank=bank, d=d, th=th):
                    ins = None
                    for i in range(4):
                        c0 = (th * 4 + i) * 128
                        ins = e_.matmul(bank[:, i * 128:(i + 1) * 128], khat[:, d, c0:c0 + 128], qhat[:, d, c0:c0 + 128], start=True, stop=True)
                    return ins
                self.S.add("pe", fn2, [("khat", d), ("qhat", d)], [("ps", b)])
                m = self.hmask[:, d, :].unsqueeze(1).to_broadcast([128, 4, 128])
                atv = AT[:, d, th * 4:(th + 1) * 4, :]
                bv = bank[:].rearrange("p (i t) -> p i t", t=128)
                self.S.add("dve", lambda e_, atv=atv, m=m, bv=bv: e_.copy_predicated(out=atv, mask=m, data=bv), [("ps", b), "c_hmask", ("AT", d), "AT0"], [("AT", d)])
                yield
        yield
        if g == 0:
            self.S.add("dve", lambda e_: e_.memset(Sst[:], 0.0), [], [("Sst", i) for i in range(8)])
        else:
            for d in range(2):
                self.dma("sp", Sst[:, d, :], dr["stb"][e, d, hd], ["arena"], [("Sst", d)], chan=("lch", "Sst", d))
                c_first = 0 if d == 0 else nst - 1
                self.ts("dve", Sst[:, d, :], Sst[:, d, :], Eh[:, d, c_first:c_first + 1], None, ALU.mult, None, [("Sst", d), "Eh"], [("Sst", d)])
        nchain = 2 * nseq
        self.cp("act", dbf[:, 0:nchain, :], Sst[:, 0:nchain, :], [("Sst", i) for i in range(nchain)], [("dbf", i) for i in range(nchain)])
        written = set()
        for step in range(nst):
            for d in range(2):
                ob_ = self.psb("st")
                obank = self.pb[ob_]
                cn0 = None
                for sq in range(nseq):
                    ci = sq * 2 + d if g == 0 else d
                    cn = sq * nst + (step if d == 0 else nst - 1 - step)
                    if cn0 is None:
                        cn0 = cn
                    tt_, hb = cn // 2, (cn % 2) * 64
                    Vn = vtok[hb:hb + 64, tt_, hd * 128:(hd + 1) * 128]
                    ops = [(Vn, AT[hb:hb + 64, d, tt_, hb:hb + 64]), (dbf[:, ci, :], qhat[:, d, cn * CH:(cn + 1) * CH])]
                    self.mm(obank[:, sq * CH:(sq + 1) * CH], ops, [("vtok", tt_), ("AT", d), ("dbf", ci), ("qhat", d)], [("ps", ob_)])
                    ub = self.psb("mm")
                    self.mm(self.pb[ub][:, 0:128], [(ktok[hb:hb + 64, tt_, d, :], Vn)], [("ktok", d), ("vtok", tt_)], [("ps", ub)])
                    self.stt("dve", Sst[:, ci, :], Sst[:, ci, :], Fb[:, d, cn:cn + 1], self.pb[ub][:, 0:128], ALU.mult, ALU.add,
                             [("Sst", ci), "Fb", ("ps", ub)], [("Sst", ci)])
                    if step < nst - 1:
                        self.cp("act", dbf[:, ci, :], Sst[:, ci, :], [("Sst", ci)], [("dbf", ci)])
                off = (cn0 % nst) * CH
                dstv = obuf[:].rearrange("p (s t) -> p s t", s=nseq)[:, :, off:off + CH]
                srcv = obank[:, 0:nseq * CH].rearrange("p (s t) -> p s t", t=CH)
                if cn0 in written:
                    self.tt("dve", dstv, dstv, srcv, ALU.add, [("ps", ob_), "obuf"], ["obuf"])
                else:
                    self.cp("act", dstv, srcv, [("ps", ob_)], ["obuf"])
                    written.add(cn0)
            yield
        if g == 0:
            for sq in range(4):
                for d in range(2):
                    ci = sq * 2 + d
                    self.dma("sp", dr["nsb"][sq, e, d, hd], Sst[:, ci, :], [("Sst", ci), "arena"], [], is_out=True, chan=("och", "Sst", ci))
        yield
        gcol = self.vecs[:, VC["gnorm"] + e:VC["gnorm"] + e + 1]
        for th in range(2):
            sl = slice(th * 512, (th + 1) * 512)
            self.act(self.zsq[:, 0, :], obuf[:, sl], AF.Square, ["obuf"], [("zsq", 0)])
            b = self.psb("aux")
            self.mm(self.pb[b][:], [(self.ones[:], self.zsq[:, 0, :])], ["ones", ("zsq", 0)], [("ps", b)])
            self.rsqrt(self.rstd[:], self.pb[b][:], 1.0 / 128, 1, [("ps", b)], ["rstd"])
            self.stt("dve", self.mean[:], obuf[:, sl], gcol, self.rstd[:], ALU.mult, ALU.mult, ["obuf", "rstd", "vecs"], ["mean"])
            self.tt("dve", mixb[:, hd, sl], self.mean[:], gsil[:, sl], ALU.mult, ["mean", "gsil"], [("mixb", hd)])
        yield

    def odd_mixer(self, g, l):
        o = l // 2
        dr = self.dram
        w2d = dr["w_in_cd"][o]
        hk = [("hm", kc) for kc in range(8)]
        NK = T if g == 0 else T + PAST
        nkt = NK // 128
        self.carve_reset()
        cqn = self.carve([3, T], BF16)
        latT = self.carve([2, T + PAST], BF16)
        kpeT = self.carve([T + PAST], BF16)
        Qc = self.carve([4, T], BF16)
        Kc = self.carve([4, T + PAST], BF16)
        vaugC = self.carve([12, 4, 128], BF16)
        qd = self.carve([4, T], BF16)
        kdT = self.carve([4, T + PAST], BF16)
        vaugD = self.carve([12, 4, 128], BF16)
        v = self.vecs
        self.S.add("dve", lambda e_: e_.memset(vaugC[:, :, :, 64:128], 1.0), [], ["vaugC_ones"])
        self.S.add("dve", lambda e_: e_.memset(Qc[:], 0.0), [], [("Qc", h_) for h_ in range(4)])
        self.S.add("dve", lambda e_: e_.memset(Kc[:], 0.0), [], [("Kc", h_) for h_ in range(4)])
        self.S.add("dve", lambda e_: e_.memset(vaugD[:, :, :, 64:128], 1.0), [], ["vaugD_ones"])

        for th in range(2):
            sl = slice(th * 512, (th + 1) * 512)

            def h_cq(oc, th_, ps, pk, sl=sl):
                self.cp("dve", cqn[:, oc, sl], ps, [pk], [("cqn", oc)])
                self.act(self.zsq[:, oc, :], ps, AF.Square, [pk], [("zsq", oc)])
            self.proj_fm(w2d, 0, 384, self.hm, "hm", 8, h_cq, halves=(th,))
            b = self.psb("st")
            self.mm(self.pb[b][:], [(self.ones[:], self.zsq[:, oc, :]) for oc in range(3)], ["ones"] + [("zsq", oc) for oc in range(3)], [("ps", b)])
            self.rsqrt(self.rstd[:], self.pb[b][:], 1.0 / 384, 1, [("ps", b)], ["rstd"])
            for oc in range(3):
                gc = v[:, VC["cqn"] + o * 3 + oc:VC["cqn"] + o * 3 + oc + 1]
                self.stt("dve", cqn[:, oc, sl], cqn[:, oc, sl], gc, self.rstd[:], ALU.mult, ALU.mult, [("cqn", oc), "rstd", "vecs"], [("cqn", oc)])

        def normed(dstbuf, dkeyname, gcol):
            pend = []

            def post(oc, sl, zi):
                b = self.psb("aux")
                self.mm(self.pb[b][:], [(self.bd[:], self.zsq[:, zi, :])], ["c_blockdiag", ("zsq", zi)], [("ps", b)])
                ti = self.rot("rdn", 2)
                tmp = self.rd[:, ti, :]
                self.rsqrt(tmp, self.pb[b][:], 1.0 / 64, 1, [("ps", b)], [("rd", ti)])
                self.stt("dve", dstbuf[:, oc, sl], dstbuf[:, oc, sl], gcol, tmp, ALU.mult, ALU.mult, [(dkeyname, oc), ("rd", ti), "vecs"], [(dkeyname, oc)])

            def h(oc, th, ps, pk):
                sl = slice(th * 512, (th + 1) * 512)
                zi = 4 + self.rot("zq", 4)
                self.cp("dve", dstbuf[:, oc, sl], ps, [pk], [(dkeyname, oc)])
                self.act(self.zsq[:, zi, :], ps, AF.Square, [pk], [("zsq", zi)])
                while len(pend) > 1:
                    post(*pend.pop(0))
                pend.append((oc, sl, zi))

            def flush():
                while pend:
                    post(*pend.pop(0))
            h.flush = flush
            return h
        hq = normed(qd, "qd", v[:, VC["dqn"] + o:VC["dqn"] + o + 1])
        hkd = normed(kdT, "kdT", v[:, VC["dkn"] + o:VC["dkn"] + o + 1])
        self.proj_fm(w2d, 672, 512, self.hm, "hm", 8, hq)
        hq.flush()
        self.proj_fm(w2d, ODD_IN, 512, self.hm, "hm", 8, hkd)
        hkd.flush()
        if g == 1:
            for c in range(4):
                self.rope(qd[:, c, :], [("qd", c)], self.perm_hd[:], self.cos_hd, self.sin_hd, T)
                self.rope(kdT[:, c, 0:T], [("kdT", c)], self.perm_hd[:], self.cos_hd, self.sin_hd, T)

        def lat_tile(buf, col0, R):
            b2 = self.psb("aux")
            bank, cst, ident = self.pb[b2], self.cst, self.ident

            def fn(e_, bank=bank, cst=cst, buf=buf, ident=ident):
                e_.transpose(bank[:, 0:128], cst[:, buf, 0:128], ident[:])
                e_.transpose(bank[:, 128:256], cst[:, buf, 128:256], ident[:])
                return e_.transpose(bank[0:32, 256:384], cst[:, buf, 256:288], ident[:])
            self.S.add("pe", fn, R + ["ident"], [("ps", b2)])
            self.cp("act", latT[:, :, col0:col0 + 128], bank[:, 0:256].rearrange("p (k t) -> p k t", k=2), [("ps", b2)], [("latT", 0), ("latT", 1)])
            self.cp("dve", kpeT[0:32, col0:col0 + 128], bank[0:32, 256:384], [("ps", b2)], ["kpeT"])

        slotA, kA = self.wtile(w2d, 0, 8, 384, 256)
        slotB, kB = self.wtile(w2d, 0, 8, 640, 32)
        ss1 = self.dummy[:, 2:3]
        for tt in range(8):
            b = self.psb("mm")
            ps = self.pb[b]
            self.mm(ps[:, 0:256], [(self.hm[:, kc, tt * 128:(tt + 1) * 128], slotA[:, kc, :]) for kc in range(8)], [kA] + hk, [("ps", b)])
            self.mm(ps[:, 256:288], [(self.hm[:, kc, tt * 128:(tt + 1) * 128], slotB[:, kc, 0:32]) for kc in range(8)], [kB] + hk, [("ps", b)])
            self.act(self.tA[:, 0:256], ps[:, 0:256], AF.Square, [("ps", b)], ["tA"])
            self.S.add("dve", lambda e_: e_.reduce_sum(out=ss1, in_=self.tA[:, 0:256], axis=mybir.AxisListType.X), ["tA"], ["ss1"])
            self.rsqrt(ss1, ss1, 1.0 / 256, 1, ["ss1"], ["ss1"])
            buf = self.rot("cst", 2)
            self.stt("dve", self.cst[:, buf, 0:256], ps[:, 0:256], ss1, self.ckvn_b[:, o * 256:(o + 1) * 256], ALU.mult, ALU.mult,
                     [("ps", b), "ss1", "ckvn_b"], [("cst", buf)])
            self.cp("act", self.cst[:, buf, 256:288], ps[:, 256:288], [("ps", b)], [("cstpe", buf)])
            if g == 0:
                seq, p0 = tt // 2, (tt % 2) * 128
                self.dma("sp", dr["nckv"][seq, o, p0:p0 + 128, :], self.cst[:, buf, 0:256], [("cst", buf)], [], is_out=True, chan=("och", "csta", buf))
                self.dma("sp", dr["ncpe"][seq, o, p0:p0 + 128, :], self.cst[:, buf, 256:288], [("cstpe", buf)], [], is_out=True, chan=("och", "cstb", buf))
            lat_tile(buf, tt * 128, [("cst", buf), ("cstpe", buf)])
        if g == 1:
            for j in range(4):
                buf = self.rot("cst", 2)
                self.dma("sp", self.cst[:, buf, 0:256], dr["cckv"][o, j * 128:(j + 1) * 128, :], [], [("cst", buf)], chan=("lch", "cst", buf))
                self.dma("act", self.cst[:, buf, 256:288], dr["ccpe"][o, j * 128:(j + 1) * 128, :], [], [("cstpe", buf)], chan=("lch", "cstpe", buf))
                lat_tile(buf, T + j * 128, [("cst", buf), ("cstpe", buf)])
            self.rope(kpeT[0:32, 0:T], ["kpeT"], self.perm_c[0:32, 0:32], self.cos_c[0:32, :], self.sin_c[0:32, :], T, rows=32)

        slot, wkey = self.wtile(w2d, 0, 8, 1440, 256)
        for tt in range(8):
            b = self.psb("mm")
            self.mm(self.pb[b][:, 0:256], [(self.hm[:, kc, tt * 128:(tt + 1) * 128], slot[:, kc, :]) for kc in range(8)], [wkey] + hk, [("ps", b)])
            self.cp("act", vaugD[:, tt, :, 0:64], self.pb[b][:, 0:256].rearrange("p (k d) -> p k d", d=64), [("ps", b)], [("vaugD", tt)])
            if g == 0:
                buf = self.rot("cst", 2)
                self.cp("dve", self.cst[:, buf, 0:256], self.pb[b][:, 0:256], [("ps", b)], [("cst", buf)])
                seq, p0 = tt // 2, (tt % 2) * 128
                self.dma("sp", dr["ndv"][seq, o, p0:p0 + 128, :], self.cst[:, buf, 0:256], [("cst", buf)], [], is_out=True, chan=("och", "csta", buf))
        if g == 1:
            for j in range(4):
                self.dma("pool", vaugD[:, 8 + j, :, 0:64], dr["cdv"][o, j * 128:(j + 1) * 128, :].rearrange("p (k d) -> p k d", d=64),
                         ["arena"], [("vaugD", 8 + j)], chan=("cch", "cdv", j))
                stg = self.stg[j % 2]
                sk = self.stgkeys(j % 2)
                self.dma("sp", stg[:, 0:512], dr["cdkd"][o, j * 128:(j + 1) * 128, :], [], sk, chan=("lch", "stg", j % 2))
                b = self.psb("mm")
                bank, ident = self.pb[b], self.ident

                def fn(e_, bank=bank, stg=stg, ident=ident):
                    ins = None
                    for c in range(4):
                        ins = e_.transpose(bank[:, c * 128:(c + 1) * 128], stg[:, c * 128:(c + 1) * 128], ident[:])
                    return ins
                self.S.add("pe", fn, sk + ["ident"], [("ps", b)])
                self.cp("act", kdT[:, :, T + j * 128:T + (j + 1) * 128], bank[:].rearrange("p (c t) -> p c t", c=4), [("ps", b)], [("kdT", c) for c in range(4)])
        else:
            for tt in range(8):
                b = self.psb("aux")
                bankb = self.pb[b][:].bitcast(BF16)
                identb = self.identb

                def fn(e_, bankb=bankb, tt=tt, identb=identb):
                    ins = None
                    for c in range(4):
                        ins = e_.transpose(bankb[:, c * 128:(c + 1) * 128], kdT[:, c, tt * 128:(tt + 1) * 128], identb[:])
                    return ins
                self.S.add("pe", fn, [("kdT", c) for c in range(4)] + ["c_ident"], [("ps", b)])
                buf = self.rot("cst", 2)
                self.cp("dve", self.cst[:, buf, 0:256].rearrange("p (c d) -> p c d", d=64), bankb[:, 0:512].rearrange("p (c x) -> p c x", x=128)[:, :, 0:64],
                        [("ps", b)], [("cst", buf)])
                seq, p0 = tt // 2, (tt % 2) * 128
                self.dma("sp", dr["ndk"][seq, o, p0:p0 + 128, :], self.cst[:, buf, 0:256], [("cst", buf)], [], is_out=True, chan=("och", "csta", buf))

        self.S.mark("g%d l%d mixC+D" % (g, l))

        def c_stream():
            wq, wkv = dr["c_w_q_up"][o], dr["c_w_kv_up"][o]
            scale_c = 96 ** -0.5
            for bi in range(2):
                for pair in range(2):
                    slot, wkey = self.wtile(wq, 0, 3, (bi * 2 + pair) * 192, 192)
                    for hh in range(2):
                        hl = pair * 2 + hh
                        for th in range(2):
                            sl = slice(th * 512, (th + 1) * 512)
                            b = self.psb("mm")
                            self.mm(self.pb[b][0:96, :], [(slot[:, kc, hh * 96:(hh + 1) * 96], cqn[:, kc, sl]) for kc in range(3)],
                                    [wkey] + [("cqn", kc) for kc in range(3)], [("ps", b)])
                            self.cp("act" if th == 0 else "dve", Qc[0:96, hl, sl], self.pb[b][0:96, :], [("ps", b)], [("Qc", hl)])
                        if g == 1:
                            for th in range(2):
                                sl = slice(th * 512, (th + 1) * 512)
                                b = self.psb("aux")
                                ps = self.pb[b][0:96, :]
                                self.mm(ps, [(self.perm_c[0:96, 0:96], Qc[0:96, hl, sl])], [("Qc", hl), "c_perm_c"], [("ps", b)])
                                t1 = self.xn[0][0:32, :]
                                t2 = self.xn[1][0:32, :]
                                self.tt("dve", t1, Qc[64:96, hl, sl], self.cos_c[64:96, sl], ALU.mult, [("Qc", hl), "c_cos_c"], ["xn0"])
                                self.tt("dve", t2, self.pb[b][64:96, :], self.sin_c[64:96, sl], ALU.mult, [("ps", b), "c_sin_c"], ["xn1"])
                                self.tt("dve", Qc[64:96, hl, sl], t1, t2, ALU.add, ["xn0", "xn1"], [("Qc", hl)])
                yield
                for pair in range(2):
                    slot, wkey = self.wtile(wkv, 0, 2, (bi * 2 + pair) * 256, 256)
                    for hh in range(2):
                        hl = pair * 2 + hh
                        for c0 in range(0, NK, 512):
                            b = self.psb("mm")
                            self.mm(self.pb[b][0:64, :], [(slot[:, kc, hh * 128:hh * 128 + 64], latT[:, kc, c0:c0 + 512]) for kc in range(2)],
                                    [wkey, ("latT", 0), ("latT", 1)], [("ps", b)])
                            self.cp("act" if (c0 // 512) % 2 == 0 else "dve", Kc[0:64, hl, c0:c0 + 512], self.pb[b][0:64, :], [("ps", b)], [("Kc", hl)])
                    for kt in range(nkt):
                        b = self.psb("mm")
                        self.mm(self.pb[b][:, 0:128], [(latT[:, kc, kt * 128:(kt + 1) * 128], slot[:, kc, :].rearrange("p (h x) -> p h x", x=128)[:, :, 64:128]) for kc in range(2)],
                                [wkey, ("latT", 0), ("latT", 1)], [("ps", b)])
                        self.cp("act" if kt % 2 == 0 else "dve", vaugC[:, kt, pair * 2:pair * 2 + 2, 0:64], self.pb[b][:, 0:128].rearrange("p (h d) -> p h d", d=64),
                                [("ps", b)], [("vaugC", kt)])
                yield
                for hl in range(4):
                    self.cp("act" if hl % 2 == 0 else "dve", Kc[64:96, hl, 0:NK], kpeT[0:32, 0:NK], ["kpeT"], [("Kc", hl)])
                if g == 0:
                    units = [(seq * 256, 256, [seq * 2, seq * 2 + 1]) for seq in range(4)]
                else:
                    units = [(qt * 512, 512, list(range(12))) for qt in range(2)]
                for hl in range(4):
                    h = bi * 4 + hl
                    for (q0, nq, kts_) in units:
                        kblocks = [([Kc[:, hl, kt * 128:(kt + 1) * 128]], vaugC[:, kt, hl, :], None, [("Kc", hl), ("vaugC", kt), "vaugC_ones"]) for kt in kts_]

                        def dst(i, h=h, q0=q0, nq=nq):
                            return self.hm[(h % 2) * 64:(h % 2) * 64 + 64, h // 2, q0:q0 + nq]

                        def dkey(i, h=h):
                            return [("hm", h // 2)]
                        self.attn_unit([Qc[:, hl, q0:q0 + nq]], [("Qc", hl)], kblocks, scale_c, nq, self.attn_fin(dst, dkey))
                        yield


        def d_stream():
            scale_d = HD ** -0.5
            if g == 0:
                units = [(seq * 256, [seq * 2, seq * 2 + 1]) for seq in range(4)]
            else:
                units = [(qt * 256, list(range(12))) for qt in range(4)]
            for c in range(4):
                for (q0, kts_) in units:
                    qaps = [qd[i * 64:(i + 1) * 64, c, q0:q0 + 256] for i in range(2)]
                    kblocks = [([kdT[i * 64:(i + 1) * 64, c, kt * 128:(kt + 1) * 128] for i in range(2)], vaugD[:, kt, c, :], None,
                                [("kdT", c), ("vaugD", kt), "vaugD_ones"]) for kt in kts_]

                    parts = [(0, 256, self.hm[0:64, 4 + c, q0:q0 + 256], None), (256, 256, self.hm[64:128, 4 + c, q0:q0 + 256], None)]
                    self.attn_unit(qaps, [("qd", c)], kblocks, scale_d, 256, self.attn_fin_batched(parts, [("hm", 4 + c)], 256, 2), ng=2)
                    yield

        self.run_lanes([c_stream(), d_stream()], [{"mm": [0, 1, 7], "st": [2], "aux": [3]}, {"mm": [4, 5], "st": [6], "aux": [6]}])

    def build(self):
        try:
            self.build_()
        except StopBuild:
            pass
        self.S.emit(self.st)

    def build_(self):
        cfg = self.cfg
        groups = cfg.get("groups", (0, 1))
        NL = cfg.get("nl", DEPTH)
        self.setup()
        self.stage(0.25)
        self.mods(0)
        self.dbg("dbg_modv", self.modv[:], [("modv", 0, 0), ("modv", 0, 1)])
        self.stage(0.5)
        first = True
        for g in groups:
            self.load_x(g)
            self.dbg("dbg_x0", self.x[:], [("x", c, h_) for c in range(8) for h_ in range(2)])
            self.stage(0.75)
            self.modulate0(g, 0)
            self.dbg("dbg_h0", self.hm[:], [("hm", c) for c in range(8)])
            self.stage(1)
            for l in range(NL):
                side = None
                if first:
                    self.lnfold(l, (0,))
                    if l + 1 < DEPTH:
                        side = self.mods_gen(l + 1)
                self.stage(1.2)
                self.S.mark("g%d l%d mixer" % (g, l))
                if l % 2 == 0:
                    self.even_mixer(g, l)
                    mixb = self.mixb
                    mixa = self.mixa
                    srcf = lambda kc, mixb=mixb, mixa=mixa: ((mixa[:, kc, :], [("zb", 2 * kc), ("zb", 2 * kc + 1)]) if kc < 4 else (mixb[:, kc - 4, :], [("mixb", kc - 4)]))
                    self.proj_fm(self.dram["w_out"][l], 0, D, None, None, 8, self.ln_z(l, 2, g), srcf=srcf)
                else:
                    self.odd_mixer(g, l)
                    self.proj_fm(self.dram["w_out"][l], 0, D, self.hm, "hm", 8, self.ln_z(l, 2, g))
                self.S.mark("g%d l%d ln1" % (g, l))
                if l % 2 == 0:
                    self.dbg("dbg_mix_bf", self.mixa, [("zb", c) for c in range(8)])
                else:
                    self.dbg("dbg_mix_bf", self.hm[:], [("hm", c) for c in range(8)])
                if l % 2 == 0:
                    self.dbg("dbg_mixb_bf", self.mixb[:], [("mixb", c) for c in range(4)])
                self.dbg("dbg_z", self.x[:], [("x", c, h_) for c in range(8) for h_ in range(2)])
                self.stage(4)
                self.ln_finish(g, l, 0, False)
                self.dbg("dbg_xmid", self.x[:], [("x", c, h_) for c in range(8) for h_ in range(2)])
                self.dbg("dbg_hmid_bf", self.hm[:], [("hm", c) for c in range(8)])
                self.stage(5)
                self.S.mark("g%d l%d ffn" % (g, l))
                self.ffn(g, l, side)
                self.tap("tap_x_%d_%d" % (g, l))
            self.S.mark("g%d store" % g)
            self.store_x(g)
            first = False


def _build_program(cfg):
    nc = bass.Bass("TRN2", target_bir_lowering=False)
    st = ExitStack()
    kb = KB(nc, st, cfg)
    kb.build()
    st.close()
    return nc, kb


def _core_inputs(inp, b, shared):
    m = dict(shared)
    f = lambda a: np.ascontiguousarray(np.asarray(a, np.float32))
    m["xp"] = f(inp["x_prompt"][4 * b:4 * b + 4].reshape(T, D))
    m["xs"] = f(inp["x_sample"][b])
    cak = np.asarray(inp["cache_a_k"][b]).reshape(N_EVEN, PAST, 2, 64)
    m["cakd"] = f(np.concatenate([cak[:, :, 0:1], cak[:, :, 0:1], cak[:, :, 1:2], cak[:, :, 1:2]], 2).reshape(N_EVEN, PAST, 256))
    m["cav"] = f(np.asarray(inp["cache_a_v"][b]).reshape(N_EVEN, PAST, 128))
    m["stb"] = f(inp["state_b"][b])
    m["cckv"] = f(inp["cache_c_kv"][b])
    m["ccpe"] = f(inp["cache_c_pe"][b])
    cdk = np.asarray(inp["cache_d_k"][b]).reshape(N_ODD, PAST, 4, 64)
    m["cdkd"] = f(np.repeat(cdk, 2, axis=2).reshape(N_ODD, PAST, 512))
    m["cdv"] = f(np.asarray(inp["cache_d_v"][b]).reshape(N_ODD, PAST, 256))
    m["vecs"] = _pack_vecs(inp, b)
    return m


def _shared_inputs(inp):
    f = lambda a: np.ascontiguousarray(np.asarray(a, np.float32))
    s = {}
    wab = np.asarray(inp["w_in_ab"], np.float32)
    s["w_in_ab"] = f(np.concatenate([wab, wab[:, :, 512:576], wab[:, :, 512:576], wab[:, :, 576:640], wab[:, :, 576:640]], 2))
    wcd = np.asarray(inp["w_in_cd"], np.float32)
    kd = wcd[:, :, 1184:1440].reshape(N_ODD, D, 4, 64)
    s["w_in_cd"] = f(np.concatenate([wcd, np.repeat(kd, 2, axis=2).reshape(N_ODD, D, 512)], 2))
    for k in ("w_ada", "c_w_q_up", "c_w_kv_up", "w_out", "w_ffn_gate", "w_ffn_up", "w_ffn_down"):
        s[k] = f(inp[k])
    s["ckvn_b"] = f(np.broadcast_to(np.asarray(inp["c_kv_norm"], np.float32).reshape(1, N_ODD * 256), (128, N_ODD * 256)))
    for k, v in _host_consts().items():
        s[k] = np.ascontiguousarray(v) if v.dtype == np.uint8 else f(v)
    return s


def _run(inp, cfg, cores):
    nc, kb = _build_program(cfg)
    shared = _shared_inputs(inp)
    in_maps = [_core_inputs(inp, b, shared) for b in cores]
    res = run_bass_kernel_spmd(nc, in_maps, core_ids=list(range(len(cores))))
    return res.results


def kernel(**inputs):
    inp = {k: np.asarray(v) for k, v in inputs.items()}
    results = _run(inp, {}, list(range(8)))
    B, SEQ = 32, 256
    yp = np.zeros((B, SEQ, D), np.float32)
    ys = np.zeros((8, T, D), np.float32)
    nak = np.zeros((B, N_EVEN, SEQ, 2, 64), np.float32)
    nav = np.zeros((B, N_EVEN, SEQ, 2, 64), np.float32)
    nsb = np.zeros((B, N_EVEN, 2, 4, 128, 128), np.float32)
    nckv = np.zeros((B, N_ODD, SEQ, 256), np.float32)
    ncpe = np.zeros((B, N_ODD, SEQ, 32), np.float32)
    ndk = np.zeros((B, N_ODD, SEQ, 4, 64), np.float32)
    ndv = np.zeros((B, N_ODD, SEQ, 4, 64), np.float32)
    for i, r in enumerate(results):
        sl = slice(4 * i, 4 * i + 4)
        yp[sl] = r["yp"].reshape(4, SEQ, D)
        ys[i] = r["ys"]
        nak[sl] = r["nak"].reshape(4, N_EVEN, SEQ, 2, 64)
        nav[sl] = r["nav"].reshape(4, N_EVEN, SEQ, 2, 64)
        nsb[sl] = r["nsb"]
        nckv[sl] = r["nckv"]
        ncpe[sl] = r["ncpe"]
        ndk[sl] = r["ndk"].reshape(4, N_ODD, SEQ, 4, 64)
        ndv[sl] = r["ndv"].reshape(4, N_ODD, SEQ, 4, 64)
    return (yp, ys, nak, nav, nsb, nckv, ncpe, ndk, ndv)
```

```python
from contextlib import ExitStack
import math
import numpy as np
import ml_dtypes
import concourse.bass as bass
import concourse.mybir as mybir
from concourse.bass_utils import run_bass_kernel_spmd

F32 = mybir.dt.float32
BF16 = mybir.dt.bfloat16
AF = mybir.ActivationFunctionType
ALU = mybir.AluOpType

D = 1024
NCH = 8
T = 1024
DEPTH = 4
N_EVEN = 2
N_ODD = 2
PAST = 512
HD = 64
GRID_W = 64
D_FF = 2816
NFF = 22
EVEN_IN = 3328
ODD_IN = 1696
ALPHA = (2 * DEPTH) ** 0.25
LN_EPS = 1e-5 / (ALPHA * ALPHA)
RMS_EPS = 1e-6
CH = 64
NSLOT = 6
STRICT_SAME_ENGINE = False
WT = 256

ENGS = ("pe", "act", "dve", "pool", "sp")


class Op:
    __slots__ = ("eng", "fn", "deps", "signal", "target", "sem", "is_dma", "chan", "lane", "dbg")

    def __init__(self, eng, fn):
        self.eng = eng
        self.fn = fn
        self.deps = {}
        self.signal = False
        self.target = None
        self.sem = None
        self.is_dma = False
        self.chan = None
        self.lane = None
        self.dbg = None


class Sched:
    def __init__(self, nc):
        self.nc = nc
        self.q = {e: [] for e in ENGS}
        self.last_w = {}
        self.readers = {}
        self.chans = {}
        self.out_chans = set()
        self.arena_key = True
        self.lane = None
        self.lq = None
        self.session = 0

    def _push(self, eng, op, reads=(), writes=()):
        if self.lane is None:
            self.q[eng].append(op)
        else:
            op.lane = (self.session, self.lane)
            op.dbg = (tuple(reads), tuple(writes))
            self.lq[self.lane][eng].append(op)
            self.lseq[self.lane].append((eng, op))

    def lanes_begin(self, n):
        self.session += 1
        self.lq = [{e: [] for e in ENGS} for _ in range(n)]
        self.lseq = [[] for _ in range(n)]

    def lanes_merge(self):
        for lane in self.lq:
            for e in ENGS:
                for op in lane[e]:
                    for d in op.deps:
                        if d.lane is not None and d.lane[0] == op.lane[0] and d.lane[1] != op.lane[1]:
                            raise RuntimeError("cross-lane dependency: %r -> %r" % (op.dbg, d.dbg))
        seqs = self.lseq
        n = [len(q) for q in seqs]
        idx = [0] * len(seqs)
        while any(idx[i] < n[i] for i in range(len(seqs))):
            best = None
            for i in range(len(seqs)):
                if idx[i] < n[i]:
                    frac = idx[i] / float(n[i])
                    if best is None or frac < best[0]:
                        best = (frac, i)
            i = best[1]
            e, op = seqs[i][idx[i]]
            self.q[e].append(op)
            idx[i] += 1
        self.lseq = None
        self.lq = None
        self.lane = None

    def _track(self, op, reads, writes):
        for r in reads:
            w = self.last_w.get(r)
            if w is not None and w is not op:
                op.deps[w] = True
            if isinstance(r, tuple) and r[0] == "ps":
                for rd in self.readers.get(r, ()):
                    if rd is not op and rd.eng != op.eng:
                        op.deps.setdefault(rd, False)
            self.readers.setdefault(r, []).append(op)
        for r in writes:
            w = self.last_w.get(r)
            if w is not None and w is not op:
                op.deps.setdefault(w, "W")
            for rd in self.readers.get(r, ()):
                if rd is not op:
                    op.deps.setdefault(rd, False)
            self.last_w[r] = op
            self.readers[r] = []

    def add(self, eng, fn, reads=(), writes=(), arena=True):
        op = Op(eng, fn)
        reads = list(reads)
        if arena:
            reads.append("arena")
        self._track(op, reads, writes)
        self._push(eng, op, reads, writes)
        return op

    def mark(self, label):
        op = Op("pe", None)
        op.chan = ("mark", label)
        self.q["pe"].append(op)

    def fence(self, eng, fn):
        op = Op(eng, fn)
        self._track(op, [], ["arena"])
        self.q[eng].append(op)
        return op

    def dma(self, eng, fn, reads=(), writes=(), chan=None, is_out=False):
        op = Op(eng, fn)
        op.is_dma = True
        self._track(op, reads, writes)
        if chan is None:
            chan = ("chan",) + tuple(writes if writes else reads)
        op.chan = chan
        st = self.chans.setdefault(chan, [None, 0])
        st[1] += 16
        op.target = st[1]
        if is_out:
            self.out_chans.add(chan)
        self._push(eng, op, reads, writes)
        return op

    def emit(self, stack):
        nc = self.nc
        esem = {e: stack.enter_context(nc.semaphore("s_" + e)) for e in ENGS if e != "sp"}
        for i, (k, st) in enumerate(self.chans.items()):
            st[0] = stack.enter_context(nc.semaphore("c%d" % i))
        for e in ENGS:
            for op in self.q[e]:
                keep = {}
                for d, raw in op.deps.items():
                    if d.is_dma:
                        keep[d] = raw
                    elif d.eng == op.eng and not op.is_dma:
                        if op.eng == "pe":
                            continue
                        if raw or STRICT_SAME_ENGINE:
                            keep[d] = raw
                    else:
                        keep[d] = raw
                op.deps = keep
                for d in keep:
                    if not d.is_dma:
                        d.signal = True
        for e in ENGS:
            cnt = 0
            for op in self.q[e]:
                if op.is_dma:
                    op.sem = self.chans[op.chan][0]
                elif op.signal:
                    cnt += 1
                    op.target = cnt
                    op.sem = esem[e]
        block = stack.enter_context(nc.Block())
        finals = [(self.chans[c][0], self.chans[c][1]) for c in self.out_chans]

        self.marks = []

        class _Cnt:
            def __init__(self, eh):
                self.eh = eh
                self.n = 0

            def matmul(self, *a, **k):
                self.n += 1
                return self.eh.matmul(*a, **k)

            def transpose(self, *a, **k):
                self.n += 1
                return self.eh.transpose(*a, **k)

            def wait_ge(self, *a, **k):
                return self.eh.wait_ge(*a, **k)

        def run(e, eh):
            waited = {}
            if e == "pe":
                eh = _Cnt(eh)
            for op in self.q[e]:
                if op.fn is None:
                    self.marks.append((op.chan[1], eh.n))
                    continue
                need = {}
                for d in op.deps:
                    key = id(d.sem)
                    if need.get(key, (None, 0))[1] < d.target:
                        need[key] = (d.sem, d.target)
                for key, (sem, tgt) in need.items():
                    if waited.get(key, 0) >= tgt:
                        continue
                    eh.wait_ge(sem, tgt)
                    waited[key] = tgt
                ins = op.fn(eh)
                if op.is_dma:
                    ins.then_inc(op.sem, 16)
                elif op.signal:
                    ins.then_inc(op.sem, 1)
            if e == "sp":
                for sem, tgt in finals:
                    eh.wait_ge(sem, tgt)

        @block.tensor
        def _(eh):
            run("pe", eh)

        @block.scalar
        def _(eh):
            run("act", eh)

        @block.vector
        def _(eh):
            run("dve", eh)

        @block.gpsimd
        def _(eh):
            run("pool", eh)

        @block.sync
        def _(eh):
            run("sp", eh)


def _rope_tables(rot_dim):
    n_rows = T // GRID_W
    t = np.arange(T)
    row = (t // GRID_W).astype(np.float32)
    col = (t % GRID_W).astype(np.float32)
    quarter = rot_dim // 4
    inv = (10000.0 ** (-np.arange(quarter, dtype=np.float32) / quarter)).astype(np.float32)
    ar = row[None, :] * inv[:, None]
    ac = col[None, :] * inv[:, None]
    cos = np.zeros((rot_dim, T), np.float32)
    sin = np.zeros((rot_dim, T), np.float32)
    partner = np.zeros(rot_dim, np.int64)
    half = rot_dim // 2
    for base, ang in ((0, ar), (half, ac)):
        for i in range(quarter):
            d1 = base + i
            d2 = base + quarter + i
            cos[d1] = np.cos(ang[i])
            cos[d2] = np.cos(ang[i])
            sin[d1] = -np.sin(ang[i])
            sin[d2] = np.sin(ang[i])
            partner[d1] = d2
            partner[d2] = d1
    return cos, sin, partner


def _host_consts():
    c = {}
    c["ident"] = np.eye(128, dtype=np.float32)
    cos, sin, partner = _rope_tables(64)
    c["cos_hd"] = np.concatenate([cos, cos], 0)
    c["sin_hd"] = np.concatenate([sin, sin], 0)
    pm = np.zeros((128, 128), np.float32)
    for hh in range(2):
        for m in range(64):
            pm[hh * 64 + partner[m], hh * 64 + m] = 1.0
    c["perm_hd"] = pm
    cos, sin, partner = _rope_tables(32)
    cc = np.zeros((128, T), np.float32)
    sc = np.zeros((128, T), np.float32)
    pc = np.zeros((128, 128), np.float32)
    for b in (0, 64):
        cc[b:b + 32] = cos
        sc[b:b + 32] = sin
        for m in range(32):
            pc[b + partner[m], b + m] = 1.0
    c["cos_c"] = cc
    c["sin_c"] = sc
    c["perm_c"] = pc
    k = np.arange(128)[:, None]
    q = np.arange(128)[None, :]
    c["m_prev"] = (k >= q).astype(np.float32)
    c["m_next"] = (k <= q).astype(np.float32)
    s = np.arange(CH)[:, None]
    tt = np.arange(CH)[None, :]
    hm = np.zeros((128, 2 * 128), np.float32)
    for b in (0, 64):
        hm[b:b + CH, b:b + CH] = (s <= tt)
        hm[b:b + CH, 128 + b:128 + b + CH] = (s >= tt)
    c["hmask"] = hm.astype(np.uint8)
    bd = np.zeros((128, 128), np.float32)
    bd[0:64, 0:64] = 1.0
    bd[64:128, 64:128] = 1.0
    c["blockdiag"] = bd
    return c


def _fm(v):
    v = np.asarray(v, np.float32)
    return np.ascontiguousarray(v.reshape(-1, 128).T)


VC = {}
_off = 0
for _name, _n in (("ln_g", 64), ("ln_b", 64), ("b_lb", 16), ("gnorm", 2), ("cqn", 6), ("dqn", 2), ("dkn", 2),
                  ("sink", 16), ("b_ada", 192), ("cond", 16)):
    VC[_name] = _off
    _off += _n
NVEC = _off


def _pack_vecs(inp, b):
    v = np.zeros((128, NVEC), np.float32)
    for l in range(DEPTH):
        for j in range(2):
            o = (l * 2 + j) * 8
            v[:, VC["ln_g"] + o:VC["ln_g"] + o + 8] = _fm(inp["ln_g"][l, j])
            v[:, VC["ln_b"] + o:VC["ln_b"] + o + 8] = _fm(inp["ln_b"][l, j])
        v[:, VC["b_ada"] + l * 48:VC["b_ada"] + (l + 1) * 48] = _fm(inp["b_ada"][l])
    for e in range(N_EVEN):
        for d in range(2):
            o = (e * 2 + d) * 4
            v[:, VC["b_lb"] + o:VC["b_lb"] + o + 4] = _fm(inp["b_lb"][e, d])
        v[:, VC["gnorm"] + e] = inp["b_gnorm"][e]
        v[:, VC["sink"] + e * 8:VC["sink"] + e * 8 + 8] = np.broadcast_to(inp["a_sink"][e][None, :], (128, 8))
    for o in range(N_ODD):
        v[:, VC["cqn"] + o * 3:VC["cqn"] + o * 3 + 3] = _fm(inp["c_q_norm"][o])
        v[:, VC["dqn"] + o] = np.tile(inp["d_q_norm"][o], 2)
        v[:, VC["dkn"] + o] = np.tile(inp["d_k_norm"][o], 2)
    cond2 = np.stack([inp["c_ctx"], inp["c"][b]], 0)
    for c2 in range(2):
        v[:, VC["cond"] + c2 * 8:VC["cond"] + c2 * 8 + 8] = _fm(cond2[c2])
    return v


class StopBuild(Exception):
    pass


class KB:
    def stage(self, n):
        if self.cfg.get("stage", 99) <= n:
            raise StopBuild()

    def dbg(self, name, ap, keys):
        if name in self.cfg.get("taps", {}):
            self.dma("sp", self.dram[name], ap, list(keys) + ["arena"], [], is_out=True, chan=("och", name))

    def __init__(self, nc, st, cfg):
        self.nc = nc
        self.st = st
        self.cfg = cfg
        self.S = Sched(nc)
        self.rr = {}
        self.wcount = 0
        self.dram = {}
        self._decl_dram()
        self._alloc()

    def din(self, name, shape, dt=F32):
        self.dram[name] = self.nc.dram_tensor(name, list(shape), dt, kind="ExternalInput").ap()
        return self.dram[name]

    def dout(self, name, shape):
        self.dram[name] = self.nc.dram_tensor(name, list(shape), F32, kind="ExternalOutput").ap()
        return self.dram[name]

    def _decl_dram(self):
        d = self.din
        d("xp", [T, D]); d("xs", [T, D])
        d("cakd", [N_EVEN, PAST, 256]); d("cav", [N_EVEN, PAST, 128])
        d("stb", [N_EVEN, 2, 4, 128, 128])
        d("cckv", [N_ODD, PAST, 256]); d("ccpe", [N_ODD, PAST, 32])
        d("cdkd", [N_ODD, PAST, 512]); d("cdv", [N_ODD, PAST, 256])
        d("w_ada", [DEPTH, D, 6 * D])
        d("w_in_ab", [N_EVEN, D, EVEN_IN + 256]); d("w_in_cd", [N_ODD, D, ODD_IN + 512])
        d("c_w_q_up", [N_ODD, 384, 768]); d("c_w_kv_up", [N_ODD, 256, 1024])
        d("w_out", [DEPTH, D, D])
        d("w_ffn_gate", [DEPTH, D, D_FF]); d("w_ffn_up", [DEPTH, D, D_FF]); d("w_ffn_down", [DEPTH, D_FF, D])
        d("vecs", [128, NVEC]); d("ckvn_b", [128, N_ODD * 256])
        for k, v in _host_consts().items():
            d(k, v.shape, mybir.dt.uint8 if v.dtype == np.uint8 else F32)
        o = self.dout
        o("yp", [T, D]); o("ys", [T, D])
        o("nak", [4, N_EVEN, 256, 128]); o("nav", [4, N_EVEN, 256, 128])
        o("nsb", [4, N_EVEN, 2, 4, 128, 128])
        o("nckv", [4, N_ODD, 256, 256]); o("ncpe", [4, N_ODD, 256, 32])
        o("ndk", [4, N_ODD, 256, 256]); o("ndv", [4, N_ODD, 256, 256])
        for name, shape in self.cfg.get("taps", {}).items():
            if name.endswith("_bf"):
                self.dram[name] = self.nc.dram_tensor(name, list(shape), BF16, kind="ExternalOutput").ap()
            else:
                o(name, shape)

    def sb(self, name, shape, dt):
        return self.st.enter_context(self.nc.sbuf_tensor("s_" + name, list(shape), dt))

    def _alloc(self):
        sb = self.sb
        self.x = sb("xres", [128, NCH, T], F32)
        self.hm = sb("hm", [128, NCH, T], BF16)
        self.wsl = [sb("wsl%d" % i, [128, 8, WT], BF16) for i in range(NSLOT)]
        self.ident = sb("ident", [128, 128], F32)
        self.identb = sb("identb", [128, 128], BF16)
        self.ones = sb("ones", [128, 128], BF16)
        self.bd = sb("bd", [128, 128], BF16)
        self.perm_hd = sb("perm_hd", [128, 128], BF16)
        self.perm_c = sb("perm_c", [128, 128], BF16)
        self.m_prev = sb("m_prev", [128, 128], BF16)
        self.m_next = sb("m_next", [128, 128], BF16)
        self.hmask = sb("hmask", [128, 2, 128], mybir.dt.uint8)
        self.epsc = sb("epsc", [128, 3], F32)
        self.cos_hd = sb("cos_hd", [128, T], BF16)
        self.sin_hd = sb("sin_hd", [128, T], BF16)
        self.cos_c = sb("cos_c", [128, T], BF16)
        self.sin_c = sb("sin_c", [128, T], BF16)
        self.scanmask = sb("scanmask", [128, 512], F32)
        self.vecs = sb("vecs", [128, NVEC], F32)
        self.ckvn_b = sb("ckvn_b", [128, N_ODD * 256], F32)
        self.modv = sb("modv", [128, DEPTH, 48, 2], F32)
        self.lnf = sb("lnf", [128, DEPTH, 2, 2, 8, 2], F32)
        self.lb = sb("lbv", [128, N_EVEN, 2, 2, 4], F32)
        self.esink = sb("esink", [128, 16], F32)
        self.esinkU = sb("esinkU", [128, 2, 2, 4], F32)
        self.scond = sb("scond", [128, 8, 2], BF16)
        self.zb = sb("zb", [128, 8, 512], BF16)
        self.zsq = sb("zsq", [128, 8, 512], BF16)
        self.stg = [self.zb[:, 0:4, :].bitcast(F32).rearrange("p a b -> p (a b)"),
                    self.zb[:, 4:8, :].bitcast(F32).rearrange("p a b -> p (a b)")]
        self.mean = sb("mean", [128, 512], F32)
        self.rstd = sb("rstd", [128, 512], F32)
        self.tA = sb("tA", [128, 512], F32)
        self.xn = [sb("xn%d" % i, [128, 512], F32) for i in range(2)]
        self.ptb = sb("ptb", [128, 4, 512], BF16)
        self.rd = sb("rd", [128, 2, 512], F32)
        self.cst = sb("cst", [128, 2, 288], F32)
        self.dummy = sb("dummy", [128, 8], F32)
        self.ARENA = 20224
        self.arena = sb("arena", [128, self.ARENA], F32)
        self.pb = [self.st.enter_context(self.nc.psum_tensor("pb%d" % i, [128, 512], F32)) for i in range(8)]

    def carve_reset(self):
        self.aoff = 0

    def carve(self, shape, dt):
        n = int(np.prod(shape))
        words = n if dt == F32 else (n + 1) // 2
        a = self.arena[:, self.aoff:self.aoff + words]
        self.aoff += words
        assert self.aoff <= self.ARENA, ("arena overflow", self.aoff)
        if dt != F32:
            a = a.bitcast(dt)
        if len(shape) == 2:
            a = a.rearrange("p (a b) -> p a b", b=shape[1])
        elif len(shape) == 3:
            a = a.rearrange("p (a b c) -> p a b c", b=shape[1], c=shape[2])
        return a

    def fence(self):
        dm = self.dummy
        self.S.fence("dve", lambda e: e.memset(dm[:, 0:1], 0.0))

    def rot(self, pool, n):
        i = self.rr.get(pool, 0)
        self.rr[pool] = i + 1
        return i % n

    def psb(self, pool):
        lb = getattr(self, "lanebanks", None)
        ln = self.S.lane
        if lb is not None and ln is not None:
            banks = lb[ln][pool]
            return banks[self.rot("%s_l%d" % (pool, ln), len(banks))]
        if pool == "mm":
            return self.rot("mm", 4)
        if pool == "st":
            return 4 + self.rot("st", 2)
        return 6 + self.rot("aux", 2)

    def ptslot(self):
        ln = self.S.lane
        if ln is None:
            return self.rot("ptb", 4)
        return 2 * ln + self.rot("ptb_l%d" % ln, 2)

    def rdslot(self):
        ln = self.S.lane
        if ln is None:
            return self.rot("rdb", 2)
        return ln

    def run_lanes(self, gens, banks, slots=None):
        self.lanebanks = banks
        self.laneslots = slots
        self.S.lanes_begin(len(gens))
        for i, gen in enumerate(gens):
            self.S.lane = i
            for _ in gen:
                pass
        self.S.lane = None
        self.S.lanes_merge()
        self.lanebanks = None
        self.laneslots = None

    def mm(self, out, ops, R, W):
        n = len(ops)

        def fn(e):
            ins = None
            for i, (l, r) in enumerate(ops):
                ins = e.matmul(out, l, r, start=(i == 0), stop=(i == n - 1))
            return ins
        return self.S.add("pe", fn, R, W)

    def tr(self, out, in_, ident, R, W):
        return self.S.add("pe", lambda e: e.transpose(out, in_, ident), R, W)

    def act(self, out, in_, func, R, W, bias=0.0, scale=1.0, eng="act"):
        return self.S.add(eng, lambda e: e.activation(out=out, in_=in_, func=func, bias=bias, scale=scale), R, W)

    def tt(self, eng, out, in0, in1, op, R, W):
        return self.S.add(eng, lambda e: e.tensor_tensor(out=out, in0=in0, in1=in1, op=op), R, W)

    def ts(self, eng, out, in0, s1, s2, op0, op1, R, W):
        if s2 is None:
            return self.S.add(eng, lambda e: e.tensor_scalar(out=out, in0=in0, scalar1=s1, scalar2=None, op0=op0), R, W)
        return self.S.add(eng, lambda e: e.tensor_scalar(out=out, in0=in0, scalar1=s1, scalar2=s2, op0=op0, op1=op1), R, W)

    def stt(self, eng, out, in0, scalar, in1, op0, op1, R, W):
        return self.S.add(eng, lambda e: e.scalar_tensor_tensor(out=out, in0=in0, scalar=scalar, in1=in1, op0=op0, op1=op1), R, W)

    def cp(self, eng, out, in_, R, W):
        if eng == "act":
            return self.S.add(eng, lambda e: e.copy(out=out, in_=in_), R, W)
        return self.S.add(eng, lambda e: e.tensor_copy(out=out, in_=in_), R, W)

    def dma(self, eng, out, in_, R, W, is_out=False, chan=None, slow=False):
        if slow:
            return self.S.dma(eng, lambda e: e.dma_start(out=out, in_=in_, allow_slow_non_contiguous=True), R, W, chan=chan, is_out=is_out)
        return self.S.dma(eng, lambda e: e.dma_start(out=out, in_=in_), R, W, chan=chan, is_out=is_out)

    def wtile(self, src2d, r0, nk, c0, ncols):
        ln = self.S.lane
        ls = getattr(self, "laneslots", None)
        if ln is None or ls is None:
            i = self.wcount % NSLOT
            self.wcount += 1
        else:
            i = ls[ln][self.rot("w_l%d" % ln, len(ls[ln]))]
        slot = self.wsl[i]
        key = ("w", i)
        src = src2d[r0:r0 + nk * 128, c0:c0 + ncols].rearrange("(kc p) n -> p kc n", p=128)
        self.dma("pool", slot[:, 0:nk, 0:ncols], src, [], [key], chan=("wch", i))
        return slot, key

    def proj_fm(self, w2d, c0, ncols, src, srckey, nk, handler, halves=(0, 1), nmm=512, srcf=None):
        col = c0
        oc_i = 0
        while col < c0 + ncols:
            w = min(WT, c0 + ncols - col)
            slot, wkey = self.wtile(w2d, 0, nk, col, w)
            for j in range(w // 128):
                for th in halves:
                    b = self.psb("mm")
                    out = self.pb[b][:, 0:nmm]
                    if srcf is None:
                        ops = [(slot[:, kc, j * 128:(j + 1) * 128], src[:, kc, th * nmm:(th + 1) * nmm]) for kc in range(nk)]
                        sk_ = [(srckey, kc) for kc in range(nk)]
                    else:
                        ops = [(slot[:, kc, j * 128:(j + 1) * 128], srcf(kc)[0][:, th * nmm:(th + 1) * nmm]) for kc in range(nk)]
                        sk_ = [k_ for kc in range(nk) for k_ in srcf(kc)[1]]
                    self.mm(out, ops, [wkey] + sk_, [("ps", b)])
                    handler(oc_i, th, out, ("ps", b))
                oc_i += 1
            col += w

    def setup(self):
        dr = self.dram
        self.dma("sp", self.ident[:], dr["ident"], [], ["ident"])
        self.dma("sp", self.vecs[:], dr["vecs"], [], ["vecs"])
        self.dma("act", self.ckvn_b[:], dr["ckvn_b"], [], ["ckvn_b"])
        for name, t in (("ident", self.identb), ("blockdiag", self.bd), ("perm_hd", self.perm_hd), ("perm_c", self.perm_c),
                        ("m_prev", self.m_prev), ("m_next", self.m_next), ("cos_hd", self.cos_hd),
                        ("sin_hd", self.sin_hd), ("cos_c", self.cos_c), ("sin_c", self.sin_c)):
            self.dma("pool", t[:], dr[name], [], ["c_" + name], chan=("cch", name))
        self.dma("act", self.hmask[:].rearrange("p a b -> p (a b)"), dr["hmask"], [], ["c_hmask"])
        ones, sm = self.ones, self.scanmask
        self.S.add("dve", lambda e: e.memset(ones[:], 1.0), [], ["ones"], arena=False)
        epsc = self.epsc
        self.S.add("dve", lambda e: e.memset(epsc[:, 0:1], LN_EPS), [], ["epsc0"], arena=False)
        self.S.add("dve", lambda e: e.memset(epsc[:, 1:2], RMS_EPS), [], ["epsc1"], arena=False)
        self.S.add("dve", lambda e: e.memset(epsc[:, 2:3], 1.0), [], ["epsc2"], arena=False)
        self.S.add("dve", lambda e: e.memset(sm[:], 1.0), [], ["scanmask"], arena=False)
        self.S.add("dve", lambda e: e.memset(sm[:].rearrange("p (n c) -> p n c", c=CH)[:, :, 0:1], 0.0), [], ["scanmask", "scanmask2"], arena=False)
        v = self.vecs
        c0 = VC["cond"]
        self.act(self.scond[:], v[:, c0:c0 + 16].rearrange("p (c k) -> p k c", c=2), AF.Silu, ["vecs"], ["scond"])
        self.act(self.esink[:], v[:, VC["sink"]:VC["sink"] + 16], AF.Exp, ["vecs"], ["esink"])
        es4 = self.esink[:].rearrange("p (e k i) -> p e k i", e=2, k=2)
        for pos, idx in enumerate((0, 2, 1, 3)):
            self.cp("dve", self.esinkU[:, :, :, pos], es4[:, :, :, idx], ["esink"], ["esinkU"])
        lb = self.lb
        self.S.add("dve", lambda e: e.memset(lb[:, 0, :, 0, :], 0.0), [], ["lb0a"], arena=False)
        self.S.add("dve", lambda e: e.memset(lb[:, 0, :, 1, :], 1.0), [], ["lb0b"], arena=False)
        b0 = VC["b_lb"]
        dtmp = self.dummy
        self.tt("dve", dtmp[:, 0:8], v[:, b0 + 8:b0 + 16], v[:, b0:b0 + 8], ALU.subtract, ["vecs"], ["dummy"])
        self.act(lb[:, 1, :, 0, :], dtmp[:, 0:8].rearrange("p (d h) -> p d h", d=2), AF.Sigmoid, ["dummy"], ["lb1a"])
        self.ts("dve", lb[:, 1, :, 1, :], lb[:, 1, :, 0, :], -1.0, 1.0, ALU.mult, ALU.add, ["lb1a"], ["lb1b"])
        self.lbkeys = ["lb0a", "lb0b", "lb1a", "lb1b"]

    def mods(self, l):
        for _ in self.mods_gen(l):
            pass

    def mods_gen(self, l):
        w2d = self.dram["w_ada"][l]
        b = self.psb("aux")
        acc = self.pb[b]
        for t in range(24):
            slot, wkey = self.wtile(w2d, 0, 8, t * WT, WT)
            for j in range(2):
                oc = t * 2 + j
                ops = [(slot[:, kc, j * 128:(j + 1) * 128], self.scond[:, kc, :]) for kc in range(8)]
                self.mm(acc[:, oc * 2:oc * 2 + 2], ops, [wkey, "scond"], [("ps", b)])
            yield
        v = self.vecs
        bcol = VC["b_ada"] + l * 48
        for c in range(2):
            src = acc[:, 0:96].rearrange("p (o c) -> p o c", c=2)[:, :, c]
            self.tt("dve", self.modv[:, l, :, c], src, v[:, bcol:bcol + 48], ALU.add, [("ps", b), "vecs"], [("modv", l, c)])
        mk = [("modv", l, 0), ("modv", l, 1)]
        for which in (1, 4):
            m = self.modv[:, l, which * 8:(which + 1) * 8, :]
            self.ts("dve", m, m, 1.0, None, ALU.add, None, mk, mk)
        for which in (2, 5):
            m = self.modv[:, l, which * 8:(which + 1) * 8, :]
            self.ts("dve", m, m, 1.0 / ALPHA, None, ALU.mult, None, mk, mk)

    def lnfold(self, l, whichs=(0, 1)):
        v = self.vecs
        for which in whichs:
            if which == 0:
                ls, sc_i, sh_i = l, 4, 3
            else:
                if l == DEPTH - 1:
                    continue
                ls, sc_i, sh_i = l + 1, 1, 0
            gcol = VC["ln_g"] + (l * 2 + which) * 8
            bcol = VC["ln_b"] + (l * 2 + which) * 8
            for c in range(2):
                sc = self.modv[:, ls, sc_i * 8:(sc_i + 1) * 8, c]
                sh = self.modv[:, ls, sh_i * 8:(sh_i + 1) * 8, c]
                G = self.lnf[:, l, which, 0, :, c]
                B = self.lnf[:, l, which, 1, :, c]
                R = [("modv", ls, c), "vecs"]
                self.tt("dve", G, sc, v[:, gcol:gcol + 8], ALU.mult, R, [("lnf", l, which, c, 0)])
                self.tt("dve", B, sc, v[:, bcol:bcol + 8], ALU.mult, R, [("lnf", l, which, c, 1)])
                self.tt("dve", B, B, sh, ALU.add, R + [("lnf", l, which, c, 1)], [("lnf", l, which, c, 1)])

    def stgkeys(self, i):
        return [("zb", k) for k in range(i * 4, i * 4 + 4)]

    def load_x(self, g):
        xd = self.dram["xp" if g == 0 else "xs"]
        for tt in range(8):
            stg = self.stg[tt % 2]
            sk = self.stgkeys(tt % 2)
            self.dma("sp" if tt % 2 == 0 else "act", stg, xd[tt * 128:(tt + 1) * 128, :], [], sk)
            for half in range(2):
                b = self.psb("mm")
                bank = self.pb[b]
                ident = self.ident

                def fn(e, bank=bank, stg=stg, half=half, ident=ident):
                    ins = None
                    for c4 in range(4):
                        c = half * 4 + c4
                        ins = e.transpose(bank[:, c4 * 128:(c4 + 1) * 128], stg[:, c * 128:(c + 1) * 128], ident[:])
                    return ins
                self.S.add("pe", fn, sk + ["ident"], [("ps", b)])
                dst = self.x[:, half * 4:(half + 1) * 4, tt * 128:(tt + 1) * 128]
                self.cp("act" if half == 0 else "dve", dst, bank[:].rearrange("p (a b) -> p a b", b=128),
                        [("ps", b)], [("x", c, tt // 4) for c in range(half * 4, half * 4 + 4)])

    def modulate0(self, g, l):
        for c in range(8):
            sc = self.modv[:, l, 8 + c, g:g + 1]
            sh = self.modv[:, l, c, g:g + 1]
            R = [("x", c, 0), ("x", c, 1), ("modv", l, g)]
            if (c % 2 == 0 or self.cfg.get("mod_act", False)) and not self.cfg.get("mod_dve", False):
                self.act(self.hm[:, c, :], self.x[:, c, :], AF.Identity, R, [("hm", c)], bias=sh, scale=sc)
            else:
                self.ts("dve", self.hm[:, c, :], self.x[:, c, :], sc, sh, ALU.mult, ALU.add, R, [("hm", c)])

    def store_x(self, g):
        yd = self.dram["yp" if g == 0 else "ys"]
        for tt in range(8):
            stg = self.stg[tt % 2]
            sk = self.stgkeys(tt % 2)
            for half in range(2):
                b = self.psb("mm")
                bank = self.pb[b]
                x, ident = self.x, self.ident

                def fn(e, bank=bank, half=half, tt=tt, x=x, ident=ident):
                    ins = None
                    for c4 in range(4):
                        c = half * 4 + c4
                        ins = e.transpose(bank[:, c4 * 128:(c4 + 1) * 128], x[:, c, tt * 128:(tt + 1) * 128], ident[:])
                    return ins
                self.S.add("pe", fn, [("x", c, tt // 4) for c in range(half * 4, half * 4 + 4)] + ["ident"], [("ps", b)])
                self.cp("act" if half == 0 else "dve", stg[:, half * 512:(half + 1) * 512], bank[:], [("ps", b)], sk)
            self.dma("sp", yd[tt * 128:(tt + 1) * 128, :], stg, sk, [], is_out=True, chan=("och", "y", tt % 2))

    def tap(self, name):
        if name in self.cfg.get("taps", {}):
            self.dma("sp", self.dram[name], self.x[:], [("x", c, h_) for c in range(8) for h_ in range(2)], [], is_out=True, chan=("och", name))

    def ln_z(self, l, gate_i, g):
        def h(oc, th, ps, pskey):
            xs = self.x[:, oc, th * 512:(th + 1) * 512]
            gc = self.modv[:, l, gate_i * 8 + oc, g:g + 1]
            self.stt("dve", xs, ps, gc, xs, ALU.mult, ALU.add, [pskey, ("x", oc, th), ("modv", l, g)], [("x", oc, th)])
        return h

    def ln_finish(self, g, l, which, last):
        for th in range(2):
            self.ln_a(th)
            self.ln_b(g, l, which, last, th)

    def ln_a(self, th):
        sl = slice(th * 512, (th + 1) * 512)
        for oc in range(8):
            xs = self.x[:, oc, sl]
            self.cp("act" if oc % 2 == 0 else "dve", self.zb[:, oc, :], xs, [("x", oc, th)], [("zb", oc)])
            if oc % 2 == 1:
                self.act(self.zsq[:, oc, :], xs, AF.Square, [("x", oc, th)], [("zsq", oc)])
            else:
                self.tt("dve", self.zsq[:, oc, :], xs, xs, ALU.mult, [("x", oc, th)], [("zsq", oc)])

    def ln_b(self, g, l, which, last, th):
        v = self.vecs
        sl = slice(th * 512, (th + 1) * 512)
        b1 = self.psb("st")
        b2 = self.psb("st")
        self.mm(self.pb[b1][:], [(self.ones[:], self.zb[:, oc, :]) for oc in range(8)], ["ones"] + [("zb", oc) for oc in range(8)], [("ps", b1)])
        self.mm(self.pb[b2][:], [(self.ones[:], self.zsq[:, oc, :]) for oc in range(8)], ["ones"] + [("zsq", oc) for oc in range(8)], [("ps", b2)])
        mean, rstd, tA, tB = self.mean, self.rstd, self.xn[0], self.xn[1]
        self.ts("dve", mean[:], self.pb[b1][:], 1.0 / D, None, ALU.mult, None, [("ps", b1)], ["mean"])
        self.tt("dve", tA[:], mean[:], mean[:], ALU.mult, ["mean"], ["xn0"])
        self.stt("dve", tB[:], self.pb[b2][:], 1.0 / D, tA[:], ALU.mult, ALU.subtract, [("ps", b2), "xn0"], ["xn1"])
        self.rsqrt(rstd[:], tB[:], 1.0, 0, ["xn1"], ["rstd"])
        gcol = VC["ln_g"] + (l * 2 + which) * 8
        bcol = VC["ln_b"] + (l * 2 + which) * 8
        for oc in range(8):
            xs = self.x[:, oc, sl]
            xn = self.xn[oc % 2]
            xk = "xn%d" % (oc % 2)
            self.tt("dve", xn[:], xs, mean[:], ALU.subtract, [("x", oc, th), "mean"], [xk])
            self.tt("dve", xn[:], xn[:], rstd[:], ALU.mult, [xk, "rstd"], [xk])
            self.act(xs, xn[:], AF.Identity, [xk, "vecs"], [("x", oc, th)], bias=v[:, bcol + oc:bcol + oc + 1], scale=v[:, gcol + oc:gcol + oc + 1])
            if not last:
                G = self.lnf[:, l, which, 0, oc, g:g + 1]
                B = self.lnf[:, l, which, 1, oc, g:g + 1]
                self.act(self.hm[:, oc, sl], xn[:], AF.Identity, [xk, ("lnf", l, which, g, 0), ("lnf", l, which, g, 1)], [("hm", oc)], bias=B, scale=G)

    def rsqrt(self, out, in_, scale, eps_i, R, W):
        self.act(out, in_, AF.Ln, list(R) + ["epsc0", "epsc1"], W, bias=self.epsc[0:out.shape[0], eps_i:eps_i + 1], scale=scale)
        self.act(out, out, AF.Exp, W, W, scale=-0.5)

    def out_proj(self, g, l):
        self.proj_fm(self.dram["w_out"][l], 0, D, self.hm, "hm", 8, self.ln_z(l, 2, g))
        self.ln_finish(g, l, 0, False)

    def ffn(self, g, l, side=None):
        self.fence()
        self.carve_reset()
        hid = self.carve([NFF, T], BF16)
        wg, wu, wd = self.dram["w_ffn_gate"][l], self.dram["w_ffn_up"][l], self.dram["w_ffn_down"][l]
        hk = [("hm", kc) for kc in range(8)]
        for t in range(D_FF // WT):
            sg_, kg = self.wtile(wg, 0, 8, t * WT, WT)
            su_, ku = self.wtile(wu, 0, 8, t * WT, WT)
            for j in range(2):
                fc = t * 2 + j
                for th in range(2):
                    sl = slice(th * 512, (th + 1) * 512)
                    bg = self.psb("mm")
                    bu = self.psb("mm")
                    self.mm(self.pb[bg][:], [(sg_[:, kc, j * 128:(j + 1) * 128], self.hm[:, kc, sl]) for kc in range(8)], [kg] + hk, [("ps", bg)])
                    self.mm(self.pb[bu][:], [(su_[:, kc, j * 128:(j + 1) * 128], self.hm[:, kc, sl]) for kc in range(8)], [ku] + hk, [("ps", bu)])
                    i = self.rot("ffnt", 2)
                    tmp = self.rd[:, i, :]
                    self.act(tmp, self.pb[bg][:], AF.Silu, [("ps", bg)], [("rd", i)])
                    self.tt("dve", hid[:, fc, sl], tmp, self.pb[bu][:], ALU.mult, [("rd", i), ("ps", bu)], [("hid", fc)])
            if side is not None:
                for _ in range(3):
                    next(side, None)
        if side is not None:
            for _ in side:
                pass
            self.lnfold(l, (1,))
        self.S.mark("g%d l%d down" % (g, l))
        hz = self.ln_z(l, 5, g)
        pieces = ((0, 8), (8, 8), (16, 6))
        last = (l == DEPTH - 1)

        def down_quarter(q, th):
            sl = slice(th * 512, (th + 1) * 512)
            slots = [self.wtile(wd, k0 * 128, nk, q * WT, WT) for (k0, nk) in pieces]
            for j in range(2):
                oc = q * 2 + j
                b = self.psb("mm")
                ops = []
                for (slot, _), (k0, nk) in zip(slots, pieces):
                    for kc in range(nk):
                        ops.append((slot[:, kc, j * 128:(j + 1) * 128], hid[:, k0 + kc, sl]))
                self.mm(self.pb[b][:], ops, [k for _, k in slots] + [("hid", fc) for fc in range(NFF)], [("ps", b)])
                hz(oc, th, self.pb[b][:], ("ps", b))
        for q in range(4):
            down_quarter(q, 0)
        self.ln_a(0)
        down_quarter(0, 1)
        self.S.mark("g%d l%d ln2" % (g, l))
        self.ln_b(g, l, 1, last, 0)
        for q in range(1, 4):
            down_quarter(q, 1)
        self.ln_a(1)
        self.ln_b(g, l, 1, last, 1)
        self.fence()

    def attn_unit(self, qaps, qkeys, kblocks, scale, nq, fin, ng=1):
        hp = len(qaps)
        gs = hp // ng
        ob = self.psb("st")
        O = self.pb[ob]
        nkb = len(kblocks)

        def stage_a(bi):
            kts, vap, mask, kkeys = kblocks[bi]
            pi = self.ptslot()
            pt = self.ptb[:, pi, 0:hp * nq]
            for gi in range(ng):
                b = self.psb("mm")
                Sb = self.pb[b]

                def fn(e, Sb=Sb, kts=kts, gi=gi):
                    ins = None
                    for j in range(gs):
                        i = gi * gs + j
                        ins = e.matmul(Sb[:, j * nq:(j + 1) * nq], kts[i], qaps[i], start=True, stop=True)
                    return ins
                self.S.add("pe", fn, list(qkeys) + list(kkeys), [("ps", b)])
                self.act(pt[:, gi * gs * nq:(gi + 1) * gs * nq], Sb[:, 0:gs * nq], AF.Exp, [("ps", b)], [("ptb", pi, gi)], scale=scale)
            pkeys = [("ptb", pi, gi) for gi in range(ng)]
            if mask is not None:
                p3 = pt.rearrange("p (h q) -> p h q", q=nq)
                m3 = mask.unsqueeze(1).to_broadcast([128, hp, nq])
                self.tt("dve", p3, p3, m3, ALU.mult, pkeys + ["c_m_prev", "c_m_next"], pkeys)
            return pt, pkeys

        def stage_b(bi, pt, pkeys):
            kts, vap, mask, kkeys = kblocks[bi]

            def fn2(e, pt=pt, vap=vap, first=(bi == 0), lastb=(bi == nkb - 1)):
                ins = None
                for i in range(hp):
                    ins = e.matmul(O[:, i * nq:(i + 1) * nq], vap, pt[:, i * nq:(i + 1) * nq], start=(first and i == 0), stop=lastb,
                                   skip_group_check=True)
                return ins
            self.S.add("pe", fn2, pkeys + list(kkeys), [("ps", ob)])

        cur = stage_a(0)
        for bi in range(nkb):
            nxt = stage_a(bi + 1) if bi + 1 < nkb else None
            stage_b(bi, *cur)
            cur = nxt
        if getattr(fin, "batched", False):
            fin(O[:, 0:hp * nq], ("ps", ob))
        else:
            for i in range(hp):
                fin(i, O[:, i * nq:(i + 1) * nq], ("ps", ob))

    def attn_fin_batched(self, parts, dkeys, nq, hp, sink_b=None):
        def fin(O, okey):
            ri = self.rdslot()
            rd = self.rd[64:128, ri, 0:hp * nq]
            if sink_b is not None:
                self.tt("dve", rd.rearrange("p (h q) -> p h q", q=nq), O[64:128, :].rearrange("p (h q) -> p h q", q=nq), sink_b, ALU.add,
                        [okey, "esinkU"], [("rd", ri)])
                self.act(rd, rd, AF.Ln, [("rd", ri)], [("rd", ri)])
            else:
                self.act(rd, O[64:128, :], AF.Ln, [okey], [("rd", ri)])
            self.act(rd, rd, AF.Exp, [("rd", ri)], [("rd", ri)], scale=-1.0)
            for (c0, ncols, out_ap, shp) in parts:
                i0 = O[0:64, c0:c0 + ncols]
                i1 = self.rd[64:128, ri, c0:c0 + ncols]
                if shp is not None:
                    i0 = i0.rearrange("p (a q) -> p a q", q=shp)
                    i1 = i1.rearrange("p (a q) -> p a q", q=shp)
                self.tt("dve", out_ap, i0, i1, ALU.mult, [okey, ("rd", ri)], dkeys)
        fin.batched = True
        return fin

    def attn_fin(self, dst, dkey, sink_ap=None):
        def fin(i, O, okey):
            nq = O.shape[-1]
            ri = self.rdslot()
            rd = self.rd[64:128, ri, 0:nq]
            if sink_ap is not None:
                self.ts("dve", rd, O[64:128, :], sink_ap(i), None, ALU.add, None, [okey, "esink"], [("rd", ri)])
                self.act(rd, rd, AF.Ln, [("rd", ri)], [("rd", ri)])
            else:
                self.act(rd, O[64:128, :], AF.Ln, [okey], [("rd", ri)])
            self.act(rd, rd, AF.Exp, [("rd", ri)], [("rd", ri)], scale=-1.0)
            self.tt("dve", dst(i), O[0:64, :], rd, ALU.mult, [okey, ("rd", ri)], dkey(i))
        return fin

    def rope(self, xap, xkey, perm, cos, sin, n, rows=128):
        for c0 in range(0, n, 512):
            w = min(512, n - c0)
            b = self.psb("aux")
            ps = self.pb[b][0:rows, 0:w]
            xs = xap[:, c0:c0 + w]
            self.mm(ps, [(perm, xs)], list(xkey) + ["c_perm_hd", "c_perm_c"], [("ps", b)])
            t1 = self.xn[0][0:rows, 0:w]
            t2 = self.xn[1][0:rows, 0:w]
            self.tt("dve", t1, xs, cos[:, c0:c0 + w], ALU.mult, list(xkey) + ["c_cos_hd", "c_cos_c"], ["xn0"])
            self.tt("dve", t2, ps, sin[:, c0:c0 + w], ALU.mult, [("ps", b), "c_sin_hd", "c_sin_c"], ["xn1"])
            self.tt("dve", xs, t1, t2, ALU.add, ["xn0", "xn1"], list(xkey))

    def even_mixer(self, g, l):
        e = l // 2
        dr = self.dram
        w2d = dr["w_in_ab"][e]
        hk = [("hm", kc) for kc in range(8)]
        self.carve_reset()
        A = {}
        A["qa"] = qa = self.carve([4, T], BF16)
        A["kaT"] = kaT = self.carve([2, T + PAST], BF16)
        A["vaug"] = vaug = self.carve([12, 2, 128], BF16)
        A["vtok"] = vtok = self.carve([8, 512], BF16)
        A["mixb"] = mixb = self.carve([4, T], BF16)
        A["qs"] = self.carve([T], BF16)
        A["gsil"] = self.carve([T], BF16)
        A["tset"] = [(self.carve([512], F32), self.carve([512], F32), self.carve([512], F32)) for _ in range(2)]
        A["qhat"] = self.carve([2, T], BF16)
        A["khat"] = self.carve([2, T], BF16)
        A["ktok"] = self.carve([8, 2, 128], BF16)
        A["AT"] = self.carve([2, 8, 128], BF16)
        A["obuf"] = self.carve([T], F32)
        A["Ebuf2"] = [self.carve([8], F32), self.carve([8], F32)]
        A["Lb"] = self.carve([2, 16], F32)
        A["Eh"] = self.carve([2, 16], F32)
        A["Fb"] = self.carve([2, 16], F32)
        A["Sst"] = self.carve([8, 128], F32)
        A["dbf"] = self.carve([8, 128], BF16)
        self.mixb = mixb

        self.S.mark("g%d l%d hgrn+attnA" % (g, l))

        def pre_hgrn():
            AT_ = A["AT"]
            self.S.add("dve", lambda e_: e_.memset(AT_[:], 0.0), [], ["AT0", ("AT", 0), ("AT", 1)])
            for ti in range(2):
                slot, wkey = self.wtile(w2d, 0, 8, 1280 + ti * 256, 256)
                for tt in range(8):
                    b = self.psb("mm")
                    self.mm(self.pb[b][:, 0:256], [(self.hm[:, kc, tt * 128:(tt + 1) * 128], slot[:, kc, :]) for kc in range(8)], [wkey] + hk, [("ps", b)])
                    self.cp("act" if tt % 2 == 0 else "dve", vtok[:, tt, ti * 256:(ti + 1) * 256], self.pb[b][:, 0:256], [("ps", b)], [("vtok", tt)])

            yield

        def pre_attn():
            self.S.add("dve", lambda e_: e_.memset(vaug[:, :, :, 64:128], 1.0), [], ["vaug_ones"])
            if g == 1:
                for j in range(4):
                    self.dma("pool", vaug[:, 8 + j, :, 0:64], dr["cav"][e, j * 128:(j + 1) * 128, :].rearrange("p (k d) -> p k d", d=64),
                             ["arena"], [("vaug", 8 + j)], chan=("cch", "cav", j))
                for j in range(4):
                    buf = self.rot("cst", 2)
                    self.dma("sp", self.cst[:, buf, 0:256], dr["cakd"][e, j * 128:(j + 1) * 128, :], [], [("cst", buf)], chan=("lch", "cst", buf))
                    b = self.psb("aux")
                    bank, cst, ident = self.pb[b], self.cst, self.ident

                    def fn(e_, bank=bank, cst=cst, buf=buf, ident=ident):
                        e_.transpose(bank[:, 0:128], cst[:, buf, 0:128], ident[:])
                        return e_.transpose(bank[:, 128:256], cst[:, buf, 128:256], ident[:])
                    self.S.add("pe", fn, [("cst", buf), "ident"], [("ps", b)])
                    self.cp("act", kaT[:, :, T + j * 128:T + (j + 1) * 128], bank[:, 0:256].rearrange("p (k t) -> p k t", k=2), [("ps", b)], [("kaT", 0), ("kaT", 1)])

            def h_qa(oc, th, ps, pk):
                self.cp("act" if th == 0 else "dve", qa[:, oc, th * 512:(th + 1) * 512], ps, [pk], [("qa", oc)])
            self.proj_fm(w2d, 0, 512, self.hm, "hm", 8, h_qa)

            def h_ka(oc, th, ps, pk):
                self.cp("act" if th == 0 else "dve", kaT[:, oc, th * 512:(th + 1) * 512], ps, [pk], [("kaT", oc)])
            self.proj_fm(w2d, EVEN_IN, 256, self.hm, "hm", 8, h_ka)
            if g == 1:
                for c in range(4):
                    self.rope(qa[:, c, :], [("qa", c)], self.perm_hd[:], self.cos_hd, self.sin_hd, T)
                for c in range(2):
                    self.rope(kaT[:, c, 0:T], [("kaT", c)], self.perm_hd[:], self.cos_hd, self.sin_hd, T)
            slot, wkey = self.wtile(w2d, 0, 8, 512, 256)
            for tt in range(8):
                b = self.psb("mm")
                self.mm(self.pb[b][:, 0:256], [(self.hm[:, kc, tt * 128:(tt + 1) * 128], slot[:, kc, :]) for kc in range(8)], [wkey] + hk, [("ps", b)])
                sk_ = self.cfg.get("skip", "")
                if "v" not in sk_:
                    self.cp("act", vaug[:, tt, :, 0:64], self.pb[b][:, 128:256].rearrange("p (k d) -> p k d", d=64), [("ps", b)], [("vaug", tt)])
                if g == 0 and "c" not in sk_:
                    buf = self.rot("cst", 2)
                    self.cp("dve", self.cst[:, buf, 0:256], self.pb[b][:, 0:256], [("ps", b)], [("cst", buf)])
                    seq, p0 = tt // 2, (tt % 2) * 128
                    if "d" not in sk_:
                        self.dma("sp", dr["nak"][seq, e, p0:p0 + 128, :], self.cst[:, buf, 0:128], [("cst", buf)], [], is_out=True, chan=("och", "csta", buf))
                        self.dma("sp", dr["nav"][seq, e, p0:p0 + 128, :], self.cst[:, buf, 128:256], [("cst", buf)], [], is_out=True, chan=("och", "cstb", buf))

            yield

        self.mixa = mixa = self.zb[:].rearrange("p (c a) b -> p c (a b)", a=2)

        def hgrn_all():
            yield from pre_hgrn()
            for hd in range(4):
                yield from self.hgrn_head(g, e, hd, w2d, A)

        def attn_all():
            yield from pre_attn()
            scale = HD ** -0.5
            if g == 0:
                units = [(seq * 256 + qb * 128, [(seq * 2 + kb, None) for kb in range(2)]) for seq in range(4) for qb in range(2)]
            else:
                units = []
                for qt in range(8):
                    kbs = []
                    for j in (qt - 1, qt, qt + 1):
                        if 0 <= j < 8:
                            kbs.append((j, None if j == qt else (self.m_prev[:] if j == qt - 1 else self.m_next[:])))
                    kbs += [(8 + j, None) for j in range(4)]
                    units.append((qt * 128, kbs))
            for q0, kbs in units:
                for kvh in range(2):
                    heads = [kvh * 4 + i for i in (0, 2, 1, 3)]
                    qaps = [qa[(h % 2) * 64:(h % 2) * 64 + 64, h // 2, q0:q0 + 128] for h in heads]
                    qkeys = [("qa", c) for c in (kvh * 2, kvh * 2 + 1)]
                    kblocks = []
                    for (kt, mask) in kbs:
                        kts = [kaT[(h % 2) * 64:(h % 2) * 64 + 64, kvh, kt * 128:(kt + 1) * 128] for h in heads]
                        kblocks.append((kts, vaug[:, kt, kvh, :], mask, [("kaT", kvh), ("vaug", kt), "vaug_ones"]))

                    parts = [(0, 256, mixa[0:64, kvh * 2:kvh * 2 + 2, q0:q0 + 128], 128),
                             (256, 256, mixa[64:128, kvh * 2:kvh * 2 + 2, q0:q0 + 128], 128)]
                    dk = [("zb", 2 * c + q0 // 512) for c in (kvh * 2, kvh * 2 + 1)]
                    sb_ = self.esinkU[64:128, e, kvh, :].unsqueeze(2).to_broadcast([64, 4, 128])
                    self.attn_unit(qaps, qkeys, kblocks, scale, 128, self.attn_fin_batched(parts, dk, 128, 4, sb_), ng=2)
                    yield
        self.run_lanes([hgrn_all(), attn_all()], [{"mm": [0, 1], "st": [2, 3], "aux": [4]}, {"mm": [5, 6], "st": [7], "aux": [7]}],
                       slots=[[0, 1, 2, 3], [4, 5]])

    def interleave(self, gens, counts):
        done = [False] * len(gens)
        acc = [0.0] * len(gens)
        while not all(done):
            for gi, gen in enumerate(gens):
                if done[gi]:
                    continue
                acc[gi] += counts[gi] / float(counts[0]) if not done[0] else 1.0
                while acc[gi] >= 1.0 and not done[gi]:
                    acc[gi] -= 1.0
                    self.stream = gi
                    try:
                        next(gen)
                    except StopIteration:
                        done[gi] = True
                    self.stream = None

    def hgrn_head(self, g, e, hd, w2d, A):
        dr = self.dram
        qs, gsil = A["qs"], A["gsil"]
        qhat, khat, ktok, AT, obuf = A["qhat"], A["khat"], A["ktok"], A["AT"], A["obuf"]
        Lb, Eh, Fb, Sst, dbf, vtok, mixb = A["Lb"], A["Eh"], A["Fb"], A["Sst"], A["dbf"], A["vtok"], A["mixb"]
        DKS = 128 ** -0.5
        NCK = T // CH
        for (col, dst, key) in ((768 + hd * 128, qs, "qs"), (2816 + hd * 128, gsil, "gsil")):
            def hh(oc, th, ps, pk, dst=dst, key=key):
                self.act(dst[:, th * 512:(th + 1) * 512], ps, AF.Silu, [pk], [key])
            self.proj_fm(w2d, col, 128, self.hm, "hm", 8, hh)
        for d in range(2):
            lbv = self.lb[:, e, d, 0, hd:hd + 1]
            omv = self.lb[:, e, d, 1, hd:hd + 1]

            def hg(oc, th, ps, pk, d=d, lbv=lbv, omv=omv):
                sl = slice(th * 512, (th + 1) * 512)
                nc_ = 512 // CH
                ti_ = self.rot("tset", 2)
                t1, t2, t3 = A["tset"][ti_]
                k1, k2, k3, ke = ("t1", ti_), ("t2", ti_), ("t3", ti_), ("Ebuf", ti_)
                Ebuf = A["Ebuf2"][ti_]
                self.act(t2[:], ps, AF.Sigmoid, [pk], [k2], scale=-1.0)
                self.act(t2[:], t2[:], AF.Identity, [k2] + self.lbkeys, [k2], scale=omv)
                self.act(t1[:], t2[:], AF.Ln, [k2, "epsc2"], [k1], bias=self.epsc[:, 2:3], scale=-1.0)
                sm = self.scanmask
                self.S.add("dve", lambda e_: e_.tensor_tensor_scan(out=t3[:], data0=sm[:], data1=t1[:], initial=0.0, op0=ALU.mult, op1=ALU.add),
                           [k1, "scanmask", "scanmask2"], [k3])
                t33 = t3[:].rearrange("p (n c) -> p n c", c=CH)
                self.cp("dve", Lb[:, d, th * nc_:(th + 1) * nc_], t33[:, :, CH - 1], [k3], [("Lb", d)])
                self.ts("dve", Ebuf[:], t33[:, :, CH - 1], 0.5, None, ALU.mult, None, [k3], [ke])
                eb = Ebuf[:].unsqueeze(2).to_broadcast([128, nc_, CH])
                if d == 0:
                    self.tt("dve", t33, t33, eb, ALU.subtract, [k3, ke], [k3])
                else:
                    self.tt("dve", t3[:], t1[:], t3[:], ALU.subtract, [k1, k3], [k3])
                    self.tt("dve", t33, t33, eb, ALU.add, [k3, ke], [k3])
                self.act(t1[:], t3[:], AF.Exp, [k3], [k1])
                self.act(t3[:], t3[:], AF.Exp, [k3], [k3], scale=-1.0)
                self.stt("dve", qhat[:, d, sl], qs[:, sl], DKS, t1[:], ALU.mult, ALU.mult, ["qs", k1], [("qhat", d)])
                self.tt("dve", khat[:, d, sl], t2[:], t3[:], ALU.mult, [k3, k2], [("khat", d)])
            self.proj_fm(w2d, 1792 + d * 512 + hd * 128, 128, self.hm, "hm", 8, hg)
            yield
        yield
        nseq = 4 if g == 0 else 1
        nst = NCK // nseq
        self.act(Eh[:], Lb[:], AF.Exp, [("Lb", 0), ("Lb", 1)], ["Eh"], scale=0.5)
        self.cp("dve", Fb[:], Eh[:], ["Eh"], ["Fb"])
        if nst > 1:
            f0 = Fb[:, 0, :].rearrange("p (s n) -> p s n", s=nseq)
            e0 = Eh[:, 0, :].rearrange("p (s n) -> p s n", s=nseq)
            self.tt("dve", f0[:, :, 0:nst - 1], f0[:, :, 0:nst - 1], e0[:, :, 1:nst], ALU.mult, ["Eh", "Fb"], ["Fb"])
            f1 = Fb[:, 1, :].rearrange("p (s n) -> p s n", s=nseq)
            e1 = Eh[:, 1, :].rearrange("p (s n) -> p s n", s=nseq)
            self.tt("dve", f1[:, :, 1:nst], f1[:, :, 1:nst], e1[:, :, 0:nst - 1], ALU.mult, ["Eh", "Fb"], ["Fb"])
        for d in range(2):
            for th in range(2):
                sl = slice(th * 512, (th + 1) * 512)
                kf = self.zsq[:, 1, :]
                fb_ = Fb[:, d, th * 8:(th + 1) * 8].unsqueeze(2).to_broadcast([128, 8, CH])
                self.tt("dve", kf.rearrange("p (n c) -> p n c", c=CH), khat[:, d, sl].rearrange("p (n c) -> p n c", c=CH), fb_, ALU.mult,
                        [("khat", d), "Fb"], [("zsq", 1)])
                b = self.psb("aux")
                bankb = self.pb[b][:].bitcast(BF16)
                identb = self.identb

                def fn(e_, bankb=bankb, kf=kf, identb=identb):
                    ins = None
                    for i in range(4):
                        ins = e_.transpose(bankb[:, i * 128:(i + 1) * 128], kf[:, i * 128:(i + 1) * 128], identb[:])
                    return ins
                self.S.add("pe", fn, [("zsq", 1), "c_ident"], [("ps", b)])
                self.cp("act", ktok[:, th * 4:(th + 1) * 4, d, :], bankb[:, 0:512].rearrange("p (i k) -> p i k", k=128), [("ps", b)], [("ktok", d)])
                b = self.psb("aux")
                bank = self.pb[b]

                def fn2(e_, bank=bank, d=d, th=th):
                    ins = None
                    for i in range(4):
                        c0 = (th * 4 + i) * 128
                        ins = e_.matmul(bank[:, i * 128:(i + 1) * 128], khat[:, d, c0:c0 + 128], qhat[:, d, c0:c0 + 128], start=True, stop=True)
                    return ins
                self.S.add("pe", fn2, [("khat", d), ("qhat", d)], [("ps", b)])
                m = self.hmask[:, d, :].unsqueeze(1).to_broadcast([128, 4, 128])
                atv = AT[:, d, th * 4:(th + 1) * 4, :]
                bv = bank[:].rearrange("p (i t) -> p i t", t=128)
                self.S.add("dve", lambda e_, atv=atv, m=m, bv=bv: e_.copy_predicated(out=atv, mask=m, data=bv), [("ps", b), "c_hmask", ("AT", d), "AT0"], [("AT", d)])
                yield
        yield
        if g == 0:
            self.S.add("dve", lambda e_: e_.memset(Sst[:], 0.0), [], [("Sst", i) for i in range(8)])
        else:
            for d in range(2):
                self.dma("sp", Sst[:, d, :], dr["stb"][e, d, hd], ["arena"], [("Sst", d)], chan=("lch", "Sst", d))
                c_first = 0 if d == 0 else nst - 1
                self.ts("dve", Sst[:, d, :], Sst[:, d, :], Eh[:, d, c_first:c_first + 1], None, ALU.mult, None, [("Sst", d), "Eh"], [("Sst", d)])
        nchain = 2 * nseq
        self.cp("act", dbf[:, 0:nchain, :], Sst[:, 0:nchain, :], [("Sst", i) for i in range(nchain)], [("dbf", i) for i in range(nchain)])
        written = set()
        for step in range(nst):
            for d in range(2):
                ob_ = self.psb("st")
                obank = self.pb[ob_]
                cn0 = None
                for sq in range(nseq):
                    ci = sq * 2 + d if g == 0 else d
                    cn = sq * nst + (step if d == 0 else nst - 1 - step)
                    if cn0 is None:
                        cn0 = cn
                    tt_, hb = cn // 2, (cn % 2) * 64
                    Vn = vtok[hb:hb + 64, tt_, hd * 128:(hd + 1) * 128]
                    ops = [(Vn, AT[hb:hb + 64, d, tt_, hb:hb + 64]), (dbf[:, ci, :], qhat[:, d, cn * CH:(cn + 1) * CH])]
                    self.mm(obank[:, sq * CH:(sq + 1) * CH], ops, [("vtok", tt_), ("AT", d), ("dbf", ci), ("qhat", d)], [("ps", ob_)])
                    ub = self.psb("mm")
                    self.mm(self.pb[ub][:, 0:128], [(ktok[hb:hb + 64, tt_, d, :], Vn)], [("ktok", d), ("vtok", tt_)], [("ps", ub)])
                    self.stt("dve", Sst[:, ci, :], Sst[:, ci, :], Fb[:, d, cn:cn + 1], self.pb[ub][:, 0:128], ALU.mult, ALU.add,
                             [("Sst", ci), "Fb", ("ps", ub)], [("Sst", ci)])
                    if step < nst - 1:
                        self.cp("act", dbf[:, ci, :], Sst[:, ci, :], [("Sst", ci)], [("dbf", ci)])
                off = (cn0 % nst) * CH
                dstv = obuf[:].rearrange("p (s t) -> p s t", s=nseq)[:, :, off:off + CH]
                srcv = obank[:, 0:nseq * CH].rearrange("p (s t) -> p s t", t=CH)
                if cn0 in written:
                    self.tt("dve", dstv, dstv, srcv, ALU.add, [("ps", ob_), "obuf"], ["obuf"])
                else:
                    self.cp("act", dstv, srcv, [("ps", ob_)], ["obuf"])
                    written.add(cn0)
            yield
        if g == 0:
            for sq in range(4):
                for d in range(2):
                    ci = sq * 2 + d
                    self.dma("sp", dr["nsb"][sq, e, d, hd], Sst[:, ci, :], [("Sst", ci), "arena"], [], is_out=True, chan=("och", "Sst", ci))
        yield
        gcol = self.vecs[:, VC["gnorm"] + e:VC["gnorm"] + e + 1]
        for th in range(2):
            sl = slice(th * 512, (th + 1) * 512)
            self.act(self.zsq[:, 0, :], obuf[:, sl], AF.Square, ["obuf"], [("zsq", 0)])
            b = self.psb("aux")
            self.mm(self.pb[b][:], [(self.ones[:], self.zsq[:, 0, :])], ["ones", ("zsq", 0)], [("ps", b)])
            self.rsqrt(self.rstd[:], self.pb[b][:], 1.0 / 128, 1, [("ps", b)], ["rstd"])
            self.stt("dve", self.mean[:], obuf[:, sl], gcol, self.rstd[:], ALU.mult, ALU.mult, ["obuf", "rstd", "vecs"], ["mean"])
            self.tt("dve", mixb[:, hd, sl], self.mean[:], gsil[:, sl], ALU.mult, ["mean", "gsil"], [("mixb", hd)])
        yield

    def odd_mixer(self, g, l):
        o = l // 2
        dr = self.dram
        w2d = dr["w_in_cd"][o]
        hk = [("hm", kc) for kc in range(8)]
        NK = T if g == 0 else T + PAST
        nkt = NK // 128
        self.carve_reset()
        cqn = self.carve([3, T], BF16)
        latT = self.carve([2, T + PAST], BF16)
        kpeT = self.carve([T + PAST], BF16)
        Qc = self.carve([4, T], BF16)
        Kc = self.carve([4, T + PAST], BF16)
        vaugC = self.carve([12, 4, 128], BF16)
        qd = self.carve([4, T], BF16)
        kdT = self.carve([4, T + PAST], BF16)
        vaugD = self.carve([12, 4, 128], BF16)
        v = self.vecs
        self.S.add("dve", lambda e_: e_.memset(vaugC[:, :, :, 64:128], 1.0), [], ["vaugC_ones"])
        self.S.add("dve", lambda e_: e_.memset(Qc[:], 0.0), [], [("Qc", h_) for h_ in range(4)])
        self.S.add("dve", lambda e_: e_.memset(Kc[:], 0.0), [], [("Kc", h_) for h_ in range(4)])
        self.S.add("dve", lambda e_: e_.memset(vaugD[:, :, :, 64:128], 1.0), [], ["vaugD_ones"])

        for th in range(2):
            sl = slice(th * 512, (th + 1) * 512)

            def h_cq(oc, th_, ps, pk, sl=sl):
                self.cp("dve", cqn[:, oc, sl], ps, [pk], [("cqn", oc)])
                self.act(self.zsq[:, oc, :], ps, AF.Square, [pk], [("zsq", oc)])
            self.proj_fm(w2d, 0, 384, self.hm, "hm", 8, h_cq, halves=(th,))
            b = self.psb("st")
            self.mm(self.pb[b][:], [(self.ones[:], self.zsq[:, oc, :]) for oc in range(3)], ["ones"] + [("zsq", oc) for oc in range(3)], [("ps", b)])
            self.rsqrt(self.rstd[:], self.pb[b][:], 1.0 / 384, 1, [("ps", b)], ["rstd"])
            for oc in range(3):
                gc = v[:, VC["cqn"] + o * 3 + oc:VC["cqn"] + o * 3 + oc + 1]
                self.stt("dve", cqn[:, oc, sl], cqn[:, oc, sl], gc, self.rstd[:], ALU.mult, ALU.mult, [("cqn", oc), "rstd", "vecs"], [("cqn", oc)])

        def normed(dstbuf, dkeyname, gcol):
            pend = []

            def post(oc, sl, zi):
                b = self.psb("aux")
                self.mm(self.pb[b][:], [(self.bd[:], self.zsq[:, zi, :])], ["c_blockdiag", ("zsq", zi)], [("ps", b)])
                ti = self.rot("rdn", 2)
                tmp = self.rd[:, ti, :]
                self.rsqrt(tmp, self.pb[b][:], 1.0 / 64, 1, [("ps", b)], [("rd", ti)])
                self.stt("dve", dstbuf[:, oc, sl], dstbuf[:, oc, sl], gcol, tmp, ALU.mult, ALU.mult, [(dkeyname, oc), ("rd", ti), "vecs"], [(dkeyname, oc)])

            def h(oc, th, ps, pk):
                sl = slice(th * 512, (th + 1) * 512)
                zi = 4 + self.rot("zq", 4)
                self.cp("dve", dstbuf[:, oc, sl], ps, [pk], [(dkeyname, oc)])
                self.act(self.zsq[:, zi, :], ps, AF.Square, [pk], [("zsq", zi)])
                while len(pend) > 1:
                    post(*pend.pop(0))
                pend.append((oc, sl, zi))

            def flush():
                while pend:
                    post(*pend.pop(0))
            h.flush = flush
            return h
        hq = normed(qd, "qd", v[:, VC["dqn"] + o:VC["dqn"] + o + 1])
        hkd = normed(kdT, "kdT", v[:, VC["dkn"] + o:VC["dkn"] + o + 1])
        self.proj_fm(w2d, 672, 512, self.hm, "hm", 8, hq)
        hq.flush()
        self.proj_fm(w2d, ODD_IN, 512, self.hm, "hm", 8, hkd)
        hkd.flush()
        if g == 1:
            for c in range(4):
                self.rope(qd[:, c, :], [("qd", c)], self.perm_hd[:], self.cos_hd, self.sin_hd, T)
                self.rope(kdT[:, c, 0:T], [("kdT", c)], self.perm_hd[:], self.cos_hd, self.sin_hd, T)

        def lat_tile(buf, col0, R):
            b2 = self.psb("aux")
            bank, cst, ident = self.pb[b2], self.cst, self.ident

            def fn(e_, bank=bank, cst=cst, buf=buf, ident=ident):
                e_.transpose(bank[:, 0:128], cst[:, buf, 0:128], ident[:])
                e_.transpose(bank[:, 128:256], cst[:, buf, 128:256], ident[:])
                return e_.transpose(bank[0:32, 256:384], cst[:, buf, 256:288], ident[:])
            self.S.add("pe", fn, R + ["ident"], [("ps", b2)])
            self.cp("act", latT[:, :, col0:col0 + 128], bank[:, 0:256].rearrange("p (k t) -> p k t", k=2), [("ps", b2)], [("latT", 0), ("latT", 1)])
            self.cp("dve", kpeT[0:32, col0:col0 + 128], bank[0:32, 256:384], [("ps", b2)], ["kpeT"])

        slotA, kA = self.wtile(w2d, 0, 8, 384, 256)
        slotB, kB = self.wtile(w2d, 0, 8, 640, 32)
        ss1 = self.dummy[:, 2:3]
        for tt in range(8):
            b = self.psb("mm")
            ps = self.pb[b]
            self.mm(ps[:, 0:256], [(self.hm[:, kc, tt * 128:(tt + 1) * 128], slotA[:, kc, :]) for kc in range(8)], [kA] + hk, [("ps", b)])
            self.mm(ps[:, 256:288], [(self.hm[:, kc, tt * 128:(tt + 1) * 128], slotB[:, kc, 0:32]) for kc in range(8)], [kB] + hk, [("ps", b)])
            self.act(self.tA[:, 0:256], ps[:, 0:256], AF.Square, [("ps", b)], ["tA"])
            self.S.add("dve", lambda e_: e_.reduce_sum(out=ss1, in_=self.tA[:, 0:256], axis=mybir.AxisListType.X), ["tA"], ["ss1"])
            self.rsqrt(ss1, ss1, 1.0 / 256, 1, ["ss1"], ["ss1"])
            buf = self.rot("cst", 2)
            self.stt("dve", self.cst[:, buf, 0:256], ps[:, 0:256], ss1, self.ckvn_b[:, o * 256:(o + 1) * 256], ALU.mult, ALU.mult,
                     [("ps", b), "ss1", "ckvn_b"], [("cst", buf)])
            self.cp("act", self.cst[:, buf, 256:288], ps[:, 256:288], [("ps", b)], [("cstpe", buf)])
            if g == 0:
                seq, p0 = tt // 2, (tt % 2) * 128
                self.dma("sp", dr["nckv"][seq, o, p0:p0 + 128, :], self.cst[:, buf, 0:256], [("cst", buf)], [], is_out=True, chan=("och", "csta", buf))
                self.dma("sp", dr["ncpe"][seq, o, p0:p0 + 128, :], self.cst[:, buf, 256:288], [("cstpe", buf)], [], is_out=True, chan=("och", "cstb", buf))
            lat_tile(buf, tt * 128, [("cst", buf), ("cstpe", buf)])
        if g == 1:
            for j in range(4):
                buf = self.rot("cst", 2)
                self.dma("sp", self.cst[:, buf, 0:256], dr["cckv"][o, j * 128:(j + 1) * 128, :], [], [("cst", buf)], chan=("lch", "cst", buf))
                self.dma("act", self.cst[:, buf, 256:288], dr["ccpe"][o, j * 128:(j + 1) * 128, :], [], [("cstpe", buf)], chan=("lch", "cstpe", buf))
                lat_tile(buf, T + j * 128, [("cst", buf), ("cstpe", buf)])
            self.rope(kpeT[0:32, 0:T], ["kpeT"], self.perm_c[0:32, 0:32], self.cos_c[0:32, :], self.sin_c[0:32, :], T, rows=32)

        slot, wkey = self.wtile(w2d, 0, 8, 1440, 256)
        for tt in range(8):
            b = self.psb("mm")
            self.mm(self.pb[b][:, 0:256], [(self.hm[:, kc, tt * 128:(tt + 1) * 128], slot[:, kc, :]) for kc in range(8)], [wkey] + hk, [("ps", b)])
            self.cp("act", vaugD[:, tt, :, 0:64], self.pb[b][:, 0:256].rearrange("p (k d) -> p k d", d=64), [("ps", b)], [("vaugD", tt)])
            if g == 0:
                buf = self.rot("cst", 2)
                self.cp("dve", self.cst[:, buf, 0:256], self.pb[b][:, 0:256], [("ps", b)], [("cst", buf)])
                seq, p0 = tt // 2, (tt % 2) * 128
                self.dma("sp", dr["ndv"][seq, o, p0:p0 + 128, :], self.cst[:, buf, 0:256], [("cst", buf)], [], is_out=True, chan=("och", "csta", buf))
        if g == 1:
            for j in range(4):
                self.dma("pool", vaugD[:, 8 + j, :, 0:64], dr["cdv"][o, j * 128:(j + 1) * 128, :].rearrange("p (k d) -> p k d", d=64),
                         ["arena"], [("vaugD", 8 + j)], chan=("cch", "cdv", j))
                stg = self.stg[j % 2]
                sk = self.stgkeys(j % 2)
                self.dma("sp", stg[:, 0:512], dr["cdkd"][o, j * 128:(j + 1) * 128, :], [], sk, chan=("lch", "stg", j % 2))
                b = self.psb("mm")
                bank, ident = self.pb[b], self.ident

                def fn(e_, bank=bank, stg=stg, ident=ident):
                    ins = None
                    for c in range(4):
                        ins = e_.transpose(bank[:, c * 128:(c + 1) * 128], stg[:, c * 128:(c + 1) * 128], ident[:])
                    return ins
                self.S.add("pe", fn, sk + ["ident"], [("ps", b)])
                self.cp("act", kdT[:, :, T + j * 128:T + (j + 1) * 128], bank[:].rearrange("p (c t) -> p c t", c=4), [("ps", b)], [("kdT", c) for c in range(4)])
        else:
            for tt in range(8):
                b = self.psb("aux")
                bankb = self.pb[b][:].bitcast(BF16)
                identb = self.identb

                def fn(e_, bankb=bankb, tt=tt, identb=identb):
                    ins = None
                    for c in range(4):
                        ins = e_.transpose(bankb[:, c * 128:(c + 1) * 128], kdT[:, c, tt * 128:(tt + 1) * 128], identb[:])
                    return ins
                self.S.add("pe", fn, [("kdT", c) for c in range(4)] + ["c_ident"], [("ps", b)])
                buf = self.rot("cst", 2)
                self.cp("dve", self.cst[:, buf, 0:256].rearrange("p (c d) -> p c d", d=64), bankb[:, 0:512].rearrange("p (c x) -> p c x", x=128)[:, :, 0:64],
                        [("ps", b)], [("cst", buf)])
                seq, p0 = tt // 2, (tt % 2) * 128
                self.dma("sp", dr["ndk"][seq, o, p0:p0 + 128, :], self.cst[:, buf, 0:256], [("cst", buf)], [], is_out=True, chan=("och", "csta", buf))

        self.S.mark("g%d l%d mixC+D" % (g, l))

        def c_stream():
            wq, wkv = dr["c_w_q_up"][o], dr["c_w_kv_up"][o]
            scale_c = 96 ** -0.5
            for bi in range(2):
                for pair in range(2):
                    slot, wkey = self.wtile(wq, 0, 3, (bi * 2 + pair) * 192, 192)
                    for hh in range(2):
                        hl = pair * 2 + hh
                        for th in range(2):
                            sl = slice(th * 512, (th + 1) * 512)
                            b = self.psb("mm")
                            self.mm(self.pb[b][0:96, :], [(slot[:, kc, hh * 96:(hh + 1) * 96], cqn[:, kc, sl]) for kc in range(3)],
                                    [wkey] + [("cqn", kc) for kc in range(3)], [("ps", b)])
                            self.cp("act" if th == 0 else "dve", Qc[0:96, hl, sl], self.pb[b][0:96, :], [("ps", b)], [("Qc", hl)])
                        if g == 1:
                            for th in range(2):
                                sl = slice(th * 512, (th + 1) * 512)
                                b = self.psb("aux")
                                ps = self.pb[b][0:96, :]
                                self.mm(ps, [(self.perm_c[0:96, 0:96], Qc[0:96, hl, sl])], [("Qc", hl), "c_perm_c"], [("ps", b)])
                                t1 = self.xn[0][0:32, :]
                                t2 = self.xn[1][0:32, :]
                                self.tt("dve", t1, Qc[64:96, hl, sl], self.cos_c[64:96, sl], ALU.mult, [("Qc", hl), "c_cos_c"], ["xn0"])
                                self.tt("dve", t2, self.pb[b][64:96, :], self.sin_c[64:96, sl], ALU.mult, [("ps", b), "c_sin_c"], ["xn1"])
                                self.tt("dve", Qc[64:96, hl, sl], t1, t2, ALU.add, ["xn0", "xn1"], [("Qc", hl)])
                yield
                for pair in range(2):
                    slot, wkey = self.wtile(wkv, 0, 2, (bi * 2 + pair) * 256, 256)
                    for hh in range(2):
                        hl = pair * 2 + hh
                        for c0 in range(0, NK, 512):
                            b = self.psb("mm")
                            self.mm(self.pb[b][0:64, :], [(slot[:, kc, hh * 128:hh * 128 + 64], latT[:, kc, c0:c0 + 512]) for kc in range(2)],
                                    [wkey, ("latT", 0), ("latT", 1)], [("ps", b)])
                            self.cp("act" if (c0 // 512) % 2 == 0 else "dve", Kc[0:64, hl, c0:c0 + 512], self.pb[b][0:64, :], [("ps", b)], [("Kc", hl)])
                    for kt in range(nkt):
                        b = self.psb("mm")
                        self.mm(self.pb[b][:, 0:128], [(latT[:, kc, kt * 128:(kt + 1) * 128], slot[:, kc, :].rearrange("p (h x) -> p h x", x=128)[:, :, 64:128]) for kc in range(2)],
                                [wkey, ("latT", 0), ("latT", 1)], [("ps", b)])
                        self.cp("act" if kt % 2 == 0 else "dve", vaugC[:, kt, pair * 2:pair * 2 + 2, 0:64], self.pb[b][:, 0:128].rearrange("p (h d) -> p h d", d=64),
                                [("ps", b)], [("vaugC", kt)])
                yield
                for hl in range(4):
                    self.cp("act" if hl % 2 == 0 else "dve", Kc[64:96, hl, 0:NK], kpeT[0:32, 0:NK], ["kpeT"], [("Kc", hl)])
                if g == 0:
                    units = [(seq * 256, 256, [seq * 2, seq * 2 + 1]) for seq in range(4)]
                else:
                    units = [(qt * 512, 512, list(range(12))) for qt in range(2)]
                for hl in range(4):
                    h = bi * 4 + hl
                    for (q0, nq, kts_) in units:
                        kblocks = [([Kc[:, hl, kt * 128:(kt + 1) * 128]], vaugC[:, kt, hl, :], None, [("Kc", hl), ("vaugC", kt), "vaugC_ones"]) for kt in kts_]

                        def dst(i, h=h, q0=q0, nq=nq):
                            return self.hm[(h % 2) * 64:(h % 2) * 64 + 64, h // 2, q0:q0 + nq]

                        def dkey(i, h=h):
                            return [("hm", h // 2)]
                        self.attn_unit([Qc[:, hl, q0:q0 + nq]], [("Qc", hl)], kblocks, scale_c, nq, self.attn_fin(dst, dkey))
                        yield


        def d_stream():
            scale_d = HD ** -0.5
            if g == 0:
                units = [(seq * 256, [seq * 2, seq * 2 + 1]) for seq in range(4)]
            else:
                units = [(qt * 256, list(range(12))) for qt in range(4)]
            for c in range(4):
                for (q0, kts_) in units:
                    qaps = [qd[i * 64:(i + 1) * 64, c, q0:q0 + 256] for i in range(2)]
                    kblocks = [([kdT[i * 64:(i + 1) * 64, c, kt * 128:(kt + 1) * 128] for i in range(2)], vaugD[:, kt, c, :], None,
                                [("kdT", c), ("vaugD", kt), "vaugD_ones"]) for kt in kts_]

                    parts = [(0, 256, self.hm[0:64, 4 + c, q0:q0 + 256], None), (256, 256, self.hm[64:128, 4 + c, q0:q0 + 256], None)]
                    self.attn_unit(qaps, [("qd", c)], kblocks, scale_d, 256, self.attn_fin_batched(parts, [("hm", 4 + c)], 256, 2), ng=2)
                    yield

        self.run_lanes([c_stream(), d_stream()], [{"mm": [0, 1, 7], "st": [2], "aux": [3]}, {"mm": [4, 5], "st": [6], "aux": [6]}])

    def build(self):
        try:
            self.build_()
        except StopBuild:
            pass
        self.S.emit(self.st)

    def build_(self):
        cfg = self.cfg
        groups = cfg.get("groups", (0, 1))
        NL = cfg.get("nl", DEPTH)
        self.setup()
        self.stage(0.25)
        first = True
        for g in groups:
            self.load_x(g)
            if first:
                self.mods(0)
                self.dbg("dbg_modv", self.modv[:], [("modv", 0, 0), ("modv", 0, 1)])
            self.dbg("dbg_x0", self.x[:], [("x", c, h_) for c in range(8) for h_ in range(2)])
            self.stage(0.75)
            self.modulate0(g, 0)
            self.dbg("dbg_h0", self.hm[:], [("hm", c) for c in range(8)])
            self.stage(1)
            for l in range(NL):
                side = None
                if first:
                    self.lnfold(l, (0,))
                    if l + 1 < DEPTH:
                        side = self.mods_gen(l + 1)
                self.stage(1.2)
                self.S.mark("g%d l%d mixer" % (g, l))
                if l % 2 == 0:
                    self.even_mixer(g, l)
                    mixb = self.mixb
                    mixa = self.mixa
                    srcf = lambda kc, mixb=mixb, mixa=mixa: ((mixa[:, kc, :], [("zb", 2 * kc), ("zb", 2 * kc + 1)]) if kc < 4 else (mixb[:, kc - 4, :], [("mixb", kc - 4)]))
                    self.proj_fm(self.dram["w_out"][l], 0, D, None, None, 8, self.ln_z(l, 2, g), srcf=srcf)
                else:
                    self.odd_mixer(g, l)
                    self.proj_fm(self.dram["w_out"][l], 0, D, self.hm, "hm", 8, self.ln_z(l, 2, g))
                self.S.mark("g%d l%d ln1" % (g, l))
                if l % 2 == 0:
                    self.dbg("dbg_mix_bf", self.mixa, [("zb", c) for c in range(8)])
                else:
                    self.dbg("dbg_mix_bf", self.hm[:], [("hm", c) for c in range(8)])
                if l % 2 == 0:
                    self.dbg("dbg_mixb_bf", self.mixb[:], [("mixb", c) for c in range(4)])
                self.dbg("dbg_z", self.x[:], [("x", c, h_) for c in range(8) for h_ in range(2)])
                self.stage(4)
                self.ln_finish(g, l, 0, False)
                self.dbg("dbg_xmid", self.x[:], [("x", c, h_) for c in range(8) for h_ in range(2)])
                self.dbg("dbg_hmid_bf", self.hm[:], [("hm", c) for c in range(8)])
                self.stage(5)
                self.S.mark("g%d l%d ffn" % (g, l))
                self.ffn(g, l, side)
                self.tap("tap_x_%d_%d" % (g, l))
            self.S.mark("g%d store" % g)
            self.store_x(g)
            first = False


def _build_program(cfg):
    nc = bass.Bass("TRN2", target_bir_lowering=False)
    st = ExitStack()
    kb = KB(nc, st, cfg)
    kb.build()
    st.close()
    return nc, kb


def _core_inputs(inp, b, shared):
    m = dict(shared)
    f = lambda a: np.ascontiguousarray(np.asarray(a, np.float32))
    m["xp"] = f(inp["x_prompt"][4 * b:4 * b + 4].reshape(T, D))
    m["xs"] = f(inp["x_sample"][b])
    cak = np.asarray(inp["cache_a_k"][b]).reshape(N_EVEN, PAST, 2, 64)
    m["cakd"] = f(np.concatenate([cak[:, :, 0:1], cak[:, :, 0:1], cak[:, :, 1:2], cak[:, :, 1:2]], 2).reshape(N_EVEN, PAST, 256))
    m["cav"] = f(np.asarray(inp["cache_a_v"][b]).reshape(N_EVEN, PAST, 128))
    m["stb"] = f(inp["state_b"][b])
    m["cckv"] = f(inp["cache_c_kv"][b])
    m["ccpe"] = f(inp["cache_c_pe"][b])
    cdk = np.asarray(inp["cache_d_k"][b]).reshape(N_ODD, PAST, 4, 64)
    m["cdkd"] = f(np.repeat(cdk, 2, axis=2).reshape(N_ODD, PAST, 512))
    m["cdv"] = f(np.asarray(inp["cache_d_v"][b]).reshape(N_ODD, PAST, 256))
    m["vecs"] = _pack_vecs(inp, b)
    return m


def _shared_inputs(inp):
    f = lambda a: np.ascontiguousarray(np.asarray(a, np.float32))
    s = {}
    wab = np.asarray(inp["w_in_ab"], np.float32)
    s["w_in_ab"] = f(np.concatenate([wab, wab[:, :, 512:576], wab[:, :, 512:576], wab[:, :, 576:640], wab[:, :, 576:640]], 2))
    wcd = np.asarray(inp["w_in_cd"], np.float32)
    kd = wcd[:, :, 1184:1440].reshape(N_ODD, D, 4, 64)
    s["w_in_cd"] = f(np.concatenate([wcd, np.repeat(kd, 2, axis=2).reshape(N_ODD, D, 512)], 2))
    for k in ("w_ada", "c_w_q_up", "c_w_kv_up", "w_out", "w_ffn_gate", "w_ffn_up", "w_ffn_down"):
        s[k] = f(inp[k])
    s["ckvn_b"] = f(np.broadcast_to(np.asarray(inp["c_kv_norm"], np.float32).reshape(1, N_ODD * 256), (128, N_ODD * 256)))
    for k, v in _host_consts().items():
        s[k] = np.ascontiguousarray(v) if v.dtype == np.uint8 else f(v)
    return s


def _run(inp, cfg, cores):
    nc, kb = _build_program(cfg)
    shared = _shared_inputs(inp)
    in_maps = [_core_inputs(inp, b, shared) for b in cores]
    res = run_bass_kernel_spmd(nc, in_maps, core_ids=list(range(len(cores))))
    return res.results


def kernel(**inputs):
    inp = {k: np.asarray(v) for k, v in inputs.items()}
    results = _run(inp, {}, list(range(8)))
    B, SEQ = 32, 256
    yp = np.zeros((B, SEQ, D), np.float32)
    ys = np.zeros((8, T, D), np.float32)
    nak = np.zeros((B, N_EVEN, SEQ, 2, 64), np.float32)
    nav = np.zeros((B, N_EVEN, SEQ, 2, 64), np.float32)
    nsb = np.zeros((B, N_EVEN, 2, 4, 128, 128), np.float32)
    nckv = np.zeros((B, N_ODD, SEQ, 256), np.float32)
    ncpe = np.zeros((B, N_ODD, SEQ, 32), np.float32)
    ndk = np.zeros((B, N_ODD, SEQ, 4, 64), np.float32)
    ndv = np.zeros((B, N_ODD, SEQ, 4, 64), np.float32)
    for i, r in enumerate(results):
        sl = slice(4 * i, 4 * i + 4)
        yp[sl] = r["yp"].reshape(4, SEQ, D)
        ys[i] = r["ys"]
        nak[sl] = r["nak"].reshape(4, N_EVEN, SEQ, 2, 64)
        nav[sl] = r["nav"].reshape(4, N_EVEN, SEQ, 2, 64)
        nsb[sl] = r["nsb"]
        nckv[sl] = r["nckv"]
        ncpe[sl] = r["ncpe"]
        ndk[sl] = r["ndk"].reshape(4, N_ODD, SEQ, 4, 64)
        ndv[sl] = r["ndv"].reshape(4, N_ODD, SEQ, 4, 64)
    return (yp, ys, nak, nav, nsb, nckv, ncpe, ndk, ndv)
```

```python
from contextlib import ExitStack
import math
import numpy as np
import ml_dtypes
import concourse.bass as bass
import concourse.mybir as mybir
from concourse.bass_utils import run_bass_kernel_spmd

F32 = mybir.dt.float32
BF16 = mybir.dt.bfloat16
AF = mybir.ActivationFunctionType
ALU = mybir.AluOpType

D = 1024
NCH = 8
T = 1024
DEPTH = 4
N_EVEN = 2
N_ODD = 2
PAST = 512
HD = 64
GRID_W = 64
D_FF = 2816
NFF = 22
EVEN_IN = 3328
ODD_IN = 1696
ALPHA = (2 * DEPTH) ** 0.25
LN_EPS = 1e-5 / (ALPHA * ALPHA)
RMS_EPS = 1e-6
CH = 64
NSLOT = 6
STRICT_SAME_ENGINE = False
WT = 256

ENGS = ("pe", "act", "dve", "pool", "sp")


class Op:
    __slots__ = ("eng", "fn", "deps", "signal", "target", "sem", "is_dma", "chan", "lane", "dbg")

    def __init__(self, eng, fn):
        self.eng = eng
        self.fn = fn
        self.deps = {}
        self.signal = False
        self.target = None
        self.sem = None
        self.is_dma = False
        self.chan = None
        self.lane = None
        self.dbg = None


class Sched:
    def __init__(self, nc):
        self.nc = nc
        self.q = {e: [] for e in ENGS}
        self.last_w = {}
        self.readers = {}
        self.chans = {}
        self.out_chans = set()
        self.arena_key = True
        self.lane = None
        self.lq = None
        self.session = 0

    def _push(self, eng, op, reads=(), writes=()):
        if self.lane is None:
            self.q[eng].append(op)
        else:
            op.lane = (self.session, self.lane)
            op.dbg = (tuple(reads), tuple(writes))
            self.lq[self.lane][eng].append(op)
            self.lseq[self.lane].append((eng, op))

    def lanes_begin(self, n):
        self.session += 1
        self.lq = [{e: [] for e in ENGS} for _ in range(n)]
        self.lseq = [[] for _ in range(n)]

    def lanes_merge(self):
        for lane in self.lq:
            for e in ENGS:
                for op in lane[e]:
                    for d in op.deps:
                        if d.lane is not None and d.lane[0] == op.lane[0] and d.lane[1] != op.lane[1]:
                            raise RuntimeError("cross-lane dependency: %r -> %r" % (op.dbg, d.dbg))
        seqs = self.lseq
        n = [len(q) for q in seqs]
        idx = [0] * len(seqs)
        while any(idx[i] < n[i] for i in range(len(seqs))):
            best = None
            for i in range(len(seqs)):
                if idx[i] < n[i]:
                    frac = idx[i] / float(n[i])
                    if best is None or frac < best[0]:
                        best = (frac, i)
            i = best[1]
            e, op = seqs[i][idx[i]]
            self.q[e].append(op)
            idx[i] += 1
        self.lseq = None
        self.lq = None
        self.lane = None

    def _track(self, op, reads, writes):
        for r in reads:
            w = self.last_w.get(r)
            if w is not None and w is not op:
                op.deps[w] = True
            if isinstance(r, tuple) and r[0] == "ps":
                for rd in self.readers.get(r, ()):
                    if rd is not op and rd.eng != op.eng:
                        op.deps.setdefault(rd, False)
            self.readers.setdefault(r, []).append(op)
        for r in writes:
            w = self.last_w.get(r)
            if w is not None and w is not op:
                op.deps.setdefault(w, "W")
            for rd in self.readers.get(r, ()):
                if rd is not op:
                    op.deps.setdefault(rd, False)
            self.last_w[r] = op
            self.readers[r] = []

    def add(self, eng, fn, reads=(), writes=(), arena=True):
        op = Op(eng, fn)
        reads = list(reads)
        if arena:
            reads.append("arena")
        self._track(op, reads, writes)
        self._push(eng, op, reads, writes)
        return op

    def mark(self, label):
        op = Op("pe", None)
        op.chan = ("mark", label)
        self.q["pe"].append(op)

    def fence(self, eng, fn):
        op = Op(eng, fn)
        self._track(op, [], ["arena"])
        self.q[eng].append(op)
        return op

    def dma(self, eng, fn, reads=(), writes=(), chan=None, is_out=False):
        op = Op(eng, fn)
        op.is_dma = True
        self._track(op, reads, writes)
        if chan is None:
            chan = ("chan",) + tuple(writes if writes else reads)
        op.chan = chan
        st = self.chans.setdefault(chan, [None, 0])
        st[1] += 16
        op.target = st[1]
        if is_out:
            self.out_chans.add(chan)
        self._push(eng, op, reads, writes)
        return op

    def emit(self, stack):
        nc = self.nc
        esem = {e: stack.enter_context(nc.semaphore("s_" + e)) for e in ENGS if e != "sp"}
        for i, (k, st) in enumerate(self.chans.items()):
            st[0] = stack.enter_context(nc.semaphore("c%d" % i))
        for e in ENGS:
            for op in self.q[e]:
                keep = {}
                for d, raw in op.deps.items():
                    if d.is_dma:
                        keep[d] = raw
                    elif d.eng == op.eng and not op.is_dma:
                        if op.eng == "pe":
                            continue
                        if raw or STRICT_SAME_ENGINE:
                            keep[d] = raw
                    else:
                        keep[d] = raw
                op.deps = keep
                for d in keep:
                    if not d.is_dma:
                        d.signal = True
        for e in ENGS:
            cnt = 0
            for op in self.q[e]:
                if op.is_dma:
                    op.sem = self.chans[op.chan][0]
                elif op.signal:
                    cnt += 1
                    op.target = cnt
                    op.sem = esem[e]
        block = stack.enter_context(nc.Block())
        finals = [(self.chans[c][0], self.chans[c][1]) for c in self.out_chans]

        self.marks = []

        class _Cnt:
            def __init__(self, eh):
                self.eh = eh
                self.n = 0

            def matmul(self, *a, **k):
                self.n += 1
                return self.eh.matmul(*a, **k)

            def transpose(self, *a, **k):
                self.n += 1
                return self.eh.transpose(*a, **k)

            def wait_ge(self, *a, **k):
                return self.eh.wait_ge(*a, **k)

        def run(e, eh):
            waited = {}
            if e == "pe":
                eh = _Cnt(eh)
            for op in self.q[e]:
                if op.fn is None:
                    self.marks.append((op.chan[1], eh.n))
                    continue
                need = {}
                for d in op.deps:
                    key = id(d.sem)
                    if need.get(key, (None, 0))[1] < d.target:
                        need[key] = (d.sem, d.target)
                for key, (sem, tgt) in need.items():
                    if waited.get(key, 0) >= tgt:
                        continue
                    eh.wait_ge(sem, tgt)
                    waited[key] = tgt
                ins = op.fn(eh)
                if op.is_dma:
                    ins.then_inc(op.sem, 16)
                elif op.signal:
                    ins.then_inc(op.sem, 1)
            if e == "sp":
                for sem, tgt in finals:
                    eh.wait_ge(sem, tgt)

        @block.tensor
        def _(eh):
            run("pe", eh)

        @block.scalar
        def _(eh):
            run("act", eh)

        @block.vector
        def _(eh):
            run("dve", eh)

        @block.gpsimd
        def _(eh):
            run("pool", eh)

        @block.sync
        def _(eh):
            run("sp", eh)


def _rope_tables(rot_dim):
    n_rows = T // GRID_W
    t = np.arange(T)
    row = (t // GRID_W).astype(np.float32)
    col = (t % GRID_W).astype(np.float32)
    quarter = rot_dim // 4
    inv = (10000.0 ** (-np.arange(quarter, dtype=np.float32) / quarter)).astype(np.float32)
    ar = row[None, :] * inv[:, None]
    ac = col[None, :] * inv[:, None]
    cos = np.zeros((rot_dim, T), np.float32)
    sin = np.zeros((rot_dim, T), np.float32)
    partner = np.zeros(rot_dim, np.int64)
    half = rot_dim // 2
    for base, ang in ((0, ar), (half, ac)):
        for i in range(quarter):
            d1 = base + i
            d2 = base + quarter + i
            cos[d1] = np.cos(ang[i])
            cos[d2] = np.cos(ang[i])
            sin[d1] = -np.sin(ang[i])
            sin[d2] = np.sin(ang[i])
            partner[d1] = d2
            partner[d2] = d1
    return cos, sin, partner


def _host_consts():
    c = {}
    c["ident"] = np.eye(128, dtype=np.float32)
    cos, sin, partner = _rope_tables(64)
    c["cos_hd"] = np.concatenate([cos, cos], 0)
    c["sin_hd"] = np.concatenate([sin, sin], 0)
    pm = np.zeros((128, 128), np.float32)
    for hh in range(2):
        for m in range(64):
            pm[hh * 64 + partner[m], hh * 64 + m] = 1.0
    c["perm_hd"] = pm
    cos, sin, partner = _rope_tables(32)
    cc = np.zeros((128, T), np.float32)
    sc = np.zeros((128, T), np.float32)
    pc = np.zeros((128, 128), np.float32)
    for b in (0, 64):
        cc[b:b + 32] = cos
        sc[b:b + 32] = sin
        for m in range(32):
            pc[b + partner[m], b + m] = 1.0
    c["cos_c"] = cc
    c["sin_c"] = sc
    c["perm_c"] = pc
    k = np.arange(128)[:, None]
    q = np.arange(128)[None, :]
    c["m_prev"] = (k >= q).astype(np.float32)
    c["m_next"] = (k <= q).astype(np.float32)
    s = np.arange(CH)[:, None]
    tt = np.arange(CH)[None, :]
    hm = np.zeros((128, 2 * 128), np.float32)
    for b in (0, 64):
        hm[b:b + CH, b:b + CH] = (s <= tt)
        hm[b:b + CH, 128 + b:128 + b + CH] = (s >= tt)
    c["hmask"] = hm.astype(np.uint8)
    bd = np.zeros((128, 128), np.float32)
    bd[0:64, 0:64] = 1.0
    bd[64:128, 64:128] = 1.0
    c["blockdiag"] = bd
    return c


def _fm(v):
    v = np.asarray(v, np.float32)
    return np.ascontiguousarray(v.reshape(-1, 128).T)


VC = {}
_off = 0
for _name, _n in (("ln_g", 64), ("ln_b", 64), ("b_lb", 16), ("gnorm", 2), ("cqn", 6), ("dqn", 2), ("dkn", 2),
                  ("sink", 16), ("b_ada", 192), ("cond", 16)):
    VC[_name] = _off
    _off += _n
NVEC = _off


def _pack_vecs(inp, b):
    v = np.zeros((128, NVEC), np.float32)
    for l in range(DEPTH):
        for j in range(2):
            o = (l * 2 + j) * 8
            v[:, VC["ln_g"] + o:VC["ln_g"] + o + 8] = _fm(inp["ln_g"][l, j])
            v[:, VC["ln_b"] + o:VC["ln_b"] + o + 8] = _fm(inp["ln_b"][l, j])
        v[:, VC["b_ada"] + l * 48:VC["b_ada"] + (l + 1) * 48] = _fm(inp["b_ada"][l])
    for e in range(N_EVEN):
        for d in range(2):
            o = (e * 2 + d) * 4
            v[:, VC["b_lb"] + o:VC["b_lb"] + o + 4] = _fm(inp["b_lb"][e, d])
        v[:, VC["gnorm"] + e] = inp["b_gnorm"][e]
        v[:, VC["sink"] + e * 8:VC["sink"] + e * 8 + 8] = np.broadcast_to(inp["a_sink"][e][None, :], (128, 8))
    for o in range(N_ODD):
        v[:, VC["cqn"] + o * 3:VC["cqn"] + o * 3 + 3] = _fm(inp["c_q_norm"][o])
        v[:, VC["dqn"] + o] = np.tile(inp["d_q_norm"][o], 2)
        v[:, VC["dkn"] + o] = np.tile(inp["d_k_norm"][o], 2)
    cond2 = np.stack([inp["c_ctx"], inp["c"][b]], 0)
    for c2 in range(2):
        v[:, VC["cond"] + c2 * 8:VC["cond"] + c2 * 8 + 8] = _fm(cond2[c2])
    return v


class StopBuild(Exception):
    pass


class KB:
    def stage(self, n):
        if self.cfg.get("stage", 99) <= n:
            raise StopBuild()

    def dbg(self, name, ap, keys):
        if name in self.cfg.get("taps", {}):
            self.dma("sp", self.dram[name], ap, list(keys) + ["arena"], [], is_out=True, chan=("och", name))

    def __init__(self, nc, st, cfg):
        self.nc = nc
        self.st = st
        self.cfg = cfg
        self.S = Sched(nc)
        self.rr = {}
        self.wcount = 0
        self.dram = {}
        self._decl_dram()
        self._alloc()

    def din(self, name, shape, dt=F32):
        self.dram[name] = self.nc.dram_tensor(name, list(shape), dt, kind="ExternalInput").ap()
        return self.dram[name]

    def dout(self, name, shape):
        self.dram[name] = self.nc.dram_tensor(name, list(shape), F32, kind="ExternalOutput").ap()
        return self.dram[name]

    def _decl_dram(self):
        d = self.din
        d("xp", [T, D]); d("xs", [T, D])
        d("cakd", [N_EVEN, PAST, 256]); d("cav", [N_EVEN, PAST, 128])
        d("stb", [N_EVEN, 2, 4, 128, 128])
        d("cckv", [N_ODD, PAST, 256]); d("ccpe", [N_ODD, PAST, 32])
        d("cdkd", [N_ODD, PAST, 512]); d("cdv", [N_ODD, PAST, 256])
        d("w_ada", [DEPTH, D, 6 * D])
        d("w_in_ab", [N_EVEN, D, EVEN_IN + 256]); d("w_in_cd", [N_ODD, D, ODD_IN + 512])
        d("c_w_q_up", [N_ODD, 384, 768]); d("c_w_kv_up", [N_ODD, 256, 1024])
        d("w_out", [DEPTH, D, D])
        d("w_ffn_gate", [DEPTH, D, D_FF]); d("w_ffn_up", [DEPTH, D, D_FF]); d("w_ffn_down", [DEPTH, D_FF, D])
        d("vecs", [128, NVEC]); d("ckvn_b", [128, N_ODD * 256])
        for k, v in _host_consts().items():
            d(k, v.shape, mybir.dt.uint8 if v.dtype == np.uint8 else F32)
        o = self.dout
        o("yp", [T, D]); o("ys", [T, D])
        o("nak", [4, N_EVEN, 256, 128]); o("nav", [4, N_EVEN, 256, 128])
        o("nsb", [4, N_EVEN, 2, 4, 128, 128])
        o("nckv", [4, N_ODD, 256, 256]); o("ncpe", [4, N_ODD, 256, 32])
        o("ndk", [4, N_ODD, 256, 256]); o("ndv", [4, N_ODD, 256, 256])
        for name, shape in self.cfg.get("taps", {}).items():
            if name.endswith("_bf"):
                self.dram[name] = self.nc.dram_tensor(name, list(shape), BF16, kind="ExternalOutput").ap()
            else:
                o(name, shape)

    def sb(self, name, shape, dt):
        return self.st.enter_context(self.nc.sbuf_tensor("s_" + name, list(shape), dt))

    def _alloc(self):
        sb = self.sb
        self.x = sb("xres", [128, NCH, T], F32)
        self.hm = sb("hm", [128, NCH, T], BF16)
        self.wsl = [sb("wsl%d" % i, [128, 8, WT], BF16) for i in range(NSLOT)]
        self.ident = sb("ident", [128, 128], F32)
        self.identb = sb("identb", [128, 128], BF16)
        self.ones = sb("ones", [128, 128], BF16)
        self.bd = sb("bd", [128, 128], BF16)
        self.perm_hd = sb("perm_hd", [128, 128], BF16)
        self.perm_c = sb("perm_c", [128, 128], BF16)
        self.m_prev = sb("m_prev", [128, 128], BF16)
        self.m_next = sb("m_next", [128, 128], BF16)
        self.hmask = sb("hmask", [128, 2, 128], mybir.dt.uint8)
        self.epsc = sb("epsc", [128, 3], F32)
        self.cos_hd = sb("cos_hd", [128, T], BF16)
        self.sin_hd = sb("sin_hd", [128, T], BF16)
        self.cos_c = sb("cos_c", [128, T], BF16)
        self.sin_c = sb("sin_c", [128, T], BF16)
        self.scanmask = sb("scanmask", [128, 512], F32)
        self.vecs = sb("vecs", [128, NVEC], F32)
        self.ckvn_b = sb("ckvn_b", [128, N_ODD * 256], F32)
        self.modv = sb("modv", [128, DEPTH, 48, 2], F32)
        self.lnf = sb("lnf", [128, DEPTH, 2, 2, 8, 2], F32)
        self.lb = sb("lbv", [128, N_EVEN, 2, 2, 4], F32)
        self.esink = sb("esink", [128, 16], F32)
        self.esinkU = sb("esinkU", [128, 2, 2, 4], F32)
        self.scond = sb("scond", [128, 8, 2], BF16)
        self.zb = sb("zb", [128, 8, 512], BF16)
        self.zsq = sb("zsq", [128, 8, 512], BF16)
        self.stg = [self.zb[:, 0:4, :].bitcast(F32).rearrange("p a b -> p (a b)"),
                    self.zb[:, 4:8, :].bitcast(F32).rearrange("p a b -> p (a b)")]
        self.mean = sb("mean", [128, 512], F32)
        self.rstd = sb("rstd", [128, 512], F32)
        self.tA = sb("tA", [128, 512], F32)
        self.xn = [sb("xn%d" % i, [128, 512], F32) for i in range(2)]
        self.ptb = sb("ptb", [128, 4, 512], BF16)
        self.rd = sb("rd", [128, 2, 512], F32)
        self.cst = sb("cst", [128, 2, 288], F32)
        self.dummy = sb("dummy", [128, 8], F32)
        self.ARENA = 20224
        self.arena = sb("arena", [128, self.ARENA], F32)
        self.pb = [self.st.enter_context(self.nc.psum_tensor("pb%d" % i, [128, 512], F32)) for i in range(8)]

    def carve_reset(self):
        self.aoff = 0

    def carve(self, shape, dt):
        n = int(np.prod(shape))
        words = n if dt == F32 else (n + 1) // 2
        a = self.arena[:, self.aoff:self.aoff + words]
        self.aoff += words
        assert self.aoff <= self.ARENA, ("arena overflow", self.aoff)
        if dt != F32:
            a = a.bitcast(dt)
        if len(shape) == 2:
            a = a.rearrange("p (a b) -> p a b", b=shape[1])
        elif len(shape) == 3:
            a = a.rearrange("p (a b c) -> p a b c", b=shape[1], c=shape[2])
        return a

    def fence(self):
        dm = self.dummy
        self.S.fence("dve", lambda e: e.memset(dm[:, 0:1], 0.0))

    def rot(self, pool, n):
        i = self.rr.get(pool, 0)
        self.rr[pool] = i + 1
        return i % n

    def psb(self, pool):
        lb = getattr(self, "lanebanks", None)
        ln = self.S.lane
        if lb is not None and ln is not None:
            banks = lb[ln][pool]
            return banks[self.rot("%s_l%d" % (pool, ln), len(banks))]
        if pool == "mm":
            return self.rot("mm", 4)
        if pool == "st":
            return 4 + self.rot("st", 2)
        return 6 + self.rot("aux", 2)

    def ptslot(self):
        ln = self.S.lane
        if ln is None:
            return self.rot("ptb", 4)
        return 2 * ln + self.rot("ptb_l%d" % ln, 2)

    def rdslot(self):
        ln = self.S.lane
        if ln is None:
            return self.rot("rdb", 2)
        return ln

    def run_lanes(self, gens, banks, slots=None):
        self.lanebanks = banks
        self.laneslots = slots
        self.S.lanes_begin(len(gens))
        for i, gen in enumerate(gens):
            self.S.lane = i
            for _ in gen:
                pass
        self.S.lane = None
        self.S.lanes_merge()
        self.lanebanks = None
        self.laneslots = None

    def mm(self, out, ops, R, W):
        n = len(ops)

        def fn(e):
            ins = None
            for i, (l, r) in enumerate(ops):
                ins = e.matmul(out, l, r, start=(i == 0), stop=(i == n - 1))
            return ins
        return self.S.add("pe", fn, R, W)

    def tr(self, out, in_, ident, R, W):
        return self.S.add("pe", lambda e: e.transpose(out, in_, ident), R, W)

    def act(self, out, in_, func, R, W, bias=0.0, scale=1.0, eng="act"):
        return self.S.add(eng, lambda e: e.activation(out=out, in_=in_, func=func, bias=bias, scale=scale), R, W)

    def tt(self, eng, out, in0, in1, op, R, W):
        return self.S.add(eng, lambda e: e.tensor_tensor(out=out, in0=in0, in1=in1, op=op), R, W)

    def ts(self, eng, out, in0, s1, s2, op0, op1, R, W):
        if s2 is None:
            return self.S.add(eng, lambda e: e.tensor_scalar(out=out, in0=in0, scalar1=s1, scalar2=None, op0=op0), R, W)
        return self.S.add(eng, lambda e: e.tensor_scalar(out=out, in0=in0, scalar1=s1, scalar2=s2, op0=op0, op1=op1), R, W)

    def stt(self, eng, out, in0, scalar, in1, op0, op1, R, W):
        return self.S.add(eng, lambda e: e.scalar_tensor_tensor(out=out, in0=in0, scalar=scalar, in1=in1, op0=op0, op1=op1), R, W)

    def cp(self, eng, out, in_, R, W):
        if eng == "act":
            return self.S.add(eng, lambda e: e.copy(out=out, in_=in_), R, W)
        return self.S.add(eng, lambda e: e.tensor_copy(out=out, in_=in_), R, W)

    def dma(self, eng, out, in_, R, W, is_out=False, chan=None, slow=False):
        if slow:
            return self.S.dma(eng, lambda e: e.dma_start(out=out, in_=in_, allow_slow_non_contiguous=True), R, W, chan=chan, is_out=is_out)
        return self.S.dma(eng, lambda e: e.dma_start(out=out, in_=in_), R, W, chan=chan, is_out=is_out)

    def wtile(self, src2d, r0, nk, c0, ncols):
        ln = self.S.lane
        ls = getattr(self, "laneslots", None)
        if ln is None or ls is None:
            i = self.wcount % NSLOT
            self.wcount += 1
        else:
            i = ls[ln][self.rot("w_l%d" % ln, len(ls[ln]))]
        slot = self.wsl[i]
        key = ("w", i)
        src = src2d[r0:r0 + nk * 128, c0:c0 + ncols].rearrange("(kc p) n -> p kc n", p=128)
        self.dma("pool", slot[:, 0:nk, 0:ncols], src, [], [key], chan=("wch", i))
        return slot, key

    def proj_fm(self, w2d, c0, ncols, src, srckey, nk, handler, halves=(0, 1), nmm=512, srcf=None):
        col = c0
        oc_i = 0
        while col < c0 + ncols:
            w = min(WT, c0 + ncols - col)
            slot, wkey = self.wtile(w2d, 0, nk, col, w)
            for j in range(w // 128):
                for th in halves:
                    b = self.psb("mm")
                    out = self.pb[b][:, 0:nmm]
                    if srcf is None:
                        ops = [(slot[:, kc, j * 128:(j + 1) * 128], src[:, kc, th * nmm:(th + 1) * nmm]) for kc in range(nk)]
                        sk_ = [(srckey, kc) for kc in range(nk)]
                    else:
                        ops = [(slot[:, kc, j * 128:(j + 1) * 128], srcf(kc)[0][:, th * nmm:(th + 1) * nmm]) for kc in range(nk)]
                        sk_ = [k_ for kc in range(nk) for k_ in srcf(kc)[1]]
                    self.mm(out, ops, [wkey] + sk_, [("ps", b)])
                    handler(oc_i, th, out, ("ps", b))
                oc_i += 1
            col += w

    def setup(self):
        dr = self.dram
        self.dma("sp", self.ident[:], dr["ident"], [], ["ident"])
        self.dma("sp", self.vecs[:], dr["vecs"], [], ["vecs"])
        self.dma("act", self.ckvn_b[:], dr["ckvn_b"], [], ["ckvn_b"])
        for name, t in (("ident", self.identb), ("blockdiag", self.bd), ("perm_hd", self.perm_hd), ("perm_c", self.perm_c),
                        ("m_prev", self.m_prev), ("m_next", self.m_next), ("cos_hd", self.cos_hd),
                        ("sin_hd", self.sin_hd), ("cos_c", self.cos_c), ("sin_c", self.sin_c)):
            self.dma("pool", t[:], dr[name], [], ["c_" + name], chan=("cch", name))
        self.dma("act", self.hmask[:].rearrange("p a b -> p (a b)"), dr["hmask"], [], ["c_hmask"])
        ones, sm = self.ones, self.scanmask
        self.S.add("dve", lambda e: e.memset(ones[:], 1.0), [], ["ones"], arena=False)
        epsc = self.epsc
        self.S.add("dve", lambda e: e.memset(epsc[:, 0:1], LN_EPS), [], ["epsc0"], arena=False)
        self.S.add("dve", lambda e: e.memset(epsc[:, 1:2], RMS_EPS), [], ["epsc1"], arena=False)
        self.S.add("dve", lambda e: e.memset(epsc[:, 2:3], 1.0), [], ["epsc2"], arena=False)
        self.S.add("dve", lambda e: e.memset(sm[:], 1.0), [], ["scanmask"], arena=False)
        self.S.add("dve", lambda e: e.memset(sm[:].rearrange("p (n c) -> p n c", c=CH)[:, :, 0:1], 0.0), [], ["scanmask", "scanmask2"], arena=False)
        v = self.vecs
        c0 = VC["cond"]
        self.act(self.scond[:], v[:, c0:c0 + 16].rearrange("p (c k) -> p k c", c=2), AF.Silu, ["vecs"], ["scond"])
        self.act(self.esink[:], v[:, VC["sink"]:VC["sink"] + 16], AF.Exp, ["vecs"], ["esink"])
        es4 = self.esink[:].rearrange("p (e k i) -> p e k i", e=2, k=2)
        for pos, idx in enumerate((0, 2, 1, 3)):
            self.cp("dve", self.esinkU[:, :, :, pos], es4[:, :, :, idx], ["esink"], ["esinkU"])
        lb = self.lb
        self.S.add("dve", lambda e: e.memset(lb[:, 0, :, 0, :], 0.0), [], ["lb0a"], arena=False)
        self.S.add("dve", lambda e: e.memset(lb[:, 0, :, 1, :], 1.0), [], ["lb0b"], arena=False)
        b0 = VC["b_lb"]
        dtmp = self.dummy
        self.tt("dve", dtmp[:, 0:8], v[:, b0 + 8:b0 + 16], v[:, b0:b0 + 8], ALU.subtract, ["vecs"], ["dummy"])
        self.act(lb[:, 1, :, 0, :], dtmp[:, 0:8].rearrange("p (d h) -> p d h", d=2), AF.Sigmoid, ["dummy"], ["lb1a"])
        self.ts("dve", lb[:, 1, :, 1, :], lb[:, 1, :, 0, :], -1.0, 1.0, ALU.mult, ALU.add, ["lb1a"], ["lb1b"])
        self.lbkeys = ["lb0a", "lb0b", "lb1a", "lb1b"]

    def mods(self, l):
        for _ in self.mods_gen(l):
            pass

    def mods_gen(self, l):
        w2d = self.dram["w_ada"][l]
        b = self.psb("aux")
        acc = self.pb[b]
        for t in range(24):
            slot, wkey = self.wtile(w2d, 0, 8, t * WT, WT)
            for j in range(2):
                oc = t * 2 + j
                ops = [(slot[:, kc, j * 128:(j + 1) * 128], self.scond[:, kc, :]) for kc in range(8)]
                self.mm(acc[:, oc * 2:oc * 2 + 2], ops, [wkey, "scond"], [("ps", b)])
            yield
        v = self.vecs
        bcol = VC["b_ada"] + l * 48
        for c in range(2):
            src = acc[:, 0:96].rearrange("p (o c) -> p o c", c=2)[:, :, c]
            self.tt("dve", self.modv[:, l, :, c], src, v[:, bcol:bcol + 48], ALU.add, [("ps", b), "vecs"], [("modv", l, c)])
        mk = [("modv", l, 0), ("modv", l, 1)]
        for which in (1, 4):
            m = self.modv[:, l, which * 8:(which + 1) * 8, :]
            self.ts("dve", m, m, 1.0, None, ALU.add, None, mk, mk)
        for which in (2, 5):
            m = self.modv[:, l, which * 8:(which + 1) * 8, :]
            self.ts("dve", m, m, 1.0 / ALPHA, None, ALU.mult, None, mk, mk)

    def lnfold(self, l, whichs=(0, 1)):
        v = self.vecs
        for which in whichs:
            if which == 0:
                ls, sc_i, sh_i = l, 4, 3
            else:
                if l == DEPTH - 1:
                    continue
                ls, sc_i, sh_i = l + 1, 1, 0
            gcol = VC["ln_g"] + (l * 2 + which) * 8
            bcol = VC["ln_b"] + (l * 2 + which) * 8
            for c in range(2):
                sc = self.modv[:, ls, sc_i * 8:(sc_i + 1) * 8, c]
                sh = self.modv[:, ls, sh_i * 8:(sh_i + 1) * 8, c]
                G = self.lnf[:, l, which, 0, :, c]
                B = self.lnf[:, l, which, 1, :, c]
                R = [("modv", ls, c), "vecs"]
                self.tt("dve", G, sc, v[:, gcol:gcol + 8], ALU.mult, R, [("lnf", l, which, c, 0)])
                self.tt("dve", B, sc, v[:, bcol:bcol + 8], ALU.mult, R, [("lnf", l, which, c, 1)])
                self.tt("dve", B, B, sh, ALU.add, R + [("lnf", l, which, c, 1)], [("lnf", l, which, c, 1)])

    def stgkeys(self, i):
        return [("zb", k) for k in range(i * 4, i * 4 + 4)]

    def load_x(self, g):
        xd = self.dram["xp" if g == 0 else "xs"]
        for tt in range(8):
            stg = self.stg[tt % 2]
            sk = self.stgkeys(tt % 2)
            self.dma("sp" if tt % 2 == 0 else "act", stg, xd[tt * 128:(tt + 1) * 128, :], [], sk)
            for half in range(2):
                b = self.psb("mm")
                bank = self.pb[b]
                ident = self.ident

                def fn(e, bank=bank, stg=stg, half=half, ident=ident):
                    ins = None
                    for c4 in range(4):
                        c = half * 4 + c4
                        ins = e.transpose(bank[:, c4 * 128:(c4 + 1) * 128], stg[:, c * 128:(c + 1) * 128], ident[:])
                    return ins
                self.S.add("pe", fn, sk + ["ident"], [("ps", b)])
                dst = self.x[:, half * 4:(half + 1) * 4, tt * 128:(tt + 1) * 128]
                self.cp("act" if half == 0 else "dve", dst, bank[:].rearrange("p (a b) -> p a b", b=128),
                        [("ps", b)], [("x", c, tt // 4) for c in range(half * 4, half * 4 + 4)])

    def modulate0(self, g, l):
        for c in range(8):
            sc = self.modv[:, l, 8 + c, g:g + 1]
            sh = self.modv[:, l, c, g:g + 1]
            R = [("x", c, 0), ("x", c, 1), ("modv", l, g)]
            if (c % 2 == 0 or self.cfg.get("mod_act", False)) and not self.cfg.get("mod_dve", False):
                self.act(self.hm[:, c, :], self.x[:, c, :], AF.Identity, R, [("hm", c)], bias=sh, scale=sc)
            else:
                self.ts("dve", self.hm[:, c, :], self.x[:, c, :], sc, sh, ALU.mult, ALU.add, R, [("hm", c)])

    def store_x(self, g):
        yd = self.dram["yp" if g == 0 else "ys"]
        for tt in range(8):
            stg = self.stg[tt % 2]
            sk = self.stgkeys(tt % 2)
            for half in range(2):
                b = self.psb("mm")
                bank = self.pb[b]
                x, ident = self.x, self.ident

                def fn(e, bank=bank, half=half, tt=tt, x=x, ident=ident):
                    ins = None
                    for c4 in range(4):
                        c = half * 4 + c4
                        ins = e.transpose(bank[:, c4 * 128:(c4 + 1) * 128], x[:, c, tt * 128:(tt + 1) * 128], ident[:])
                    return ins
                self.S.add("pe", fn, [("x", c, tt // 4) for c in range(half * 4, half * 4 + 4)] + ["ident"], [("ps", b)])
                self.cp("act" if half == 0 else "dve", stg[:, half * 512:(half + 1) * 512], bank[:], [("ps", b)], sk)
            self.dma("sp", yd[tt * 128:(tt + 1) * 128, :], stg, sk, [], is_out=True, chan=("och", "y", tt % 2))

    def tap(self, name):
        if name in self.cfg.get("taps", {}):
            self.dma("sp", self.dram[name], self.x[:], [("x", c, h_) for c in range(8) for h_ in range(2)], [], is_out=True, chan=("och", name))

    def ln_z(self, l, gate_i, g):
        def h(oc, th, ps, pskey):
            xs = self.x[:, oc, th * 512:(th + 1) * 512]
            gc = self.modv[:, l, gate_i * 8 + oc, g:g + 1]
            self.stt("dve", xs, ps, gc, xs, ALU.mult, ALU.add, [pskey, ("x", oc, th), ("modv", l, g)], [("x", oc, th)])
        return h

    def ln_finish(self, g, l, which, last):
        for th in range(2):
            self.ln_a(th)
            self.ln_b(g, l, which, last, th)

    def ln_a(self, th):
        sl = slice(th * 512, (th + 1) * 512)
        for oc in range(8):
            xs = self.x[:, oc, sl]
            self.cp("act" if oc % 2 == 0 else "dve", self.zb[:, oc, :], xs, [("x", oc, th)], [("zb", oc)])
            if oc % 2 == 1:
                self.act(self.zsq[:, oc, :], xs, AF.Square, [("x", oc, th)], [("zsq", oc)])
            else:
                self.tt("dve", self.zsq[:, oc, :], xs, xs, ALU.mult, [("x", oc, th)], [("zsq", oc)])

    def ln_b(self, g, l, which, last, th):
        v = self.vecs
        sl = slice(th * 512, (th + 1) * 512)
        b1 = self.psb("st")
        b2 = self.psb("st")
        self.mm(self.pb[b1][:], [(self.ones[:], self.zb[:, oc, :]) for oc in range(8)], ["ones"] + [("zb", oc) for oc in range(8)], [("ps", b1)])
        self.mm(self.pb[b2][:], [(self.ones[:], self.zsq[:, oc, :]) for oc in range(8)], ["ones"] + [("zsq", oc) for oc in range(8)], [("ps", b2)])
        mean, rstd, tA, tB = self.mean, self.rstd, self.xn[0], self.xn[1]
        self.ts("dve", mean[:], self.pb[b1][:], 1.0 / D, None, ALU.mult, None, [("ps", b1)], ["mean"])
        self.tt("dve", tA[:], mean[:], mean[:], ALU.mult, ["mean"], ["xn0"])
        self.stt("dve", tB[:], self.pb[b2][:], 1.0 / D, tA[:], ALU.mult, ALU.subtract, [("ps", b2), "xn0"], ["xn1"])
        self.rsqrt(rstd[:], tB[:], 1.0, 0, ["xn1"], ["rstd"])
        gcol = VC["ln_g"] + (l * 2 + which) * 8
        bcol = VC["ln_b"] + (l * 2 + which) * 8
        for oc in range(8):
            xs = self.x[:, oc, sl]
            xn = self.xn[oc % 2]
            xk = "xn%d" % (oc % 2)
            self.tt("dve", xn[:], xs, mean[:], ALU.subtract, [("x", oc, th), "mean"], [xk])
            self.tt("dve", xn[:], xn[:], rstd[:], ALU.mult, [xk, "rstd"], [xk])
            self.act(xs, xn[:], AF.Identity, [xk, "vecs"], [("x", oc, th)], bias=v[:, bcol + oc:bcol + oc + 1], scale=v[:, gcol + oc:gcol + oc + 1])
            if not last:
                G = self.lnf[:, l, which, 0, oc, g:g + 1]
                B = self.lnf[:, l, which, 1, oc, g:g + 1]
                self.act(self.hm[:, oc, sl], xn[:], AF.Identity, [xk, ("lnf", l, which, g, 0), ("lnf", l, which, g, 1)], [("hm", oc)], bias=B, scale=G)

    def rsqrt(self, out, in_, scale, eps_i, R, W):
        self.act(out, in_, AF.Ln, list(R) + ["epsc0", "epsc1"], W, bias=self.epsc[0:out.shape[0], eps_i:eps_i + 1], scale=scale)
        self.act(out, out, AF.Exp, W, W, scale=-0.5)

    def out_proj(self, g, l):
        self.proj_fm(self.dram["w_out"][l], 0, D, self.hm, "hm", 8, self.ln_z(l, 2, g))
        self.ln_finish(g, l, 0, False)

    def ffn(self, g, l, side=None):
        self.fence()
        self.carve_reset()
        hid = self.carve([NFF, T], BF16)
        wg, wu, wd = self.dram["w_ffn_gate"][l], self.dram["w_ffn_up"][l], self.dram["w_ffn_down"][l]
        hk = [("hm", kc) for kc in range(8)]
        for t in range(D_FF // WT):
            sg_, kg = self.wtile(wg, 0, 8, t * WT, WT)
            su_, ku = self.wtile(wu, 0, 8, t * WT, WT)
            for j in range(2):
                fc = t * 2 + j
                for th in range(2):
                    sl = slice(th * 512, (th + 1) * 512)
                    bg = self.psb("mm")
                    bu = self.psb("mm")
                    self.mm(self.pb[bg][:], [(sg_[:, kc, j * 128:(j + 1) * 128], self.hm[:, kc, sl]) for kc in range(8)], [kg] + hk, [("ps", bg)])
                    self.mm(self.pb[bu][:], [(su_[:, kc, j * 128:(j + 1) * 128], self.hm[:, kc, sl]) for kc in range(8)], [ku] + hk, [("ps", bu)])
                    i = self.rot("ffnt", 2)
                    tmp = self.rd[:, i, :]
                    self.act(tmp, self.pb[bg][:], AF.Silu, [("ps", bg)], [("rd", i)])
                    self.tt("dve", hid[:, fc, sl], tmp, self.pb[bu][:], ALU.mult, [("rd", i), ("ps", bu)], [("hid", fc)])
            if side is not None:
                for _ in range(3):
                    next(side, None)
        if side is not None:
            for _ in side:
                pass
            self.lnfold(l, (1,))
        self.S.mark("g%d l%d down" % (g, l))
        hz = self.ln_z(l, 5, g)
        pieces = ((0, 8), (8, 8), (16, 6))
        last = (l == DEPTH - 1)

        def down_quarter(q, th):
            sl = slice(th * 512, (th + 1) * 512)
            slots = [self.wtile(wd, k0 * 128, nk, q * WT, WT) for (k0, nk) in pieces]
            for j in range(2):
                oc = q * 2 + j
                b = self.psb("mm")
                ops = []
                for (slot, _), (k0, nk) in zip(slots, pieces):
                    for kc in range(nk):
                        ops.append((slot[:, kc, j * 128:(j + 1) * 128], hid[:, k0 + kc, sl]))
                self.mm(self.pb[b][:], ops, [k for _, k in slots] + [("hid", fc) for fc in range(NFF)], [("ps", b)])
                hz(oc, th, self.pb[b][:], ("ps", b))
        for q in range(4):
            down_quarter(q, 0)
        self.ln_a(0)
        down_quarter(0, 1)
        self.S.mark("g%d l%d ln2" % (g, l))
        self.ln_b(g, l, 1, last, 0)
        for q in range(1, 4):
            down_quarter(q, 1)
        self.ln_a(1)
        self.ln_b(g, l, 1, last, 1)
        self.fence()

    def attn_unit(self, qaps, qkeys, kblocks, scale, nq, fin, ng=1):
        hp = len(qaps)
        gs = hp // ng
        ob = self.psb("st")
        O = self.pb[ob]
        nkb = len(kblocks)

        def stage_a(bi):
            kts, vap, mask, kkeys = kblocks[bi]
            pi = self.ptslot()
            pt = self.ptb[:, pi, 0:hp * nq]
            for gi in range(ng):
                b = self.psb("mm")
                Sb = self.pb[b]

                def fn(e, Sb=Sb, kts=kts, gi=gi):
                    ins = None
                    for j in range(gs):
                        i = gi * gs + j
                        ins = e.matmul(Sb[:, j * nq:(j + 1) * nq], kts[i], qaps[i], start=True, stop=True)
                    return ins
                self.S.add("pe", fn, list(qkeys) + list(kkeys), [("ps", b)])
                self.act(pt[:, gi * gs * nq:(gi + 1) * gs * nq], Sb[:, 0:gs * nq], AF.Exp, [("ps", b)], [("ptb", pi, gi)], scale=scale)
            pkeys = [("ptb", pi, gi) for gi in range(ng)]
            if mask is not None:
                p3 = pt.rearrange("p (h q) -> p h q", q=nq)
                m3 = mask.unsqueeze(1).to_broadcast([128, hp, nq])
                self.tt("dve", p3, p3, m3, ALU.mult, pkeys + ["c_m_prev", "c_m_next"], pkeys)
            return pt, pkeys

        def stage_b(bi, pt, pkeys):
            kts, vap, mask, kkeys = kblocks[bi]

            def fn2(e, pt=pt, vap=vap, first=(bi == 0), lastb=(bi == nkb - 1)):
                ins = None
                for i in range(hp):
                    ins = e.matmul(O[:, i * nq:(i + 1) * nq], vap, pt[:, i * nq:(i + 1) * nq], start=(first and i == 0), stop=lastb,
                                   skip_group_check=True)
                return ins
            self.S.add("pe", fn2, pkeys + list(kkeys), [("ps", ob)])

        cur = stage_a(0)
        for bi in range(nkb):
            nxt = stage_a(bi + 1) if bi + 1 < nkb else None
            stage_b(bi, *cur)
            cur = nxt
        if getattr(fin, "batched", False):
            fin(O[:, 0:hp * nq], ("ps", ob))
        else:
            for i in range(hp):
                fin(i, O[:, i * nq:(i + 1) * nq], ("ps", ob))

    def attn_fin_batched(self, parts, dkeys, nq, hp, sink_b=None):
        def fin(O, okey):
            ri = self.rdslot()
            rd = self.rd[64:128, ri, 0:hp * nq]
            if sink_b is not None:
                self.tt("dve", rd.rearrange("p (h q) -> p h q", q=nq), O[64:128, :].rearrange("p (h q) -> p h q", q=nq), sink_b, ALU.add,
                        [okey, "esinkU"], [("rd", ri)])
                self.act(rd, rd, AF.Ln, [("rd", ri)], [("rd", ri)])
            else:
                self.act(rd, O[64:128, :], AF.Ln, [okey], [("rd", ri)])
            self.act(rd, rd, AF.Exp, [("rd", ri)], [("rd", ri)], scale=-1.0)
            for (c0, ncols, out_ap, shp) in parts:
                i0 = O[0:64, c0:c0 + ncols]
                i1 = self.rd[64:128, ri, c0:c0 + ncols]
                if shp is not None:
                    i0 = i0.rearrange("p (a q) -> p a q", q=shp)
                    i1 = i1.rearrange("p (a q) -> p a q", q=shp)
                self.tt("dve", out_ap, i0, i1, ALU.mult, [okey, ("rd", ri)], dkeys)
        fin.batched = True
        return fin

    def attn_fin(self, dst, dkey, sink_ap=None):
        def fin(i, O, okey):
            nq = O.shape[-1]
            ri = self.rdslot()
            rd = self.rd[64:128, ri, 0:nq]
            if sink_ap is not None:
                self.ts("dve", rd, O[64:128, :], sink_ap(i), None, ALU.add, None, [okey, "esink"], [("rd", ri)])
                self.act(rd, rd, AF.Ln, [("rd", ri)], [("rd", ri)])
            else:
                self.act(rd, O[64:128, :], AF.Ln, [okey], [("rd", ri)])
            self.act(rd, rd, AF.Exp, [("rd", ri)], [("rd", ri)], scale=-1.0)
            self.tt("dve", dst(i), O[0:64, :], rd, ALU.mult, [okey, ("rd", ri)], dkey(i))
        return fin

    def rope(self, xap, xkey, perm, cos, sin, n, rows=128):
        for c0 in range(0, n, 512):
            w = min(512, n - c0)
            b = self.psb("aux")
            ps = self.pb[b][0:rows, 0:w]
            xs = xap[:, c0:c0 + w]
            self.mm(ps, [(perm, xs)], list(xkey) + ["c_perm_hd", "c_perm_c"], [("ps", b)])
            t1 = self.xn[0][0:rows, 0:w]
            t2 = self.xn[1][0:rows, 0:w]
            self.tt("dve", t1, xs, cos[:, c0:c0 + w], ALU.mult, list(xkey) + ["c_cos_hd", "c_cos_c"], ["xn0"])
            self.tt("dve", t2, ps, sin[:, c0:c0 + w], ALU.mult, [("ps", b), "c_sin_hd", "c_sin_c"], ["xn1"])
            self.tt("dve", xs, t1, t2, ALU.add, ["xn0", "xn1"], list(xkey))

    def even_mixer(self, g, l):
        e = l // 2
        dr = self.dram
        w2d = dr["w_in_ab"][e]
        hk = [("hm", kc) for kc in range(8)]
        self.carve_reset()
        A = {}
        A["qa"] = qa = self.carve([4, T], BF16)
        A["kaT"] = kaT = self.carve([2, T + PAST], BF16)
        A["vaug"] = vaug = self.carve([12, 2, 128], BF16)
        A["vtok"] = vtok = self.carve([8, 512], BF16)
        A["mixb"] = mixb = self.carve([4, T], BF16)
        A["qs"] = self.carve([T], BF16)
        A["gsil"] = self.carve([T], BF16)
        A["tset"] = [(self.carve([512], F32), self.carve([512], F32), self.carve([512], F32)) for _ in range(2)]
        A["qhat"] = self.carve([2, T], BF16)
        A["khat"] = self.carve([2, T], BF16)
        A["ktok"] = self.carve([8, 2, 128], BF16)
        A["AT"] = self.carve([2, 8, 128], BF16)
        A["obuf"] = self.carve([T], F32)
        A["Ebuf2"] = [self.carve([8], F32), self.carve([8], F32)]
        A["Lb"] = self.carve([2, 16], F32)
        A["Eh"] = self.carve([2, 16], F32)
        A["Fb"] = self.carve([2, 16], F32)
        A["Sst"] = self.carve([8, 128], F32)
        A["dbf"] = self.carve([8, 128], BF16)
        self.mixb = mixb

        self.S.mark("g%d l%d hgrn+attnA" % (g, l))

        def pre_hgrn():
            AT_ = A["AT"]
            self.S.add("dve", lambda e_: e_.memset(AT_[:], 0.0), [], ["AT0", ("AT", 0), ("AT", 1)])
            for ti in range(2):
                slot, wkey = self.wtile(w2d, 0, 8, 1280 + ti * 256, 256)
                for tt in range(8):
                    b = self.psb("mm")
                    self.mm(self.pb[b][:, 0:256], [(self.hm[:, kc, tt * 128:(tt + 1) * 128], slot[:, kc, :]) for kc in range(8)], [wkey] + hk, [("ps", b)])
                    self.cp("act" if tt % 2 == 0 else "dve", vtok[:, tt, ti * 256:(ti + 1) * 256], self.pb[b][:, 0:256], [("ps", b)], [("vtok", tt)])

            yield

        def pre_attn():
            self.S.add("dve", lambda e_: e_.memset(vaug[:, :, :, 64:128], 1.0), [], ["vaug_ones"])
            if g == 1:
                for j in range(4):
                    self.dma("pool", vaug[:, 8 + j, :, 0:64], dr["cav"][e, j * 128:(j + 1) * 128, :].rearrange("p (k d) -> p k d", d=64),
                             ["arena"], [("vaug", 8 + j)], chan=("cch", "cav", j))
                for j in range(4):
                    buf = self.rot("cst", 2)
                    self.dma("sp", self.cst[:, buf, 0:256], dr["cakd"][e, j * 128:(j + 1) * 128, :], [], [("cst", buf)], chan=("lch", "cst", buf))
                    b = self.psb("aux")
                    bank, cst, ident = self.pb[b], self.cst, self.ident

                    def fn(e_, bank=bank, cst=cst, buf=buf, ident=ident):
                        e_.transpose(bank[:, 0:128], cst[:, buf, 0:128], ident[:])
                        return e_.transpose(bank[:, 128:256], cst[:, buf, 128:256], ident[:])
                    self.S.add("pe", fn, [("cst", buf), "ident"], [("ps", b)])
                    self.cp("act", kaT[:, :, T + j * 128:T + (j + 1) * 128], bank[:, 0:256].rearrange("p (k t) -> p k t", k=2), [("ps", b)], [("kaT", 0), ("kaT", 1)])

            def h_qa(oc, th, ps, pk):
                self.cp("act" if th == 0 else "dve", qa[:, oc, th * 512:(th + 1) * 512], ps, [pk], [("qa", oc)])
            self.proj_fm(w2d, 0, 512, self.hm, "hm", 8, h_qa)

            def h_ka(oc, th, ps, pk):
                self.cp("act" if th == 0 else "dve", kaT[:, oc, th * 512:(th + 1) * 512], ps, [pk], [("kaT", oc)])
            self.proj_fm(w2d, EVEN_IN, 256, self.hm, "hm", 8, h_ka)
            if g == 1:
                for c in range(4):
                    self.rope(qa[:, c, :], [("qa", c)], self.perm_hd[:], self.cos_hd, self.sin_hd, T)
                for c in range(2):
                    self.rope(kaT[:, c, 0:T], [("kaT", c)], self.perm_hd[:], self.cos_hd, self.sin_hd, T)
            slot, wkey = self.wtile(w2d, 0, 8, 512, 256)
            for tt in range(8):
                b = self.psb("mm")
                self.mm(self.pb[b][:, 0:256], [(self.hm[:, kc, tt * 128:(tt + 1) * 128], slot[:, kc, :]) for kc in range(8)], [wkey] + hk, [("ps", b)])
                sk_ = self.cfg.get("skip", "")
                if "v" not in sk_:
                    self.cp("act", vaug[:, tt, :, 0:64], self.pb[b][:, 128:256].rearrange("p (k d) -> p k d", d=64), [("ps", b)], [("vaug", tt)])
                if g == 0 and "c" not in sk_:
                    buf = self.rot("cst", 2)
                    self.cp("dve", self.cst[:, buf, 0:256], self.pb[b][:, 0:256], [("ps", b)], [("cst", buf)])
                    seq, p0 = tt // 2, (tt % 2) * 128
                    if "d" not in sk_:
                        self.dma("sp", dr["nak"][seq, e, p0:p0 + 128, :], self.cst[:, buf, 0:128], [("cst", buf)], [], is_out=True, chan=("och", "csta", buf))
                        self.dma("sp", dr["nav"][seq, e, p0:p0 + 128, :], self.cst[:, buf, 128:256], [("cst", buf)], [], is_out=True, chan=("och", "cstb", buf))

            yield

        self.mixa = mixa = self.zb[:].rearrange("p (c a) b -> p c (a b)", a=2)

        def hgrn_all():
            yield from pre_hgrn()
            for hd in range(4):
                yield from self.hgrn_head(g, e, hd, w2d, A)

        def attn_all():
            yield from pre_attn()
            scale = HD ** -0.5
            if g == 0:
                units = [(seq * 256 + qb * 128, [(seq * 2 + kb, None) for kb in range(2)]) for seq in range(4) for qb in range(2)]
            else:
                units = []
                for qt in range(8):
                    kbs = []
                    for j in (qt - 1, qt, qt + 1):
                        if 0 <= j < 8:
                            kbs.append((j, None if j == qt else (self.m_prev[:] if j == qt - 1 else self.m_next[:])))
                    kbs += [(8 + j, None) for j in range(4)]
                    units.append((qt * 128, kbs))
            for q0, kbs in units:
                for kvh in range(2):
                    heads = [kvh * 4 + i for i in (0, 2, 1, 3)]
                    qaps = [qa[(h % 2) * 64:(h % 2) * 64 + 64, h // 2, q0:q0 + 128] for h in heads]
                    qkeys = [("qa", c) for c in (kvh * 2, kvh * 2 + 1)]
                    kblocks = []
                    for (kt, mask) in kbs:
                        kts = [kaT[(h % 2) * 64:(h % 2) * 64 + 64, kvh, kt * 128:(kt + 1) * 128] for h in heads]
                        kblocks.append((kts, vaug[:, kt, kvh, :], mask, [("kaT", kvh), ("vaug", kt), "vaug_ones"]))

                    parts = [(0, 256, mixa[0:64, kvh * 2:kvh * 2 + 2, q0:q0 + 128], 128),
                             (256, 256, mixa[64:128, kvh * 2:kvh * 2 + 2, q0:q0 + 128], 128)]
                    dk = [("zb", 2 * c + q0 // 512) for c in (kvh * 2, kvh * 2 + 1)]
                    sb_ = self.esinkU[64:128, e, kvh, :].unsqueeze(2).to_broadcast([64, 4, 128])
                    self.attn_unit(qaps, qkeys, kblocks, scale, 128, self.attn_fin_batched(parts, dk, 128, 4, sb_), ng=2)
                    yield
        self.run_lanes([hgrn_all(), attn_all()], [{"mm": [0, 1], "st": [2, 3], "aux": [4]}, {"mm": [5, 6], "st": [7], "aux": [7]}],
                       slots=[[0, 1, 2, 3], [4, 5]])

    def interleave(self, gens, counts):
        done = [False] * len(gens)
        acc = [0.0] * len(gens)
        while not all(done):
            for gi, gen in enumerate(gens):
                if done[gi]:
                    continue
                acc[gi] += counts[gi] / float(counts[0]) if not done[0] else 1.0
                while acc[gi] >= 1.0 and not done[gi]:
                    acc[gi] -= 1.0
                    self.stream = gi
                    try:
                        next(gen)
                    except StopIteration:
                        done[gi] = True
                    self.stream = None

    def hgrn_head(self, g, e, hd, w2d, A):
        dr = self.dram
        qs, gsil = A["qs"], A["gsil"]
        qhat, khat, ktok, AT, obuf = A["qhat"], A["khat"], A["ktok"], A["AT"], A["obuf"]
        Lb, Eh, Fb, Sst, dbf, vtok, mixb = A["Lb"], A["Eh"], A["Fb"], A["Sst"], A["dbf"], A["vtok"], A["mixb"]
        DKS = 128 ** -0.5
        NCK = T // CH
        for (col, dst, key) in ((768 + hd * 128, qs, "qs"), (2816 + hd * 128, gsil, "gsil")):
            def hh(oc, th, ps, pk, dst=dst, key=key):
                self.act(dst[:, th * 512:(th + 1) * 512], ps, AF.Silu, [pk], [key])
            self.proj_fm(w2d, col, 128, self.hm, "hm", 8, hh)
        for d in range(2):
            lbv = self.lb[:, e, d, 0, hd:hd + 1]
            omv = self.lb[:, e, d, 1, hd:hd + 1]

            def hg(oc, th, ps, pk, d=d, lbv=lbv, omv=omv):
                sl = slice(th * 512, (th + 1) * 512)
                nc_ = 512 // CH
                ti_ = self.rot("tset", 2)
                t1, t2, t3 = A["tset"][ti_]
                k1, k2, k3, ke = ("t1", ti_), ("t2", ti_), ("t3", ti_), ("Ebuf", ti_)
                Ebuf = A["Ebuf2"][ti_]
                self.act(t2[:], ps, AF.Sigmoid, [pk], [k2], scale=-1.0)
                self.act(t2[:], t2[:], AF.Identity, [k2] + self.lbkeys, [k2], scale=omv)
                self.act(t1[:], t2[:], AF.Ln, [k2, "epsc2"], [k1], bias=self.epsc[:, 2:3], scale=-1.0)
                sm = self.scanmask
                self.S.add("dve", lambda e_: e_.tensor_tensor_scan(out=t3[:], data0=sm[:], data1=t1[:], initial=0.0, op0=ALU.mult, op1=ALU.add),
                           [k1, "scanmask", "scanmask2"], [k3])
                t33 = t3[:].rearrange("p (n c) -> p n c", c=CH)
                self.cp("dve", Lb[:, d, th * nc_:(th + 1) * nc_], t33[:, :, CH - 1], [k3], [("Lb", d)])
                self.ts("dve", Ebuf[:], t33[:, :, CH - 1], 0.5, None, ALU.mult, None, [k3], [ke])
                eb = Ebuf[:].unsqueeze(2).to_broadcast([128, nc_, CH])
                if d == 0:
                    self.tt("dve", t33, t33, eb, ALU.subtract, [k3, ke], [k3])
                else:
                    self.tt("dve", t3[:], t1[:], t3[:], ALU.subtract, [k1, k3], [k3])
                    self.tt("dve", t33, t33, eb, ALU.add, [k3, ke], [k3])
                self.act(t1[:], t3[:], AF.Exp, [k3], [k1])
                self.act(t3[:], t3[:], AF.Exp, [k3], [k3], scale=-1.0)
                self.stt("dve", qhat[:, d, sl], qs[:, sl], DKS, t1[:], ALU.mult, ALU.mult, ["qs", k1], [("qhat", d)])
                self.tt("dve", khat[:, d, sl], t2[:], t3[:], ALU.mult, [k3, k2], [("khat", d)])
            self.proj_fm(w2d, 1792 + d * 512 + hd * 128, 128, self.hm, "hm", 8, hg)
            yield
        yield
        nseq = 4 if g == 0 else 1
        nst = NCK // nseq
        self.act(Eh[:], Lb[:], AF.Exp, [("Lb", 0), ("Lb", 1)], ["Eh"], scale=0.5)
        self.cp("dve", Fb[:], Eh[:], ["Eh"], ["Fb"])
        if nst > 1:
            f0 = Fb[:, 0, :].rearrange("p (s n) -> p s n", s=nseq)
            e0 = Eh[:, 0, :].rearrange("p (s n) -> p s n", s=nseq)
            self.tt("dve", f0[:, :, 0:nst - 1], f0[:, :, 0:nst - 1], e0[:, :, 1:nst], ALU.mult, ["Eh", "Fb"], ["Fb"])
            f1 = Fb[:, 1, :].rearrange("p (s n) -> p s n", s=nseq)
            e1 = Eh[:, 1, :].rearrange("p (s n) -> p s n", s=nseq)
            self.tt("dve", f1[:, :, 1:nst], f1[:, :, 1:nst], e1[:, :, 0:nst - 1], ALU.mult, ["Eh", "Fb"], ["Fb"])
        for d in range(2):
            for th in range(2):
                sl = slice(th * 512, (th + 1) * 512)
                kf = self.zsq[:, 1, :]
                fb_ = Fb[:, d, th * 8:(th + 1) * 8].unsqueeze(2).to_broadcast([128, 8, CH])
                self.tt("dve", kf.rearrange("p (n c) -> p n c", c=CH), khat[:, d, sl].rearrange("p (n c) -> p n c", c=CH), fb_, ALU.mult,
                        [("khat", d), "Fb"], [("zsq", 1)])
                b = self.psb("aux")
                bankb = self.pb[b][:].bitcast(BF16)
                identb = self.identb

                def fn(e_, bankb=bankb, kf=kf, identb=identb):
                    ins = None
                    for i in range(4):
                        ins = e_.transpose(bankb[:, i * 128:(i + 1) * 128], kf[:, i * 128:(i + 1) * 128], identb[:])
                    return ins
                self.S.add("pe", fn, [("zsq", 1), "c_ident"], [("ps", b)])
                self.cp("act", ktok[:, th * 4:(th + 1) * 4, d, :], bankb[:, 0:512].rearrange("p (i k) -> p i k", k=128), [("ps", b)], [("ktok", d)])
                b = self.psb("aux")
                bank = self.pb[b]

                def fn2(e_, bank=bank, d=d, th=th):
                    ins = None
                    for i in range(4):
                        c0 = (th * 4 + i) * 128
                        ins = e_.matmul(bank[:, i * 128:(i + 1) * 128], khat[:, d, c0:c0 + 128], qhat[:, d, c0:c0 + 128], start=True, stop=True)
                    return ins
                self.S.add("pe", fn2, [("khat", d), ("qhat", d)], [("ps", b)])
                m = self.hmask[:, d, :].unsqueeze(1).to_broadcast([128, 4, 128])
                atv = AT[:, d, th * 4:(th + 1) * 4, :]
                bv = bank[:].rearrange("p (i t) -> p i t", t=128)
                self.S.add("dve", lambda e_, atv=atv, m=m, bv=bv: e_.copy_predicated(out=atv, mask=m, data=bv), [("ps", b), "c_hmask", ("AT", d), "AT0"], [("AT", d)])
                yield
        yield
        if g == 0:
            self.S.add("dve", lambda e_: e_.memset(Sst[:], 0.0), [], [("Sst", i) for i in range(8)])
        else:
            for d in range(2):
                self.dma("sp", Sst[:, d, :], dr["stb"][e, d, hd], ["arena"], [("Sst", d)], chan=("lch", "Sst", d))
                c_first = 0 if d == 0 else nst - 1
                self.ts("dve", Sst[:, d, :], Sst[:, d, :], Eh[:, d, c_first:c_first + 1], None, ALU.mult, None, [("Sst", d), "Eh"], [("Sst", d)])
        nchain = 2 * nseq
        self.cp("act", dbf[:, 0:nchain, :], Sst[:, 0:nchain, :], [("Sst", i) for i in range(nchain)], [("dbf", i) for i in range(nchain)])
        written = set()
        for step in range(nst):
            for d in range(2):
                ob_ = self.psb("st")
                obank = self.pb[ob_]
                cn0 = None
                for sq in range(nseq):
                    ci = sq * 2 + d if g == 0 else d
                    cn = sq * nst + (step if d == 0 else nst - 1 - step)
                    if cn0 is None:
                        cn0 = cn
                    tt_, hb = cn // 2, (cn % 2) * 64
                    Vn = vtok[hb:hb + 64, tt_, hd * 128:(hd + 1) * 128]
                    ops = [(Vn, AT[hb:hb + 64, d, tt_, hb:hb + 64]), (dbf[:, ci, :], qhat[:, d, cn * CH:(cn + 1) * CH])]
                    self.mm(obank[:, sq * CH:(sq + 1) * CH], ops, [("vtok", tt_), ("AT", d), ("dbf", ci), ("qhat", d)], [("ps", ob_)])
                    ub = self.psb("mm")
                    self.mm(self.pb[ub][:, 0:128], [(ktok[hb:hb + 64, tt_, d, :], Vn)], [("ktok", d), ("vtok", tt_)], [("ps", ub)])
                    self.stt("dve", Sst[:, ci, :], Sst[:, ci, :], Fb[:, d, cn:cn + 1], self.pb[ub][:, 0:128], ALU.mult, ALU.add,
                             [("Sst", ci), "Fb", ("ps", ub)], [("Sst", ci)])
                    if step < nst - 1:
                        self.cp("act", dbf[:, ci, :], Sst[:, ci, :], [("Sst", ci)], [("dbf", ci)])
                off = (cn0 % nst) * CH
                dstv = obuf[:].rearrange("p (s t) -> p s t", s=nseq)[:, :, off:off + CH]
                srcv = obank[:, 0:nseq * CH].rearrange("p (s t) -> p s t", t=CH)
                if cn0 in written:
                    self.tt("dve", dstv, dstv, srcv, ALU.add, [("ps", ob_), "obuf"], ["obuf"])
                else:
                    self.cp("act", dstv, srcv, [("ps", ob_)], ["obuf"])
                    written.add(cn0)
            yield
        if g == 0:
            for sq in range(4):
                for d in range(2):
                    ci = sq * 2 + d
                    self.dma("sp", dr["nsb"][sq, e, d, hd], Sst[:, ci, :], [("Sst", ci), "arena"], [], is_out=True, chan=("och", "Sst", ci))
        yield
        gcol = self.vecs[:, VC["gnorm"] + e:VC["gnorm"] + e + 1]
        for th in range(2):
            sl = slice(th * 512, (th + 1) * 512)
            self.act(self.zsq[:, 0, :], obuf[:, sl], AF.Square, ["obuf"], [("zsq", 0)])
            b = self.psb("aux")
            self.mm(self.pb[b][:], [(self.ones[:], self.zsq[:, 0, :])], ["ones", ("zsq", 0)], [("ps", b)])
            self.rsqrt(self.rstd[:], self.pb[b][:], 1.0 / 128, 1, [("ps", b)], ["rstd"])
            self.stt("dve", self.mean[:], obuf[:, sl], gcol, self.rstd[:], ALU.mult, ALU.mult, ["obuf", "rstd", "vecs"], ["mean"])
            self.tt("dve", mixb[:, hd, sl], self.mean[:], gsil[:, sl], ALU.mult, ["mean", "gsil"], [("mixb", hd)])
        yield

    def odd_mixer(self, g, l):
        o = l // 2
        dr = self.dram
        w2d = dr["w_in_cd"][o]
        hk = [("hm", kc) for kc in range(8)]
        NK = T if g == 0 else T + PAST
        nkt = NK // 128
        self.carve_reset()
        cqn = self.carve([3, T], BF16)
        latT = self.carve([2, T + PAST], BF16)
        kpeT = self.carve([T + PAST], BF16)
        Qc = self.carve([4, T], BF16)
        Kc = self.carve([4, T + PAST], BF16)
        vaugC = self.carve([12, 4, 128], BF16)
        qd = self.carve([4, T], BF16)
        kdT = self.carve([4, T + PAST], BF16)
        vaugD = self.carve([12, 4, 128], BF16)
        v = self.vecs
        self.S.add("dve", lambda e_: e_.memset(vaugC[:, :, :, 64:128], 1.0), [], ["vaugC_ones"])
        self.S.add("dve", lambda e_: e_.memset(Qc[:], 0.0), [], [("Qc", h_) for h_ in range(4)])
        self.S.add("dve", lambda e_: e_.memset(Kc[:], 0.0), [], [("Kc", h_) for h_ in range(4)])
        self.S.add("dve", lambda e_: e_.memset(vaugD[:, :, :, 64:128], 1.0), [], ["vaugD_ones"])

        for th in range(2):
            sl = slice(th * 512, (th + 1) * 512)

            def h_cq(oc, th_, ps, pk, sl=sl):
                self.cp("dve", cqn[:, oc, sl], ps, [pk], [("cqn", oc)])
                self.act(self.zsq[:, oc, :], ps, AF.Square, [pk], [("zsq", oc)])
            self.proj_fm(w2d, 0, 384, self.hm, "hm", 8, h_cq, halves=(th,))
            b = self.psb("st")
            self.mm(self.pb[b][:], [(self.ones[:], self.zsq[:, oc, :]) for oc in range(3)], ["ones"] + [("zsq", oc) for oc in range(3)], [("ps", b)])
            self.rsqrt(self.rstd[:], self.pb[b][:], 1.0 / 384, 1, [("ps", b)], ["rstd"])
            for oc in range(3):
                gc = v[:, VC["cqn"] + o * 3 + oc:VC["cqn"] + o * 3 + oc + 1]
                self.stt("dve", cqn[:, oc, sl], cqn[:, oc, sl], gc, self.rstd[:], ALU.mult, ALU.mult, [("cqn", oc), "rstd", "vecs"], [("cqn", oc)])

        def normed(dstbuf, dkeyname, gcol):
            pend = []

            def post(oc, sl, zi):
                b = self.psb("aux")
                self.mm(self.pb[b][:], [(self.bd[:], self.zsq[:, zi, :])], ["c_blockdiag", ("zsq", zi)], [("ps", b)])
                ti = self.rot("rdn", 2)
                tmp = self.rd[:, ti, :]
                self.rsqrt(tmp, self.pb[b][:], 1.0 / 64, 1, [("ps", b)], [("rd", ti)])
                self.stt("dve", dstbuf[:, oc, sl], dstbuf[:, oc, sl], gcol, tmp, ALU.mult, ALU.mult, [(dkeyname, oc), ("rd", ti), "vecs"], [(dkeyname, oc)])

            def h(oc, th, ps, pk):
                sl = slice(th * 512, (th + 1) * 512)
                zi = 4 + self.rot("zq", 4)
                self.cp("dve", dstbuf[:, oc, sl], ps, [pk], [(dkeyname, oc)])
                self.act(self.zsq[:, zi, :], ps, AF.Square, [pk], [("zsq", zi)])
                while len(pend) > 1:
                    post(*pend.pop(0))
                pend.append((oc, sl, zi))

            def flush():
                while pend:
                    post(*pend.pop(0))
            h.flush = flush
            return h
        hq = normed(qd, "qd", v[:, VC["dqn"] + o:VC["dqn"] + o + 1])
        hkd = normed(kdT, "kdT", v[:, VC["dkn"] + o:VC["dkn"] + o + 1])
        self.proj_fm(w2d, 672, 512, self.hm, "hm", 8, hq)
        hq.flush()
        self.proj_fm(w2d, ODD_IN, 512, self.hm, "hm", 8, hkd)
        hkd.flush()
        if g == 1:
            for c in range(4):
                self.rope(qd[:, c, :], [("qd", c)], self.perm_hd[:], self.cos_hd, self.sin_hd, T)
                self.rope(kdT[:, c, 0:T], [("kdT", c)], self.perm_hd[:], self.cos_hd, self.sin_hd, T)

        def lat_tile(buf, col0, R):
            b2 = self.psb("aux")
            bank, cst, ident = self.pb[b2], self.cst, self.ident

            def fn(e_, bank=bank, cst=cst, buf=buf, ident=ident):
                e_.transpose(bank[:, 0:128], cst[:, buf, 0:128], ident[:])
                e_.transpose(bank[:, 128:256], cst[:, buf, 128:256], ident[:])
                return e_.transpose(bank[0:32, 256:384], cst[:, buf, 256:288], ident[:])
            self.S.add("pe", fn, R + ["ident"], [("ps", b2)])
            self.cp("act", latT[:, :, col0:col0 + 128], bank[:, 0:256].rearrange("p (k t) -> p k t", k=2), [("ps", b2)], [("latT", 0), ("latT", 1)])
            self.cp("dve", kpeT[0:32, col0:col0 + 128], bank[0:32, 256:384], [("ps", b2)], ["kpeT"])

        slotA, kA = self.wtile(w2d, 0, 8, 384, 256)
        slotB, kB = self.wtile(w2d, 0, 8, 640, 32)
        ss1 = self.dummy[:, 2:3]
        for tt in range(8):
            b = self.psb("mm")
            ps = self.pb[b]
            self.mm(ps[:, 0:256], [(self.hm[:, kc, tt * 128:(tt + 1) * 128], slotA[:, kc, :]) for kc in range(8)], [kA] + hk, [("ps", b)])
            self.mm(ps[:, 256:288], [(self.hm[:, kc, tt * 128:(tt + 1) * 128], slotB[:, kc, 0:32]) for kc in range(8)], [kB] + hk, [("ps", b)])
            self.act(self.tA[:, 0:256], ps[:, 0:256], AF.Square, [("ps", b)], ["tA"])
            self.S.add("dve", lambda e_: e_.reduce_sum(out=ss1, in_=self.tA[:, 0:256], axis=mybir.AxisListType.X), ["tA"], ["ss1"])
            self.rsqrt(ss1, ss1, 1.0 / 256, 1, ["ss1"], ["ss1"])
            buf = self.rot("cst", 2)
            self.stt("dve", self.cst[:, buf, 0:256], ps[:, 0:256], ss1, self.ckvn_b[:, o * 256:(o + 1) * 256], ALU.mult, ALU.mult,
                     [("ps", b), "ss1", "ckvn_b"], [("cst", buf)])
            self.cp("act", self.cst[:, buf, 256:288], ps[:, 256:288], [("ps", b)], [("cstpe", buf)])
            if g == 0:
                seq, p0 = tt // 2, (tt % 2) * 128
                self.dma("sp", dr["nckv"][seq, o, p0:p0 + 128, :], self.cst[:, buf, 0:256], [("cst", buf)], [], is_out=True, chan=("och", "csta", buf))
                self.dma("sp", dr["ncpe"][seq, o, p0:p0 + 128, :], self.cst[:, buf, 256:288], [("cstpe", buf)], [], is_out=True, chan=("och", "cstb", buf))
            lat_tile(buf, tt * 128, [("cst", buf), ("cstpe", buf)])
        if g == 1:
            for j in range(4):
                buf = self.rot("cst", 2)
                self.dma("sp", self.cst[:, buf, 0:256], dr["cckv"][o, j * 128:(j + 1) * 128, :], [], [("cst", buf)], chan=("lch", "cst", buf))
                self.dma("act", self.cst[:, buf, 256:288], dr["ccpe"][o, j * 128:(j + 1) * 128, :], [], [("cstpe", buf)], chan=("lch", "cstpe", buf))
                lat_tile(buf, T + j * 128, [("cst", buf), ("cstpe", buf)])
            self.rope(kpeT[0:32, 0:T], ["kpeT"], self.perm_c[0:32, 0:32], self.cos_c[0:32, :], self.sin_c[0:32, :], T, rows=32)

        slot, wkey = self.wtile(w2d, 0, 8, 1440, 256)
        for tt in range(8):
            b = self.psb("mm")
            self.mm(self.pb[b][:, 0:256], [(self.hm[:, kc, tt * 128:(tt + 1) * 128], slot[:, kc, :]) for kc in range(8)], [wkey] + hk, [("ps", b)])
            self.cp("act", vaugD[:, tt, :, 0:64], self.pb[b][:, 0:256].rearrange("p (k d) -> p k d", d=64), [("ps", b)], [("vaugD", tt)])
            if g == 0:
                buf = self.rot("cst", 2)
                self.cp("dve", self.cst[:, buf, 0:256], self.pb[b][:, 0:256], [("ps", b)], [("cst", buf)])
                seq, p0 = tt // 2, (tt % 2) * 128
                self.dma("sp", dr["ndv"][seq, o, p0:p0 + 128, :], self.cst[:, buf, 0:256], [("cst", buf)], [], is_out=True, chan=("och", "csta", buf))
        if g == 1:
            for j in range(4):
                self.dma("pool", vaugD[:, 8 + j, :, 0:64], dr["cdv"][o, j * 128:(j + 1) * 128, :].rearrange("p (k d) -> p k d", d=64),
                         ["arena"], [("vaugD", 8 + j)], chan=("cch", "cdv", j))
                stg = self.stg[j % 2]
                sk = self.stgkeys(j % 2)
                self.dma("sp", stg[:, 0:512], dr["cdkd"][o, j * 128:(j + 1) * 128, :], [], sk, chan=("lch", "stg", j % 2))
                b = self.psb("mm")
                bank, ident = self.pb[b], self.ident

                def fn(e_, bank=bank, stg=stg, ident=ident):
                    ins = None
                    for c in range(4):
                        ins = e_.transpose(bank[:, c * 128:(c + 1) * 128], stg[:, c * 128:(c + 1) * 128], ident[:])
                    return ins
                self.S.add("pe", fn, sk + ["ident"], [("ps", b)])
                self.cp("act", kdT[:, :, T + j * 128:T + (j + 1) * 128], bank[:].rearrange("p (c t) -> p c t", c=4), [("ps", b)], [("kdT", c) for c in range(4)])
        else:
            for tt in range(8):
                b = self.psb("aux")
                bankb = self.pb[b][:].bitcast(BF16)
                identb = self.identb

                def fn(e_, bankb=bankb, tt=tt, identb=identb):
                    ins = None
                    for c in range(4):
                        ins = e_.transpose(bankb[:, c * 128:(c + 1) * 128], kdT[:, c, tt * 128:(tt + 1) * 128], identb[:])
                    return ins
                self.S.add("pe", fn, [("kdT", c) for c in range(4)] + ["c_ident"], [("ps", b)])
                buf = self.rot("cst", 2)
                self.cp("dve", self.cst[:, buf, 0:256].rearrange("p (c d) -> p c d", d=64), bankb[:, 0:512].rearrange("p (c x) -> p c x", x=128)[:, :, 0:64],
                        [("ps", b)], [("cst", buf)])
                seq, p0 = tt // 2, (tt % 2) * 128
                self.dma("sp", dr["ndk"][seq, o, p0:p0 + 128, :], self.cst[:, buf, 0:256], [("cst", buf)], [], is_out=True, chan=("och", "csta", buf))

        self.S.mark("g%d l%d mixC+D" % (g, l))

        def c_stream():
            wq, wkv = dr["c_w_q_up"][o], dr["c_w_kv_up"][o]
            scale_c = 96 ** -0.5
            for bi in range(2):
                for pair in range(2):
                    slot, wkey = self.wtile(wq, 0, 3, (bi * 2 + pair) * 192, 192)
                    for hh in range(2):
                        hl = pair * 2 + hh
                        for th in range(2):
                            sl = slice(th * 512, (th + 1) * 512)
                            b = self.psb("mm")
                            self.mm(self.pb[b][0:96, :], [(slot[:, kc, hh * 96:(hh + 1) * 96], cqn[:, kc, sl]) for kc in range(3)],
                                    [wkey] + [("cqn", kc) for kc in range(3)], [("ps", b)])
                            self.cp("act" if th == 0 else "dve", Qc[0:96, hl, sl], self.pb[b][0:96, :], [("ps", b)], [("Qc", hl)])
                        if g == 1:
                            for th in range(2):
                                sl = slice(th * 512, (th + 1) * 512)
                                b = self.psb("aux")
                                ps = self.pb[b][0:96, :]
                                self.mm(ps, [(self.perm_c[0:96, 0:96], Qc[0:96, hl, sl])], [("Qc", hl), "c_perm_c"], [("ps", b)])
                                t1 = self.xn[0][0:32, :]
                                t2 = self.xn[1][0:32, :]
                                self.tt("dve", t1, Qc[64:96, hl, sl], self.cos_c[64:96, sl], ALU.mult, [("Qc", hl), "c_cos_c"], ["xn0"])
                                self.tt("dve", t2, self.pb[b][64:96, :], self.sin_c[64:96, sl], ALU.mult, [("ps", b), "c_sin_c"], ["xn1"])
                                self.tt("dve", Qc[64:96, hl, sl], t1, t2, ALU.add, ["xn0", "xn1"], [("Qc", hl)])
                yield
                for pair in range(2):
                    slot, wkey = self.wtile(wkv, 0, 2, (bi * 2 + pair) * 256, 256)
                    for hh in range(2):
                        hl = pair * 2 + hh
                        for c0 in range(0, NK, 512):
                            b = self.psb("mm")
                            self.mm(self.pb[b][0:64, :], [(slot[:, kc, hh * 128:hh * 128 + 64], latT[:, kc, c0:c0 + 512]) for kc in range(2)],
                                    [wkey, ("latT", 0), ("latT", 1)], [("ps", b)])
                            self.cp("act" if (c0 // 512) % 2 == 0 else "dve", Kc[0:64, hl, c0:c0 + 512], self.pb[b][0:64, :], [("ps", b)], [("Kc", hl)])
                    for kt in range(nkt):
                        b = self.psb("mm")
                        self.mm(self.pb[b][:, 0:128], [(latT[:, kc, kt * 128:(kt + 1) * 128], slot[:, kc, :].rearrange("p (h x) -> p h x", x=128)[:, :, 64:128]) for kc in range(2)],
                                [wkey, ("latT", 0), ("latT", 1)], [("ps", b)])
                        self.cp("act" if kt % 2 == 0 else "dve", vaugC[:, kt, pair * 2:pair * 2 + 2, 0:64], self.pb[b][:, 0:128].rearrange("p (h d) -> p h d", d=64),
                                [("ps", b)], [("vaugC", kt)])
                yield
                for hl in range(4):
                    self.cp("act" if hl % 2 == 0 else "dve", Kc[64:96, hl, 0:NK], kpeT[0:32, 0:NK], ["kpeT"], [("Kc", hl)])
                if g == 0:
                    units = [(seq * 256, 256, [seq * 2, seq * 2 + 1]) for seq in range(4)]
                else:
                    units = [(qt * 512, 512, list(range(12))) for qt in range(2)]
                for hl in range(4):
                    h = bi * 4 + hl
                    for (q0, nq, kts_) in units:
                        kblocks = [([Kc[:, hl, kt * 128:(kt + 1) * 128]], vaugC[:, kt, hl, :], None, [("Kc", hl), ("vaugC", kt), "vaugC_ones"]) for kt in kts_]

                        def dst(i, h=h, q0=q0, nq=nq):
                            return self.hm[(h % 2) * 64:(h % 2) * 64 + 64, h // 2, q0:q0 + nq]

                        def dkey(i, h=h):
                            return [("hm", h // 2)]
                        self.attn_unit([Qc[:, hl, q0:q0 + nq]], [("Qc", hl)], kblocks, scale_c, nq, self.attn_fin(dst, dkey))
                        yield


        def d_stream():
            scale_d = HD ** -0.5
            if g == 0:
                units = [(seq * 256, [seq * 2, seq * 2 + 1]) for seq in range(4)]
            else:
                units = [(qt * 256, list(range(12))) for qt in range(4)]
            if g == 1:
                for h in range(8):
                    c, i = h // 2, h % 2
                    for q0 in (0, 512):
                        kblocks = [([kdT[i * 64:(i + 1) * 64, c, kt * 128:(kt + 1) * 128]], vaugD[:, kt, c, :], None,
                                    [("kdT", c), ("vaugD", kt), "vaugD_ones"]) for kt in range(12)]
                        parts = [(0, 512, self.hm[i * 64:(i + 1) * 64, 4 + c, q0:q0 + 512], None)]
                        self.attn_unit([qd[i * 64:(i + 1) * 64, c, q0:q0 + 512]], [("qd", c)], kblocks, scale_d, 512,
                                       self.attn_fin_batched(parts, [("hm", 4 + c)], 512, 1))
                        yield
                return
            for c in range(4):
                for (q0, kts_) in units:
                    qaps = [qd[i * 64:(i + 1) * 64, c, q0:q0 + 256] for i in range(2)]
                    kblocks = [([kdT[i * 64:(i + 1) * 64, c, kt * 128:(kt + 1) * 128] for i in range(2)], vaugD[:, kt, c, :], None,
                                [("kdT", c), ("vaugD", kt), "vaugD_ones"]) for kt in kts_]

                    parts = [(0, 256, self.hm[0:64, 4 + c, q0:q0 + 256], None), (256, 256, self.hm[64:128, 4 + c, q0:q0 + 256], None)]
                    self.attn_unit(qaps, [("qd", c)], kblocks, scale_d, 256, self.attn_fin_batched(parts, [("hm", 4 + c)], 256, 2), ng=2)
                    yield

        self.run_lanes([c_stream(), d_stream()], [{"mm": [0, 1, 7], "st": [2], "aux": [3]}, {"mm": [4, 5], "st": [6], "aux": [6]}])

    def build(self):
        try:
            self.build_()
        except StopBuild:
            pass
        self.S.emit(self.st)

    def build_(self):
        cfg = self.cfg
        groups = cfg.get("groups", (0, 1))
        NL = cfg.get("nl", DEPTH)
        self.setup()
        self.stage(0.25)
        first = True
        for g in groups:
            self.load_x(g)
            if first:
                self.mods(0)
                self.dbg("dbg_modv", self.modv[:], [("modv", 0, 0), ("modv", 0, 1)])
            self.dbg("dbg_x0", self.x[:], [("x", c, h_) for c in range(8) for h_ in range(2)])
            self.stage(0.75)
            self.modulate0(g, 0)
            self.dbg("dbg_h0", self.hm[:], [("hm", c) for c in range(8)])
            self.stage(1)
            for l in range(NL):
                side = None
                if first:
                    self.lnfold(l, (0,))
                    if l + 1 < DEPTH:
                        side = self.mods_gen(l + 1)
                self.stage(1.2)
                self.S.mark("g%d l%d mixer" % (g, l))
                if l % 2 == 0:
                    self.even_mixer(g, l)
                    mixb = self.mixb
                    mixa = self.mixa
                    srcf = lambda kc, mixb=mixb, mixa=mixa: ((mixa[:, kc, :], [("zb", 2 * kc), ("zb", 2 * kc + 1)]) if kc < 4 else (mixb[:, kc - 4, :], [("mixb", kc - 4)]))
                    self.proj_fm(self.dram["w_out"][l], 0, D, None, None, 8, self.ln_z(l, 2, g), srcf=srcf)
                else:
                    self.odd_mixer(g, l)
                    self.proj_fm(self.dram["w_out"][l], 0, D, self.hm, "hm", 8, self.ln_z(l, 2, g))
                self.S.mark("g%d l%d ln1" % (g, l))
                if l % 2 == 0:
                    self.dbg("dbg_mix_bf", self.mixa, [("zb", c) for c in range(8)])
                else:
                    self.dbg("dbg_mix_bf", self.hm[:], [("hm", c) for c in range(8)])
                if l % 2 == 0:
                    self.dbg("dbg_mixb_bf", self.mixb[:], [("mixb", c) for c in range(4)])
                self.dbg("dbg_z", self.x[:], [("x", c, h_) for c in range(8) for h_ in range(2)])
                self.stage(4)
                self.ln_finish(g, l, 0, False)
                self.dbg("dbg_xmid", self.x[:], [("x", c, h_) for c in range(8) for h_ in range(2)])
                self.dbg("dbg_hmid_bf", self.hm[:], [("hm", c) for c in range(8)])
                self.stage(5)
                self.S.mark("g%d l%d ffn" % (g, l))
                self.ffn(g, l, side)
                self.tap("tap_x_%d_%d" % (g, l))
            self.S.mark("g%d store" % g)
            self.store_x(g)
            first = False


def _build_program(cfg):
    nc = bass.Bass("TRN2", target_bir_lowering=False)
    st = ExitStack()
    kb = KB(nc, st, cfg)
    kb.build()
    st.close()
    return nc, kb


def _core_inputs(inp, b, shared):
    m = dict(shared)
    f = lambda a: np.ascontiguousarray(np.asarray(a, np.float32))
    m["xp"] = f(inp["x_prompt"][4 * b:4 * b + 4].reshape(T, D))
    m["xs"] = f(inp["x_sample"][b])
    cak = np.asarray(inp["cache_a_k"][b]).reshape(N_EVEN, PAST, 2, 64)
    m["cakd"] = f(np.concatenate([cak[:, :, 0:1], cak[:, :, 0:1], cak[:, :, 1:2], cak[:, :, 1:2]], 2).reshape(N_EVEN, PAST, 256))
    m["cav"] = f(np.asarray(inp["cache_a_v"][b]).reshape(N_EVEN, PAST, 128))
    m["stb"] = f(inp["state_b"][b])
    m["cckv"] = f(inp["cache_c_kv"][b])
    m["ccpe"] = f(inp["cache_c_pe"][b])
    cdk = np.asarray(inp["cache_d_k"][b]).reshape(N_ODD, PAST, 4, 64)
    m["cdkd"] = f(np.repeat(cdk, 2, axis=2).reshape(N_ODD, PAST, 512))
    m["cdv"] = f(np.asarray(inp["cache_d_v"][b]).reshape(N_ODD, PAST, 256))
    m["vecs"] = _pack_vecs(inp, b)
    return m


def _shared_inputs(inp):
    f = lambda a: np.ascontiguousarray(np.asarray(a, np.float32))
    s = {}
    wab = np.asarray(inp["w_in_ab"], np.float32)
    s["w_in_ab"] = f(np.concatenate([wab, wab[:, :, 512:576], wab[:, :, 512:576], wab[:, :, 576:640], wab[:, :, 576:640]], 2))
    wcd = np.asarray(inp["w_in_cd"], np.float32)
    kd = wcd[:, :, 1184:1440].reshape(N_ODD, D, 4, 64)
    s["w_in_cd"] = f(np.concatenate([wcd, np.repeat(kd, 2, axis=2).reshape(N_ODD, D, 512)], 2))
    for k in ("w_ada", "c_w_q_up", "c_w_kv_up", "w_out", "w_ffn_gate", "w_ffn_up", "w_ffn_down"):
        s[k] = f(inp[k])
    s["ckvn_b"] = f(np.broadcast_to(np.asarray(inp["c_kv_norm"], np.float32).reshape(1, N_ODD * 256), (128, N_ODD * 256)))
    for k, v in _host_consts().items():
        s[k] = np.ascontiguousarray(v) if v.dtype == np.uint8 else f(v)
    return s


def _run(inp, cfg, cores):
    nc, kb = _build_program(cfg)
    shared = _shared_inputs(inp)
    in_maps = [_core_inputs(inp, b, shared) for b in cores]
    res = run_bass_kernel_spmd(nc, in_maps, core_ids=list(range(len(cores))))
    return res.results


def kernel(**inputs):
    inp = {k: np.asarray(v) for k, v in inputs.items()}
    results = _run(inp, {}, list(range(8)))
    B, SEQ = 32, 256
    yp = np.zeros((B, SEQ, D), np.float32)
    ys = np.zeros((8, T, D), np.float32)
    nak = np.zeros((B, N_EVEN, SEQ, 2, 64), np.float32)
    nav = np.zeros((B, N_EVEN, SEQ, 2, 64), np.float32)
    nsb = np.zeros((B, N_EVEN, 2, 4, 128, 128), np.float32)
    nckv = np.zeros((B, N_ODD, SEQ, 256), np.float32)
    ncpe = np.zeros((B, N_ODD, SEQ, 32), np.float32)
    ndk = np.zeros((B, N_ODD, SEQ, 4, 64), np.float32)
    ndv = np.zeros((B, N_ODD, SEQ, 4, 64), np.float32)
    for i, r in enumerate(results):
        sl = slice(4 * i, 4 * i + 4)
        yp[sl] = r["yp"].reshape(4, SEQ, D)
        ys[i] = r["ys"]
        nak[sl] = r["nak"].reshape(4, N_EVEN, SEQ, 2, 64)
        nav[sl] = r["nav"].reshape(4, N_EVEN, SEQ, 2, 64)
        nsb[sl] = r["nsb"]
        nckv[sl] = r["nckv"]
        ncpe[sl] = r["ncpe"]
        ndk[sl] = r["ndk"].reshape(4, N_ODD, SEQ, 4, 64)
        ndv[sl] = r["ndv"].reshape(4, N_ODD, SEQ, 4, 64)
    return (yp, ys, nak, nav, nsb, nckv, ncpe, ndk, ndv)
```
